# Optimizing a Trainium2 kernel written in Bass

```python
import jax, jax.numpy as jnp
from jax import lax
import numpy as np

D_MODEL = 1024
BATCH = 8
SEQ = 2048
DEPTH = 1

PLE_DIM = 256
LRU_WIDTH = D_MODEL
LRU_BLOCKS = 8
LRU_BLOCK = LRU_WIDTH // LRU_BLOCKS
CONV_WIDTH = 4
LRU_C = 8.0
GLA_HEADS = 4
GLA_DK = D_MODEL // 2 // GLA_HEADS
GLA_DV = D_MODEL // GLA_HEADS
GLA_GATE_RANK = 16
GLA_TAU = 16.0
GLA_CHUNK = 64
PEER_HEADS = 8
PEER_KEYS = 128
PEER_EXPERTS = PEER_KEYS * PEER_KEYS
PEER_TOPK = 16
PEER_QDIM = 128
PEER_HALF = PEER_QDIM // 2
PEER_TOKEN_BLOCK = 128
DN_ALPHA = (2.0 * DEPTH) ** 0.25
DN_BETA = (8.0 * DEPTH) ** -0.25
LN_EPS = 1e-5
RMS_EPS = 1e-6
IN_SPLITS = (LRU_WIDTH, LRU_WIDTH, GLA_HEADS * GLA_DK, GLA_HEADS * GLA_DK,
             GLA_HEADS * GLA_DV, GLA_HEADS * GLA_DV, GLA_GATE_RANK, D_MODEL, D_MODEL)
IN_WIDTH = sum(IN_SPLITS)

kernel_name = "hawk_gla_peer_deepnorm_hybrid"


def layer_norm(x, g, b):
    xf = x.astype(jnp.float32)
    mu = jnp.mean(xf, axis=-1, keepdims=True)
    var = jnp.mean(jnp.square(xf - mu), axis=-1, keepdims=True)
    return ((xf - mu) * lax.rsqrt(var + LN_EPS) * g.astype(jnp.float32) + b.astype(jnp.float32)).astype(x.dtype)


def causal_depthwise_conv(x, w, b):
    y = lax.conv_general_dilated(
        x, w[:, None, :].astype(x.dtype), window_strides=(1,), padding=[(CONV_WIDTH - 1, 0)],
        dimension_numbers=("NWC", "WIO", "NWC"), feature_group_count=x.shape[-1])
    return y + b.astype(x.dtype)


def rg_lru(u, w_r, b_r, w_i, b_i, lam):
    B, S, W = u.shape
    uf = u.astype(jnp.float32)
    ub = uf.reshape(B, S, LRU_BLOCKS, LRU_BLOCK)
    r = jax.nn.sigmoid(jnp.einsum("bsnc,ncd->bsnd", ub, w_r.astype(jnp.float32)) + b_r.astype(jnp.float32)).reshape(B, S, W)
    i = jax.nn.sigmoid(jnp.einsum("bsnc,ncd->bsnd", ub, w_i.astype(jnp.float32)) + b_i.astype(jnp.float32)).reshape(B, S, W)
    log_a = -LRU_C * r * jax.nn.softplus(-lam.astype(jnp.float32))
    a = jnp.exp(log_a)
    mult = jnp.sqrt(-jnp.expm1(2.0 * log_a))
    inp = mult * (i * uf)

    def combine(c1, c2):
        a1, b1 = c1
        a2, b2 = c2
        return a1 * a2, a2 * b1 + b2

    _, h = lax.associative_scan(combine, (a, inp), axis=1)
    return h


def gla_chunked(q, k, v, log_f):
    B, S, H, _ = q.shape
    n = S // GLA_CHUNK

    def to_chunks(t):
        return t.reshape(B, n, GLA_CHUNK, H, t.shape[-1]).transpose(1, 0, 3, 2, 4)

    qc, kc, vc, gc = to_chunks(q), to_chunks(k), to_chunks(v), to_chunks(log_f)
    causal = jnp.tril(jnp.ones((GLA_CHUNK, GLA_CHUNK), dtype=bool))[:, :, None]

    def step(state, inp):
        q_, k_, v_, g_ = inp
        b = jnp.cumsum(g_, axis=2)
        b_last = b[:, :, -1:, :]
        o_inter = jnp.einsum("bhcd,bhde->bhce", q_ * jnp.exp(b), state)
        diff = b[:, :, :, None, :] - b[:, :, None, :, :]
        decay = jnp.exp(jnp.where(causal, diff, -jnp.inf))
        attn = jnp.einsum("bhid,bhjd,bhijd->bhij", q_, k_, decay)
        o = o_inter + jnp.einsum("bhij,bhje->bhie", attn, v_)
        new_state = (jnp.exp(b_last[:, :, 0, :, None]) * state
                     + jnp.einsum("bhcd,bhce->bhde", k_ * jnp.exp(b_last - b), v_))
        return new_state, o

    state0 = jnp.zeros((B, H, q.shape[-1], v.shape[-1]), jnp.float32)
    _, o = lax.scan(step, state0, (qc, kc, vc, gc))
    return o.transpose(1, 0, 3, 2, 4).reshape(B, S, H, v.shape[-1])


def hybrid_mixer(x, w_in, conv_w, conv_b, lru_wr, lru_br, lru_wi, lru_bi, lru_lambda,
                 gla_wf2, gla_bf, gla_norm_g, w_out):
    B, S, _ = x.shape
    z = x @ w_in
    offs = [int(o) for o in np.cumsum(IN_SPLITS)[:-1]]
    lru_x, lru_y, q, k, v, gla_r, f_low, gate_a, gate_b = jnp.split(z, offs, axis=-1)

    ua = causal_depthwise_conv(lru_x, conv_w, conv_b)
    ha = rg_lru(ua, lru_wr, lru_br, lru_wi, lru_bi, lru_lambda)
    ya = ha.astype(x.dtype) * jax.nn.gelu(lru_y, approximate=False)

    qf = q.astype(jnp.float32).reshape(B, S, GLA_HEADS, GLA_DK) * (GLA_DK ** -0.5)
    kf = k.astype(jnp.float32).reshape(B, S, GLA_HEADS, GLA_DK)
    vf = v.astype(jnp.float32).reshape(B, S, GLA_HEADS, GLA_DV)
    f_logit = f_low.astype(jnp.float32) @ gla_wf2.astype(jnp.float32) + gla_bf.astype(jnp.float32)
    log_f = (jax.nn.log_sigmoid(f_logit) / GLA_TAU).reshape(B, S, GLA_HEADS, GLA_DK)
    o = gla_chunked(qf, kf, vf, log_f)
    o = o * lax.rsqrt(jnp.mean(jnp.square(o), axis=-1, keepdims=True) + RMS_EPS)
    o = o.reshape(B, S, GLA_HEADS * GLA_DV) * gla_norm_g.astype(jnp.float32)
    yb = o.astype(x.dtype) * jax.nn.silu(gla_r)

    m = jax.nn.sigmoid(gate_a) * ya + jax.nn.sigmoid(gate_b) * yb
    return m @ w_out


def peer(x, w_q, sub_keys, u_tab, v_tab):
    B, S, D = x.shape
    T = B * S
    xt = x.reshape(T, D)
    q = (xt @ w_q).astype(jnp.float32).reshape(T, PEER_HEADS, 2, PEER_HALF)
    s = jnp.einsum("thpc,hpkc->thpk", q, sub_keys.astype(jnp.float32))
    top_s, top_i = lax.top_k(s, PEER_TOPK)
    cand_s = (top_s[:, :, 0, :, None] + top_s[:, :, 1, None, :]).reshape(T, PEER_HEADS, PEER_TOPK * PEER_TOPK)
    cand_i = (top_i[:, :, 0, :, None] * PEER_KEYS + top_i[:, :, 1, None, :]).reshape(T, PEER_HEADS, PEER_TOPK * PEER_TOPK)
    best_s, best_pos = lax.top_k(cand_s, PEER_TOPK)
    expert_idx = jnp.take_along_axis(cand_i, best_pos, axis=-1)
    gate = jax.nn.softmax(best_s, axis=-1)

    n_blk = T // PEER_TOKEN_BLOCK
    n_sel = PEER_HEADS * PEER_TOPK
    x_b = xt.reshape(n_blk, PEER_TOKEN_BLOCK, D)
    idx_b = expert_idx.reshape(n_blk, PEER_TOKEN_BLOCK, n_sel)
    gate_b = gate.reshape(n_blk, PEER_TOKEN_BLOCK, n_sel).astype(x.dtype)

    def block(args):
        xb, ib, gb = args
        u = jnp.take(u_tab, ib, axis=0)
        hid = jax.nn.gelu(jnp.einsum("td,ted->te", xb, u), approximate=False)
        vv = jnp.take(v_tab, ib, axis=0)
        return jnp.einsum("te,ted->td", gb * hid, vv)

    out = lax.map(block, (x_b, idx_b, gate_b))
    return out.reshape(B, S, D)


def setup_inputs(seed: int = 0) -> dict:
    key = jax.random.key(seed)
    ks = jax.random.split(key, 24)
    f32 = jnp.float32
    L = DEPTH

    def nrm(k, shape, scale):
        return jax.random.normal(k, shape, f32) * scale

    a_pow = jax.random.uniform(ks[10], (L, LRU_WIDTH), f32, 0.9, 0.999)
    sig = a_pow ** (1.0 / LRU_C)
    lru_lambda = jnp.log(sig) - jnp.log1p(-sig)
    return {
        "x": nrm(ks[0], (BATCH, SEQ, D_MODEL), 1.0),
        "p": nrm(ks[1], (DEPTH, BATCH, SEQ, PLE_DIM), 1.0),
        "w_in": nrm(ks[2], (L, D_MODEL, IN_WIDTH), D_MODEL ** -0.5),
        "conv_w": nrm(ks[3], (L, CONV_WIDTH, LRU_WIDTH), CONV_WIDTH ** -0.5),
        "conv_b": nrm(ks[4], (L, LRU_WIDTH), 0.01),
        "lru_wr": nrm(ks[5], (L, LRU_BLOCKS, LRU_BLOCK, LRU_BLOCK), LRU_BLOCK ** -0.5),
        "lru_br": nrm(ks[6], (L, LRU_BLOCKS, LRU_BLOCK), 0.01),
        "lru_wi": nrm(ks[7], (L, LRU_BLOCKS, LRU_BLOCK, LRU_BLOCK), LRU_BLOCK ** -0.5),
        "lru_bi": nrm(ks[8], (L, LRU_BLOCKS, LRU_BLOCK), 0.01),
        "lru_lambda": lru_lambda,
        "gla_wf2": nrm(ks[9], (L, GLA_GATE_RANK, GLA_HEADS * GLA_DK), GLA_GATE_RANK ** -0.5),
        "gla_bf": nrm(ks[11], (L, GLA_HEADS * GLA_DK), 0.01),
        "gla_norm_g": 1.0 + nrm(ks[12], (L, GLA_HEADS * GLA_DV), 0.02),
        "w_out": nrm(ks[13], (L, D_MODEL, D_MODEL), DN_BETA * D_MODEL ** -0.5),
        "ln1_g": 1.0 + nrm(ks[14], (L, D_MODEL), 0.02),
        "ln1_b": nrm(ks[15], (L, D_MODEL), 0.01),
        "peer_wq": nrm(ks[16], (L, D_MODEL, PEER_HEADS * PEER_QDIM), D_MODEL ** -0.5),
        "peer_subkeys": nrm(ks[17], (L, PEER_HEADS, 2, PEER_KEYS, PEER_HALF), PEER_HALF ** -0.5),
        "peer_u": nrm(ks[18], (L, PEER_EXPERTS, D_MODEL), D_MODEL ** -0.5),
        "peer_v": nrm(ks[19], (L, PEER_EXPERTS, D_MODEL), DN_BETA * PEER_HEADS ** -0.5),
        "ple_gate_w": nrm(ks[20], (L, D_MODEL, D_MODEL), D_MODEL ** -0.5),
        "ple_proj_w": nrm(ks[21], (L, PLE_DIM, D_MODEL), PLE_DIM ** -0.5),
        "ln2_g": 1.0 + nrm(ks[22], (L, D_MODEL), 0.02),
        "ln2_b": nrm(ks[23], (L, D_MODEL), 0.01),
    }


def reference(x, p, w_in, conv_w, conv_b, lru_wr, lru_br, lru_wi, lru_bi, lru_lambda,
              gla_wf2, gla_bf, gla_norm_g, w_out, ln1_g, ln1_b, peer_wq, peer_subkeys,
              peer_u, peer_v, ple_gate_w, ple_proj_w, ln2_g, ln2_b):
    for i in range(DEPTH):
        mix = hybrid_mixer(x, w_in[i], conv_w[i], conv_b[i], lru_wr[i], lru_br[i], lru_wi[i],
                           lru_bi[i], lru_lambda[i], gla_wf2[i], gla_bf[i], gla_norm_g[i], w_out[i])
        x = layer_norm(DN_ALPHA * x + mix, ln1_g[i], ln1_b[i])
        ple = jax.nn.sigmoid(x @ ple_gate_w[i]) * (p[i] @ ple_proj_w[i])
        ffn = peer(x, peer_wq[i], peer_subkeys[i], peer_u[i], peer_v[i])
        x = layer_norm(DN_ALPHA * x + ffn + ple, ln2_g[i], ln2_b[i])
    return x
```

```python
import numpy as np
from contextlib import ExitStack
import concourse.bass as bass
import concourse.mybir as mybir
from concourse.bass_utils import run_bass_kernel_spmd

F32 = mybir.dt.float32
F32R = mybir.dt.float32r
BF16 = mybir.dt.bfloat16
U32 = mybir.dt.uint32
AF = mybir.ActivationFunctionType
ALU = mybir.AluOpType
AX = mybir.AxisListType

SEQ = 2048
DM = 1024
NT = SEQ // 128
ALPHA = 2.0 ** 0.25
LN_EPS = 1e-5
RMS_EPS = 1e-6
OFF_X, OFF_Y, OFF_Q, OFF_K, OFF_V, OFF_R, OFF_F, OFF_GA, OFF_GB = 0, 1024, 2048, 2560, 3072, 4096, 5120, 5136, 6160
IN_W = 7184
NE = 16384
TB = 256
W_CHUNK_COLS = ([OFF_X + n * 128 for n in range(8)] + [OFF_Y + n * 128 for n in range(8)] + [OFF_Q + h * 128 for h in range(4)]
                + [OFF_K + h * 128 for h in range(4)] + [OFF_R + n * 128 for n in range(8)] + [OFF_F]
                + [OFF_GA + n * 128 for n in range(8)] + [OFF_GB + n * 128 for n in range(8)])
W_CHUNK_IDX = {c: i for i, c in enumerate(W_CHUNK_COLS)}


class Tok:
    __slots__ = ("name", "w", "r")

    def __init__(self, name=""):
        self.name = name
        self.w = []
        self.r = []


class Sched:
    def __init__(self, nc, es):
        self.nc = nc
        self.es = es
        self.eng = {"pe": nc.tensor, "act": nc.scalar, "dve": nc.vector, "pool": nc.gpsimd, "sp": nc.sync}
        self.sem = {k: es.enter_context(nc.semaphore("s_" + k)) for k in self.eng}
        self.cnt = {k: 0 for k in self.eng}
        self.known = {k: {} for k in self.eng}
        self.dsem = {}
        self.dcnt = {}

    def _wait(self, ek, deps):
        e = self.eng[ek]
        best = {}
        for kind, key, val in deps:
            if kind == "c" and key == ek and ek == "pe":
                continue
            k2 = (kind, key)
            if best.get(k2, 0) < val:
                best[k2] = val
        for (kind, key), val in best.items():
            if self.known[ek].get((kind, key), 0) >= val:
                continue
            e.wait_ge(self.sem[key] if kind == "c" else self.dsem[key], val)
            self.known[ek][(kind, key)] = val

    @staticmethod
    def _deps(reads, writes):
        deps = []
        for b in reads:
            deps += b.w
        for b in writes:
            deps += b.w
            deps += b.r
        return deps

    def _guard(self, writes):
        pend = getattr(self, "_deferred", None)
        if pend:
            ids = set(id(b) for b in writes)
            for a, kw in pend:
                if any(id(b) in ids for b in kw.get("reads", ())):
                    self.flush()
                    return

    def op(self, ek, fn, reads=(), writes=()):
        self._guard(writes)
        self._wait(ek, self._deps(reads, writes))
        inst = fn(self.eng[ek])
        self.cnt[ek] += 1
        inst.then_inc(self.sem[ek], 1)
        me = ("c", ek, self.cnt[ek])
        for b in reads:
            b.r.append(me)
        for b in writes:
            b.w = [me]
            b.r = []
        return me

    def dma(self, qk, tok, out, in_, reads=(), writes=(), append=False):
        name = tok.name
        if name not in self.dsem:
            self.dsem[name] = self.es.enter_context(self.nc.semaphore("d_" + name))
            self.dcnt[name] = 0
        self._guard(writes)
        self._wait(qk, self._deps(reads, writes))
        inst = self.eng[qk].dma_start(out=out, in_=in_)
        self.dcnt[name] += 16
        inst.then_inc(self.dsem[name], 16)
        me = ("d", name, self.dcnt[name])
        for b in reads:
            b.r.append(me)
        for b in writes:
            b.w = (b.w + [me]) if append else [me]
            b.r = []
        return me

    def defer(self, *a, **kw):
        if not hasattr(self, "_deferred"):
            self._deferred = []
        self._deferred.append((a, kw))

    def flush(self):
        for a, kw in getattr(self, "_deferred", []):
            self.dma(*a, **kw)
        self._deferred = []

    def barrier(self, toks=()):
        self.flush()
        deps = [("c", k, v) for k, v in self.cnt.items() if v > 0]
        deps += [("d", k, v) for k, v in self.dcnt.items() if v > 0]
        for ek in self.eng:
            self._wait(ek, [d for d in deps if not (d[0] == "c" and d[1] == ek)])


class Rot:
    def __init__(self, items):
        self.items = items
        self.i = 0

    def next(self):
        it = self.items[self.i % len(self.items)]
        self.i += 1
        return it


def build_nc(dbg=False, phases="ABCD"):
    nc = bass.Bass("TRN2", target_bir_lowering=False)
    nc.dge_precook = False

    def dram(name, shape, dtype=F32, kind="ExternalInput"):
        return nc.dram_tensor(name, shape, dtype, kind=kind).ap()

    xT_d = dram("xT", [DM, SEQ], F32R)
    x_d = dram("x", [SEQ, DM])
    pT_d = dram("pT", [256, SEQ], F32R)
    wch_d = dram("wch", [len(W_CHUNK_COLS), 128, DM], F32R)
    wv_d = dram("wvh", [4, 128, 8 * 256], F32R)
    convw_d = dram("convw", [128, 8, 4])
    pvec_d = dram("pvec", [128, 8, 5])
    bf_d = dram("bf", [128, 4])
    wr_d = dram("wr", [8, 128, 128], F32R)
    wi_d = dram("wi", [8, 128, 128], F32R)
    wf2_d = dram("wf2", [16, 512])
    w_out_d = dram("w_out", [DM, DM], F32R)
    wq_d = dram("wq", [DM, DM], F32R)
    wpg_d = dram("wpg", [DM, DM], F32R)
    wpp_d = dram("wpp", [256, DM], F32R)
    ln_d = dram("ln", [4, DM])
    skT_d = dram("skT", [128, 8, 128], F32R)
    uT_d = dram("uT", [128, 128, DM], F32R)
    v_d = dram("v", [NE, DM], F32R)
    okind = "ExternalOutput"
    out_d = dram("out", [SEQ, DM], F32, okind)
    skind = okind if dbg else "Internal"
    mT_d = dram("mT_s", [DM, SEQ], F32R, skind)
    x1_d = dram("x1_s", [SEQ, DM], F32, skind)
    x1T_d = dram("x1T_s", [DM, SEQ], F32R, skind)
    acc_d = dram("acc_s", [SEQ, DM], F32, skind)
    s_d = dram("sc_s", [NT, 128, 8 * 256], F32, skind)
    if dbg:
        sel_d = dram("sel_s", [3, 128, SEQ], F32, okind)

    with ExitStack() as es:
        S = Sched(nc, es)

        def sb(stk, name, shape, dtype=F32):
            return stk.enter_context(nc.sbuf_tensor("sb_" + name, shape, dtype)), Tok(name)

        banks = []
        for i in range(8):
            banks.append((es.enter_context(nc.psum_tensor("ps%d" % i, [128, 512], F32)), Tok("ps%d" % i)))
        es.enter_context(nc.Block())

        iot, iot_k = sb(es, "iot", [128, 128])
        pid, pid_k = sb(es, "pid", [128, 1])
        ident, ident_k = sb(es, "ident", [128, 128])
        triu, triu_k = sb(es, "triu", [128, 128])
        S.op("pool", lambda e: e.iota(iot[:], [[1, 128]], base=0, channel_multiplier=0, allow_small_or_imprecise_dtypes=True), writes=[iot_k])
        S.op("pool", lambda e: e.iota(pid[:], [[0, 1]], base=0, channel_multiplier=1, allow_small_or_imprecise_dtypes=True), writes=[pid_k])
        S.op("dve", lambda e: e.tensor_scalar(ident[:], iot[:], pid[:, 0:1], None, op0=ALU.is_equal), reads=[iot_k, pid_k], writes=[ident_k])
        S.op("dve", lambda e: e.tensor_scalar(triu[:], iot[:], pid[:, 0:1], None, op0=ALU.is_ge), reads=[iot_k, pid_k], writes=[triu_k])
        def load_lnb(stk, gi):
            lnb, lnb_k = sb(stk, "lnb%d" % gi, [128, 2, DM])
            for i in range(2):
                S.dma("sp", lnb_k, lnb[:, i, :], ln_d[gi + i].partition_broadcast(128), writes=[lnb_k], append=True)
            return lnb, lnb_k

        def layer_norm(src, src_k, dst, dst_k, lnb_, tmp):
            lnb, lnb_k = lnb_
            st, st_k = tmp["st"]
            mv, mv_k = tmp["mv"]
            for c in range(2):
                S.op("dve", lambda e, c=c: e.bn_stats(st[:, c, :], src[:, c * 512:(c + 1) * 512]), reads=[src_k], writes=[st_k])
            S.op("dve", lambda e: e.bn_aggr(mv[:, 0:2], st[:].rearrange("p a b -> p (a b)")), reads=[st_k], writes=[mv_k])
            S.op("act", lambda e: e.activation(mv[:, 2:3], mv[:, 1:2], AF.Sqrt, bias=tmp["eps"][0][:, 0:1]), reads=[mv_k, tmp["eps"][1]], writes=[mv_k])
            S.op("dve", lambda e: e.reciprocal(mv[:, 3:4], mv[:, 2:3]), reads=[mv_k], writes=[mv_k])
            S.op("dve", lambda e: e.tensor_scalar(dst[:], src[:], mv[:, 0:1], mv[:, 3:4], op0=ALU.subtract, op1=ALU.mult), reads=[src_k, mv_k], writes=[dst_k])
            S.op("dve", lambda e: e.tensor_tensor(dst[:], dst[:], lnb[:, 0, :], ALU.mult), reads=[dst_k, lnb_k], writes=[dst_k])
            S.op("dve", lambda e: e.tensor_tensor(dst[:], dst[:], lnb[:, 1, :], ALU.add), reads=[dst_k, lnb_k], writes=[dst_k])

        epsln, epsln_k = sb(es, "epsln", [128, 1])
        epsrms, epsrms_k = sb(es, "epsrms", [128, 1])
        S.op("dve", lambda e: e.memset(epsln[:], LN_EPS), writes=[epsln_k])
        S.op("dve", lambda e: e.memset(epsrms[:], RMS_EPS), writes=[epsrms_k])
        lnst = sb(es, "lnst", [128, 2, 6])
        lnmv = sb(es, "lnmv", [128, 4])
        lntmp = {"st": lnst, "mv": lnmv, "eps": (epsln, epsln_k)}

        if "A" in phases:
          with ExitStack() as esA:
            xT, xT_k = sb(esA, "xTs", [128, 8, SEQ], F32R)
            for kc in range(8):
                S.dma("sp" if kc % 2 == 0 else "act", xT_k, xT[:, kc, :], xT_d[kc * 128:(kc + 1) * 128, :], writes=[xT_k], append=True)
            wbufs = Rot([sb(esA, "wb%d" % i, [128, 8, 128], F32R) for i in range(2)])
            cw, cw_k = sb(esA, "cw", [128, 8, 4])
            pv, pv_k = sb(esA, "pv", [128, 8, 5])
            bfs, bfs_k = sb(esA, "bfs", [128, 4])
            nbf, nbf_k = sb(esA, "nbf", [128, 4])
            nsp, nsp_k = sb(esA, "nsp", [128, 8])
            wr, wr_k = sb(esA, "wrs", [128, 8, 128], F32R)
            wi, wi_k = sb(esA, "wis", [128, 8, 128], F32R)
            wf2, wf2_k = sb(esA, "wf2s", [16, 512])
            S.dma("sp", cw_k, cw[:], convw_d, writes=[cw_k])
            S.dma("sp", pv_k, pv[:], pvec_d, writes=[pv_k])
            S.dma("sp", bfs_k, bfs[:], bf_d, writes=[bfs_k])
            S.dma("sp", wr_k, wr[:], wr_d.rearrange("n c d -> c n d"), writes=[wr_k])
            S.dma("sp", wi_k, wi[:], wi_d.rearrange("n c d -> c n d"), writes=[wi_k])
            S.dma("sp", wf2_k, wf2[:], wf2_d, writes=[wf2_k])
            S.op("act", lambda e: e.activation(nsp[:], pv[:, :, 1], AF.Exp, scale=-1.0), reads=[pv_k], writes=[nsp_k])
            S.op("act", lambda e: e.activation(nsp[:], nsp[:], AF.Ln, bias=1.0), reads=[nsp_k], writes=[nsp_k])
            S.op("dve", lambda e: e.tensor_scalar(nsp[:], nsp[:], -8.0, None, op0=ALU.mult), reads=[nsp_k], writes=[nsp_k])
            S.op("dve", lambda e: e.tensor_scalar(nbf[:], bfs[:], -1.0, None, op0=ALU.mult), reads=[bfs_k], writes=[nbf_k])
            tl = [sb(esA, "tl%d" % i, [128, SEQ + 4]) for i in range(5)]
            zbank = Rot(banks[0:4])

            def zsection(col, evac, flush=True):
                wb, wk = wbufs.next()
                S.dma("sp", wk, wb[:].rearrange("p k c -> p (k c)"), wch_d[W_CHUNK_IDX[col]], writes=[wk])
                if flush:
                    S.flush()
                for tt in range(4):
                    pb, pk = zbank.next()

                    def mm(e, pb=pb, tt=tt, wb=wb):
                        r = None
                        for kc in range(8):
                            r = e.matmul(pb[:, :], lhsT=wb[:, kc, :], rhs=xT[:, kc, tt * 512:(tt + 1) * 512], start=(kc == 0), stop=(kc == 7))
                        return r
                    S.op("pe", mm, reads=[wk, xT_k], writes=[pk])
                    evac(tt, pb, pk)

            def act_evac(dst, dst_k, func, off=0, **kw):
                def f(tt, pb, pk):
                    S.op("act", lambda e: e.activation(dst[:, off + tt * 512: off + (tt + 1) * 512], pb[:, :], func, **kw), reads=[pk], writes=[dst_k])
                return f


            with ExitStack() as esG:
                flT, flT_k = sb(esG, "flT", [16, SEQ])
                ones1, ones1_k = sb(esG, "ones1", [128, 128])
                S.op("pool", lambda e: e.memset(ones1[:], 1.0), writes=[ones1_k])
                v_sb, v_k = sb(esG, "v_sb", [128, NT, 256], F32R)
                khat, khat_k = sb(esG, "khat", [128, NT, 128], F32R)
                oT, oT_k = sb(esG, "oT", [128, 2, SEQ])
                S_sb, S_k = sb(esG, "S_sb", [128, 256], F32R)
                attn = Rot([sb(esG, "attn%d" % i, [128, 128], F32R) for i in range(2)])
                on_ = Rot([sb(esG, "on%d" % i, [128, 256]) for i in range(2)])
                ss_ = Rot([sb(esG, "ss%d" % i, [128, 2]) for i in range(2)])
                junk, junk_k = sb(esG, "junk", [128, 256])
                wv = Rot([sb(esG, "wv%d" % i, [128, 8, 256], F32R) for i in range(1)])
                qt, qt_k = sb(esG, "qtR", [128, SEQ], F32R)
                kt, kt_k = sb(esG, "ktR", [128, SEQ], F32R)
                (lB, lB_k), (Eb, Eb_k), (Ei, Ei_k) = tl[0:3]
                (kh, kh_k), (sr, sr_k), (sg, sg_k) = tl[0], tl[3], tl[4]
                pk_tr, pk_dS, pk_at, pk_ms = banks[2], banks[3], banks[4], banks[7]
                obank = Rot([banks[5], banks[6]])
                zsection(OFF_F, lambda tt, pb, pk: S.op("act", lambda e: e.copy(flT[:, tt * 512:(tt + 1) * 512], pb[0:16, :]), reads=[pk], writes=[flT_k]))
                for h in range(4):
                    for tt in range(4):
                        pb, pk = zbank.next()
                        S.op("pe", lambda e, pb=pb, tt=tt: e.matmul(pb[:, :], lhsT=wf2[:, h * 128:(h + 1) * 128], rhs=flT[:, tt * 512:(tt + 1) * 512], start=True, stop=True),
                             reads=[wf2_k, flT_k], writes=[pk])
                        S.op("act", lambda e, pb=pb, tt=tt: e.activation(lB[:, tt * 512:(tt + 1) * 512], pb[:, :], AF.Exp, scale=-1.0, bias=nbf[:, h:h + 1]), reads=[pk, nbf_k], writes=[lB_k])
                    S.op("act", lambda e: e.activation(lB[:, 0:SEQ], lB[:, 0:SEQ], AF.Ln, bias=1.0), reads=[lB_k], writes=[lB_k])
                    for c in range(NT):
                        S.op("dve", lambda e, c=c: e.tensor_tensor_scan(Ei[:, c * 128:(c + 1) * 128], ones1[:], lB[:, c * 128:(c + 1) * 128], 0.0, ALU.mult, ALU.add), reads=[lB_k, ones1_k], writes=[Ei_k])
                    S.op("act", lambda e: e.activation(Eb[:, 0:SEQ], Ei[:, 0:SEQ], AF.Exp, scale=-1.0 / 16.0), reads=[Ei_k], writes=[Eb_k])
                    S.op("act", lambda e: e.activation(Ei[:, 0:SEQ], Ei[:, 0:SEQ], AF.Exp, scale=1.0 / 16.0), reads=[Ei_k], writes=[Ei_k])
                    zsection(OFF_Q + h * 128, lambda tt, pb, pk: S.op("dve", lambda e: e.scalar_tensor_tensor(out=qt[:, tt * 512:(tt + 1) * 512], in0=pb[:, :], scalar=128.0 ** -0.5, in1=Eb[:, tt * 512:(tt + 1) * 512], op0=ALU.mult, op1=ALU.mult), reads=[pk, Eb_k], writes=[qt_k]))
                    zsection(OFF_K + h * 128, lambda tt, pb, pk: S.op("dve", lambda e: e.tensor_tensor(kt[:, tt * 512:(tt + 1) * 512], pb[:, :], Ei[:, tt * 512:(tt + 1) * 512], ALU.mult), reads=[pk, Ei_k], writes=[kt_k]))
                    S.op("dve", lambda e: e.tensor_tensor(kh[:, 0:SEQ].rearrange("p (c k) -> p c k", k=128), kt[:, 0:SEQ].bitcast(F32).rearrange("p (c k) -> p c k", k=128),
                                                          Eb[:, 127:SEQ:128].unsqueeze(2).to_broadcast([128, NT, 128]), ALU.mult), reads=[kt_k, Eb_k], writes=[kh_k])
                    wvb, wvk = wv.next()
                    S.dma("sp", wvk, wvb[:].rearrange("p k c -> p (k c)"), wv_d[h], writes=[wvk])
                    for ti in range(NT):
                        pb, pk = zbank.next()

                        def mmv(e, pb=pb, ti=ti, wvb=wvb):
                            r = None
                            for kc in range(8):
                                r = e.matmul(pb[:, 0:256], lhsT=xT[:, kc, ti * 128:(ti + 1) * 128], rhs=wvb[:, kc, :], start=(kc == 0), stop=(kc == 7))
                            return r
                        S.op("pe", mmv, reads=[wvk, xT_k], writes=[pk])
                        S.op("act", lambda e, pb=pb, ti=ti: e.copy(v_sb[:, ti, :], pb[:, 0:256]), reads=[pk], writes=[v_k])
                    for g in range(NT // 4):
                        pb, pk = pk_tr

                        def trs(e, g=g, pb=pb):
                            r = None
                            for j in range(4):
                                c = g * 4 + j
                                r = e.transpose(pb[:, j * 128:(j + 1) * 128], kh[:, c * 128:(c + 1) * 128], ident[:])
                            return r
                        S.op("pe", trs, reads=[kh_k, ident_k], writes=[pk])
                        S.op("act", lambda e, g=g, pb=pb: e.copy(khat[:, g * 4:(g + 1) * 4, :], pb[:, :].rearrange("p (j d) -> p j d", d=128)), reads=[pk], writes=[khat_k])
                    pend_tr = None

                    def do_tr(st):
                        on, on_k, cs_ = st

                        def trs(e):
                            r = None
                            for ec in range(2):
                                r = e.transpose(pk_ms[0][:, ec * 128:(ec + 1) * 128], on[:, ec * 128:(ec + 1) * 128], ident[:])
                            return r
                        S.op("pe", trs, reads=[on_k, ident_k], writes=[pk_ms[1]])
                        S.op("act", lambda e: e.copy(oT[:, :, cs_], pk_ms[0][:, 0:256].rearrange("p (a b) -> p a b", b=128)), reads=[pk_ms[1]], writes=[oT_k])

                    pend_rms = None

                    def do_rms(st):
                        ob, ok_, cs_ = st
                        ssb, ss_k = ss_.next()
                        S.op("act", lambda e: e.activation(junk[:], ob[:, 0:256], AF.Square, accum_out=ssb[:, 0:1]), reads=[ok_], writes=[ss_k, junk_k])
                        S.op("act", lambda e: e.activation(ssb[:, 1:2], ssb[:, 0:1], AF.Sqrt, scale=1.0 / 256.0, bias=epsrms[:, 0:1]), reads=[ss_k, epsrms_k], writes=[ss_k])
                        S.op("dve", lambda e: e.reciprocal(ssb[:, 1:2], ssb[:, 1:2]), reads=[ss_k], writes=[ss_k])
                        on, on_k = on_.next()
                        S.op("dve", lambda e: e.tensor_scalar(on[:], ob[:, 0:256], ssb[:, 1:2], None, op0=ALU.mult), reads=[ok_, ss_k], writes=[on_k])
                        return (on, on_k, cs_)

                    for c in range(NT):
                        cs = slice(c * 128, (c + 1) * 128)
                        at_sb, at_k = attn.next()
                        S.op("pe", lambda e: e.matmul(pk_at[0][:, 0:128], lhsT=kt[:, cs], rhs=qt[:, cs], start=True, stop=True), reads=[kt_k, qt_k], writes=[pk_at[1]])
                        S.op("dve", lambda e: e.tensor_tensor(at_sb[:], pk_at[0][:, 0:128], triu[:], ALU.mult), reads=[pk_at[1], triu_k], writes=[at_k])
                        ob, ok_ = obank.next()

                        def mmo(e, ob=ob, c=c, at_sb=at_sb, cs=cs):
                            r = e.matmul(ob[:, 0:256], lhsT=at_sb[:], rhs=v_sb[:, c, :], start=True, stop=(c == 0))
                            if c > 0:
                                r = e.matmul(ob[:, 0:256], lhsT=qt[:, cs], rhs=S_sb[:], start=False, stop=True)
                            return r
                        S.op("pe", mmo, reads=[v_k, at_k, S_k, qt_k], writes=[ok_])
                        if c < NT - 1:
                            S.op("pe", lambda e: e.matmul(pk_dS[0][:, 0:256], lhsT=khat[:, c, :], rhs=v_sb[:, c, :], start=True, stop=True), reads=[khat_k, v_k], writes=[pk_dS[1]])
                            if c == 0:
                                S.op("dve", lambda e: e.tensor_copy(S_sb[:], pk_dS[0][:, 0:256]), reads=[pk_dS[1]], writes=[S_k])
                            else:
                                S.op("dve", lambda e: e.scalar_tensor_tensor(out=S_sb[:], in0=S_sb[:].bitcast(F32), scalar=Eb[:, c * 128 + 127: c * 128 + 128], in1=pk_dS[0][:, 0:256], op0=ALU.mult, op1=ALU.add),
                                     reads=[S_k, Eb_k, pk_dS[1]], writes=[S_k])
                        new_tr = do_rms(pend_rms) if pend_rms is not None else None
                        if pend_tr is not None:
                            do_tr(pend_tr)
                        pend_tr = new_tr
                        pend_rms = (ob, ok_, cs)
                    new_tr = do_rms(pend_rms)
                    if pend_tr is not None:
                        do_tr(pend_tr)
                    do_tr(new_tr)
                    for ec in range(2):
                        n = 2 * h + ec
                        zsection(OFF_R + n * 128, act_evac(sr, sr_k, AF.Silu))
                        zsection(OFF_GB + n * 128, act_evac(sg, sg_k, AF.Sigmoid))
                        S.op("dve", lambda e, ec=ec, n=n: e.scalar_tensor_tensor(out=sr[:, 0:SEQ], in0=oT[:, ec, :], scalar=pv[:, n, 4:5], in1=sr[:, 0:SEQ], op0=ALU.mult, op1=ALU.mult),
                             reads=[oT_k, pv_k, sr_k], writes=[sr_k])
                        S.op("dve", lambda e: e.tensor_tensor(sg[:, 0:SEQ].bitcast(F32R), sr[:, 0:SEQ], sg[:, 0:SEQ], ALU.mult), reads=[sr_k, sg_k], writes=[sg_k])
                        S.defer("sp", sg_k, mT_d[n * 128:(n + 1) * 128, :], sg[:, 0:SEQ].bitcast(F32R), reads=[sg_k])
            S.barrier()
            tl = tl + [sb(esA, "tl%d" % i, [128, SEQ + 4]) for i in range(5, 11)]
            wbufs.items.append(sb(esA, "wb2", [128, 8, 128], F32R))
            zbank.items = [banks[i] for i in (0, 1, 2, 3, 6, 7)]
            (zxp, zxp_k), (u32, u32_k), (t3, t3_k) = tl[0:3]
            ra_ = [tl[3], tl[4]]
            ia_ = [tl[5], tl[6]]
            gy_ = [tl[7], tl[8]]
            sga_ = [tl[9], tl[10]]
            ur_ = [sb(esA, "ur%d" % i, [128, SEQ], F32R) for i in range(2)]
            mb, mb_k = sb(esA, "mb", [128, SEQ], F32R)
            S.op("pool", lambda e: e.memset(zxp[:, 0:3], 0.0), writes=[zxp_k])
            pr, pi = banks[4], banks[5]
            for n in range(8):
                (ra, ra_k), (ia, ia_k), (gy, gy_k), (sga, sga_k), (ur, ur_k) = ra_[n % 2], ia_[n % 2], gy_[n % 2], sga_[n % 2], ur_[n % 2]
                zsection(OFF_X + n * 128, lambda tt, pb, pk: S.op("act", lambda e: e.copy(zxp[:, 3 + tt * 512: 3 + (tt + 1) * 512], pb[:, :]), reads=[pk], writes=[zxp_k]), flush=False)
                zsection(OFF_Y + n * 128, act_evac(gy, gy_k, AF.Gelu), flush=False)
                zsection(OFF_GA + n * 128, act_evac(sga, sga_k, AF.Sigmoid), flush=False)
                S.flush()
                S.dma("sp", mb_k, mb[:], mT_d[n * 128:(n + 1) * 128, :], writes=[mb_k])
                S.op("dve", lambda e: e.tensor_scalar(u32[:, 0:SEQ], zxp[:, 3:3 + SEQ], cw[:, n, 3:4], pv[:, n, 0:1], op0=ALU.mult, op1=ALU.add), reads=[zxp_k, cw_k, pv_k], writes=[u32_k])
                for j in (2, 1):
                    S.op("dve", lambda e, j=j: e.scalar_tensor_tensor(out=u32[:, 0:SEQ], in0=zxp[:, j:j + SEQ], scalar=cw[:, n, j:j + 1], in1=u32[:, 0:SEQ], op0=ALU.mult, op1=ALU.add),
                         reads=[zxp_k, cw_k, u32_k], writes=[u32_k])
                S.op("dve", lambda e: e.scalar_tensor_tensor(out=ur[:, 0:SEQ], in0=zxp[:, 0:SEQ], scalar=cw[:, n, 0:1], in1=u32[:, 0:SEQ], op0=ALU.mult, op1=ALU.add),
                     reads=[zxp_k, cw_k, u32_k], writes=[ur_k])
                for tt in range(4):
                    ts_ = slice(tt * 512, (tt + 1) * 512)
                    S.op("pe", lambda e: e.matmul(pr[0][:, :], lhsT=wr[:, n, :], rhs=ur[:, ts_], start=True, stop=True), reads=[ur_k, wr_k], writes=[pr[1]])
                    S.op("act", lambda e: e.activation(ra[:, ts_], pr[0][:, :], AF.Sigmoid, bias=pv[:, n, 2:3]), reads=[pr[1], pv_k], writes=[ra_k])
                    S.op("pe", lambda e: e.matmul(pi[0][:, :], lhsT=wi[:, n, :], rhs=ur[:, ts_], start=True, stop=True), reads=[ur_k, wi_k], writes=[pi[1]])
                    S.op("act", lambda e: e.activation(ia[:, ts_], pi[0][:, :], AF.Sigmoid, bias=pv[:, n, 3:4]), reads=[pi[1], pv_k], writes=[ia_k])
                S.op("act", lambda e: e.activation(ra[:, 0:SEQ], ra[:, 0:SEQ], AF.Exp, scale=nsp[:, n:n + 1]), reads=[ra_k, nsp_k], writes=[ra_k])
                S.op("act", lambda e: e.activation(t3[:, 0:SEQ], ra[:, 0:SEQ], AF.Square), reads=[ra_k], writes=[t3_k])
                S.op("act", lambda e: e.activation(t3[:, 0:SEQ], t3[:, 0:SEQ], AF.Sqrt, scale=-1.0, bias=1.0), reads=[t3_k], writes=[t3_k])
                S.op("dve", lambda e: e.tensor_tensor(ia[:, 0:SEQ], ia[:, 0:SEQ], ur[:, 0:SEQ].bitcast(F32), ALU.mult), reads=[ia_k, ur_k], writes=[ia_k])
                S.op("dve", lambda e: e.tensor_tensor(ia[:, 0:SEQ], ia[:, 0:SEQ], t3[:, 0:SEQ], ALU.mult), reads=[ia_k, t3_k], writes=[ia_k])
                S.op("dve", lambda e: e.tensor_tensor_scan(t3[:, 0:SEQ], ra[:, 0:SEQ], ia[:, 0:SEQ], 0.0, ALU.mult, ALU.add), reads=[ra_k, ia_k, t3_k], writes=[t3_k])
                S.op("dve", lambda e: e.tensor_tensor(gy[:, 0:SEQ], gy[:, 0:SEQ], sga[:, 0:SEQ], ALU.mult), reads=[gy_k, sga_k], writes=[gy_k])
                S.op("dve", lambda e: e.tensor_tensor(gy[:, 0:SEQ], gy[:, 0:SEQ], t3[:, 0:SEQ], ALU.mult), reads=[gy_k, t3_k], writes=[gy_k])
                S.op("dve", lambda e: e.tensor_tensor(mb[:], gy[:, 0:SEQ], mb[:].bitcast(F32), ALU.add), reads=[gy_k, mb_k], writes=[mb_k])
                S.defer("sp", mb_k, mT_d[n * 128:(n + 1) * 128, :], mb[:], reads=[mb_k])
          S.barrier()

        if "B" in phases:
          with ExitStack() as esB:
            lnb1 = load_lnb(esB, 0)
            wo, wo_k = sb(esB, "wo", [128, 8, DM], F32R)
            for kc in range(8):
                S.dma("sp", wo_k, wo[:, kc, :], w_out_d[kc * 128:(kc + 1) * 128, :], writes=[wo_k], append=True)
            wpg, wpg_k = sb(esB, "wpg", [128, 8, DM], F32R)
            wpp, wpp_k = sb(esB, "wpp", [128, 2, DM], F32R)
            pg = Rot([sb(esB, "pg%d" % i, [128, 2, 512], F32R) for i in range(2)])
            sgt = Rot([sb(esB, "sgt%d" % i, [128, DM]) for i in range(2)])
            pg_cur = [None]
            mg = Rot([sb(esB, "mg%d" % i, [128, 8, 512], F32R) for i in range(2)])
            xt = Rot([sb(esB, "xt%d" % i, [128, DM]) for i in range(2)])
            yt = Rot([sb(esB, "yt%d" % i, [128, DM]) for i in range(2)])
            x1t = Rot([sb(esB, "x1t%d" % i, [128, DM]) for i in range(3)])
            x1Tt = Rot([sb(esB, "x1Tt%d" % i, [128, 8, 128], F32R) for i in range(3)])
            mixb = Rot([(banks[0], banks[1])])
            trb = Rot([(banks[2], banks[3])])
            gateb = (banks[4], banks[5])
            ppb_ = (banks[6], banks[7])
            mg_cur = [None]

            def b_front(ti):
                gi, tj = divmod(ti, 4)
                if tj == 0:
                    mgb, mgk = mg.next()
                    S.dma("sp", mgk, mgb[:], mT_d[:, gi * 512:(gi + 1) * 512].rearrange("(k p) t -> p k t", p=128), writes=[mgk])
                    mg_cur[0] = (mgb, mgk)
                    pgb, pgk = pg.next()
                    S.dma("sp", pgk, pgb[:], pT_d[:, gi * 512:(gi + 1) * 512].rearrange("(k p) t -> p k t", p=128), writes=[pgk])
                    pg_cur[0] = (pgb, pgk)
                    if ti == 0:
                        for kc in range(8):
                            S.dma("sp", wpg_k, wpg[:, kc, :], wpg_d[kc * 128:(kc + 1) * 128, :], writes=[wpg_k], append=True)
                        for kc in range(2):
                            S.dma("sp", wpp_k, wpp[:, kc, :], wpp_d[kc * 128:(kc + 1) * 128, :], writes=[wpp_k], append=True)
                mgb, mgk = mg_cur[0]
                xb, xk = xt.next()
                S.dma("act", xk, xb[:], x_d[ti * 128:(ti + 1) * 128, :], writes=[xk])
                S.flush()
                (b0, b1) = mixb.next()
                for half, (pb, pk) in enumerate((b0, b1)):
                    def mm(e, pb=pb, half=half):
                        r = None
                        for kc in range(8):
                            r = e.matmul(pb[:, :], lhsT=mgb[:, kc, tj * 128:(tj + 1) * 128], rhs=wo[:, kc, half * 512:(half + 1) * 512], start=(kc == 0), stop=(kc == 7))
                        return r
                    S.op("pe", mm, reads=[mgk, wo_k], writes=[pk])
                yb, yk = yt.next()
                for half, (pb, pk) in enumerate((b0, b1)):
                    hs = slice(half * 512, (half + 1) * 512)
                    S.op("dve", lambda e, pb=pb, hs=hs: e.scalar_tensor_tensor(out=yb[:, hs], in0=xb[:, hs], scalar=ALPHA, in1=pb[:, :], op0=ALU.mult, op1=ALU.add), reads=[xk, pk], writes=[yk])
                x1b, x1k = x1t.next()
                layer_norm(yb, yk, x1b, x1k, lnb1, lntmp)
                return (ti, x1b, x1k, pg_cur[0])

            def b_back1(st):
                ti, x1b, x1k, (pgb, pgk) = st
                (t0, t1) = trb.next()
                x1Tb, x1Tk = x1Tt.next()
                for half, (pb, pk) in enumerate((t0, t1)):
                    def trs(e, pb=pb, half=half):
                        r = None
                        for j in range(4):
                            kc = half * 4 + j
                            r = e.transpose(pb[:, j * 128:(j + 1) * 128], x1b[:, kc * 128:(kc + 1) * 128], ident[:])
                        return r
                    S.op("pe", trs, reads=[x1k, ident_k], writes=[pk])
                    S.op("act", lambda e, pb=pb, half=half: e.copy(x1Tb[:, half * 4:(half + 1) * 4, :], pb[:, :].rearrange("p (j d) -> p j d", d=128)), reads=[pk], writes=[x1Tk])
                S.defer("sp", x1Tk, x1T_d[:, ti * 128:(ti + 1) * 128].rearrange("(k p) t -> p k t", p=128), x1Tb[:], reads=[x1Tk])
                return st + (x1Tb, x1Tk)

            def b_back2(st):
                ti, x1b, x1k, (pgb, pgk), x1Tb, x1Tk = st
                tsl = slice((ti % 4) * 128, (ti % 4 + 1) * 128)
                sgb, sgk = sgt.next()
                for half in range(2):
                    hs = slice(half * 512, (half + 1) * 512)
                    pbk, pkk = gateb[half]

                    def mmg(e, pbk=pbk, hs=hs):
                        r = None
                        for kc in range(8):
                            r = e.matmul(pbk[:, :], lhsT=x1Tb[:, kc, :], rhs=wpg[:, kc, hs], start=(kc == 0), stop=(kc == 7))
                        return r
                    S.op("pe", mmg, reads=[x1Tk, wpg_k], writes=[pkk])
                    S.op("act", lambda e, pbk=pbk, hs=hs: e.activation(sgb[:, hs], pbk[:, :], AF.Sigmoid), reads=[pkk], writes=[sgk])
                    ppb, ppk = ppb_[half]

                    def mmp(e, ppb=ppb, hs=hs):
                        r = None
                        for kc in range(2):
                            r = e.matmul(ppb[:, :], lhsT=pgb[:, kc, tsl], rhs=wpp[:, kc, hs], start=(kc == 0), stop=(kc == 1))
                        return r
                    S.op("pe", mmp, reads=[pgk, wpp_k], writes=[ppk])
                    S.op("dve", lambda e, ppb=ppb, hs=hs: e.tensor_tensor(sgb[:, hs], sgb[:, hs], ppb[:, :], ALU.mult), reads=[sgk, ppk], writes=[sgk])
                S.op("dve", lambda e: e.scalar_tensor_tensor(out=sgb[:], in0=x1b[:], scalar=ALPHA, in1=sgb[:], op0=ALU.mult, op1=ALU.add), reads=[x1k, sgk], writes=[sgk])
                S.defer("sp", sgk, acc_d[ti * 128:(ti + 1) * 128, :], sgb[:], reads=[sgk])

            st1 = None
            st2 = None
            for ti in range(NT):
                cur = b_front(ti)
                if st2 is not None:
                    b_back2(st2)
                st2 = b_back1(st1) if st1 is not None else None
                st1 = cur
            S.flush()
            if st2 is not None:
                b_back2(st2)
            st2 = b_back1(st1)
            S.flush()
            b_back2(st2)
          S.barrier()

        if "C" in phases:
          with ExitStack() as esCD:
            NSEL = 3
            iT, _ = sb(esCD, "iT", [128, NSEL * TB])
            jT, _ = sb(esCD, "jT", [128, NSEL * TB])
            gT, _ = sb(esCD, "gT", [128, NSEL * TB])
            selk = [(Tok("iT%d" % i), Tok("jT%d" % i), Tok("gT%d" % i)) for i in range(NSEL)]
            s_sbD, s_kD = sb(esCD, "s_sbD", [128, 8, 256])
            sw, sw_k = sb(esCD, "sw", [128, 256])
            top, top_k = sb(esCD, "top", [128, 8, 2, 16])
            idxu, idxu_k = sb(esCD, "idxu", [128, 8, 2, 16], U32)
            idxf, idxf_k = sb(esCD, "idxf", [128, 8, 2, 16])
            cand, cand_k = sb(esCD, "cand", [128, 8, 256])
            c16, c16_k = sb(esCD, "c16", [128, 8, 16])
            posu, posu_k = sb(esCD, "posu", [128, 8, 16], U32)
            abu, abu_k = sb(esCD, "abu", [128, 2, 8, 16], U32)
            abf, abf_k = sb(esCD, "abf", [128, 2, 8, 16])
            eq, eq_k = cand[:].rearrange("p h (a b) -> p h a b", b=16), cand_k
            exs = [sb(esCD, "ex%d" % i, [128, 8, 16]) for i in range(4)]
            zz, zz_k = sb(esCD, "zz", [128, 8])
            ijgs = [sb(esCD, "ijg%d" % i, [128, 3, 128]) for i in range(4)]

            SELPAD = 6

            def select_tile(ti, src_=None, trbank=None, ijg_=None, ex_=None):
                blk_ = ti // (TB // 128)
                sl = blk_ % NSEL
                col = sl * TB + (ti % (TB // 128)) * 128
                ijg, ijg_k = ijg_ if ijg_ is not None else ijgs[0]
                ex, ex_k = ex_ if ex_ is not None else exs[0]
                if src_ is None:
                    s_sb, s_k = s_sbD, s_kD
                    S.dma("sp", s_k, s_sb[:].rearrange("p h k -> p (h k)"), s_d[ti], writes=[s_k])
                    yield
                else:
                    s_sb, s_k = src_
                for h in range(8):
                    for p in range(2):
                        src = s_sb[:, h, p * 128:(p + 1) * 128]
                        S.op("dve", lambda e: e.max(top[:, h, p, 0:8], src), reads=[s_k], writes=[top_k])
                        yield
                        S.op("dve", lambda e: e.max_index(idxu[:, h, p, 0:8], top[:, h, p, 0:8], src), reads=[s_k, top_k], writes=[idxu_k])
                        yield
                        S.op("dve", lambda e: e.match_replace(sw[:, 0:128], top[:, h, p, 0:8], src, -1e30), reads=[s_k, top_k], writes=[sw_k])
                        yield
                        S.op("dve", lambda e: e.max(top[:, h, p, 8:16], sw[:, 0:128]), reads=[sw_k], writes=[top_k])
                        yield
                        S.op("dve", lambda e: e.max_index(idxu[:, h, p, 8:16], top[:, h, p, 8:16], sw[:, 0:128]), reads=[sw_k, top_k], writes=[idxu_k])
                        yield
                S.op("dve", lambda e: e.tensor_copy(idxf[:], idxu[:]), reads=[idxu_k], writes=[idxf_k])
                yield
                S.op("dve", lambda e: e.tensor_tensor(cand[:].rearrange("p h (a b) -> p h a b", b=16), top[:, :, 0, :].unsqueeze(3).to_broadcast([128, 8, 16, 16]),
                                                      top[:, :, 1, :].unsqueeze(2).to_broadcast([128, 8, 16, 16]), ALU.add), reads=[top_k], writes=[cand_k])
                yield
                for h in range(8):
                    src = cand[:, h, :]
                    S.op("dve", lambda e: e.max(c16[:, h, 0:8], src), reads=[cand_k], writes=[c16_k])
                    yield
                    S.op("dve", lambda e: e.max_index(posu[:, h, 0:8], c16[:, h, 0:8], src), reads=[cand_k, c16_k], writes=[posu_k])
                    yield
                    S.op("dve", lambda e: e.match_replace(sw[:], c16[:, h, 0:8], src, -1e30), reads=[cand_k, c16_k], writes=[sw_k])
                    yield
                    S.op("dve", lambda e: e.max(c16[:, h, 8:16], sw[:]), reads=[sw_k], writes=[c16_k])
                    yield
                    S.op("dve", lambda e: e.max_index(posu[:, h, 8:16], c16[:, h, 8:16], sw[:]), reads=[sw_k, c16_k], writes=[posu_k])
                    yield
                S.op("dve", lambda e: e.tensor_tensor(ex[:], c16[:], c16[:, :, 0:1].to_broadcast([128, 8, 16]), ALU.subtract), reads=[c16_k], writes=[ex_k])
                yield
                S.op("dve", lambda e: e.tensor_single_scalar(abu[:, 0, :, :], posu[:], 4, ALU.logical_shift_right), reads=[posu_k], writes=[abu_k])
                yield
                S.op("dve", lambda e: e.tensor_single_scalar(abu[:, 1, :, :], posu[:], 15, ALU.bitwise_and), reads=[posu_k], writes=[abu_k])
                yield
                S.op("dve", lambda e: e.tensor_copy(abf[:], abu[:]), reads=[abu_k], writes=[abf_k])
                yield
                for p in range(2):
                    S.op("dve", lambda e: e.tensor_tensor(eq, abf[:, p, :, :].unsqueeze(3).to_broadcast([128, 8, 16, 16]),
                                                          iot[:, 0:16].unsqueeze(1).unsqueeze(1).to_broadcast([128, 8, 16, 16]), ALU.is_equal), reads=[abf_k, iot_k], writes=[eq_k])
                    yield
                    S.op("dve", lambda e: e.tensor_tensor(eq, eq, idxf[:, :, p, :].unsqueeze(2).to_broadcast([128, 8, 16, 16]), ALU.mult), reads=[eq_k, idxf_k], writes=[eq_k])
                    yield
                    S.op("dve", lambda e: e.tensor_reduce(ijg[:, p, :], eq.rearrange("p h k a -> p (h k) a"), AX.X, ALU.add), reads=[eq_k], writes=[ijg_k])
                    yield
                yield "act"
                for _ in range(SELPAD):
                    yield
                S.op("act", lambda e: e.activation(ex[:], ex[:], AF.Exp), reads=[ex_k], writes=[ex_k])
                for _ in range(SELPAD):
                    yield
                S.op("dve", lambda e: e.tensor_reduce(zz[:], ex[:], AX.X, ALU.add), reads=[ex_k], writes=[zz_k])
                yield
                S.op("dve", lambda e: e.reciprocal(zz[:], zz[:]), reads=[zz_k], writes=[zz_k])
                yield
                S.op("dve", lambda e: e.tensor_tensor(ijg[:, 2, :].rearrange("p (h k) -> p h k", k=16), ex[:], zz[:].unsqueeze(2).to_broadcast([128, 8, 16]), ALU.mult), reads=[ex_k, zz_k], writes=[ijg_k])
                yield
                yield "pe"
                for _ in range(SELPAD):
                    yield
                pb, pk = trbank if trbank is not None else gbk.next()

                def trs(e):
                    r = None
                    for j in range(3):
                        r = e.transpose(pb[:, j * 128:(j + 1) * 128], ijg[:, j, :], ident[:])
                    return r
                S.op("pe", trs, reads=[ijg_k, ident_k], writes=[pk])
                for j, dst in enumerate((iT, jT, gT)):
                    S.op("act", lambda e: e.copy(dst[:, col:col + 128], pb[:, j * 128:(j + 1) * 128]), reads=[pk], writes=[selk[sl][j]])
                yield
                if dbg:
                    for j, dst in enumerate((iT, jT, gT)):
                        S.dma("sp", selk[sl][j], sel_d[j, :, ti * 128:(ti + 1) * 128], dst[:, col:col + 128], reads=[selk[sl][j]])

            def select_block(blk_):
                for tt_ in range(TB // 128):
                    yield from select_tile(blk_ * (TB // 128) + tt_)

            gbk = Rot([banks[6], banks[7]])

            with ExitStack() as esC:
                wq, wq_k = sb(esC, "wq", [128, 8, DM], F32R)
                for kc in range(8):
                    S.dma("sp", wq_k, wq[:, kc, :], wq_d[kc * 128:(kc + 1) * 128, :], writes=[wq_k], append=True)
                bd, bd_k = sb(esC, "bd", [128, 8, 256], F32R)
                S.op("dve", lambda e: e.tensor_scalar(bd[:].rearrange("p h k -> p (h k)"), iot[:, 0:1].to_broadcast([128, 2048]), 0.0, None, op0=ALU.mult), reads=[iot_k], writes=[bd_k])
                S.dma("sp", bd_k, bd[0:64, :, 0:128], skT_d[0:64], writes=[bd_k], append=True)
                S.dma("sp", bd_k, bd[64:128, :, 128:256], skT_d[64:128], writes=[bd_k], append=True)
                xg = Rot([sb(esC, "xq%d" % i, [128, 8, 512], F32R) for i in range(2)])
                qT, qT_k = sb(esC, "qT", [128, 8, 512], F32R)
                ssb = Rot([sb(esC, "s_sb%d" % i, [128, 8, 256]) for i in range(2)])
                ssb_early = [sb(esC, "s_sbe%d" % i, [128, 8, 256]) for i in range(TB // 128)]
                zb = Rot(banks[0:2])
                early_sel = []
                for gi in range(4):
                    xgb, xgk = xg.next()
                    S.dma("sp", xgk, xgb[:], x1T_d[:, gi * 512:(gi + 1) * 512].rearrange("(k p) t -> p k t", p=128), writes=[xgk])
                    for h in range(8):
                        pb, pk = zb.next()

                        def mmq(e, pb=pb, h=h, xgb=xgb):
                            r = None
                            for kc in range(8):
                                r = e.matmul(pb[:, :], lhsT=wq[:, kc, h * 128:(h + 1) * 128], rhs=xgb[:, kc, :], start=(kc == 0), stop=(kc == 7))
                            return r
                        S.op("pe", mmq, reads=[wq_k, xgk], writes=[pk])
                        S.op("act", lambda e, pb=pb, h=h: e.copy(qT[:, h, :], pb[:, :]), reads=[pk], writes=[qT_k])
                    for tj in range(4):
                        ti = gi * 4 + tj
                        tsl = slice(tj * 128, (tj + 1) * 128)
                        s_sb, s_k = ssb_early[ti] if ti < TB // 128 else ssb.next()
                        for hp in range(4):
                            pb, pk = banks[2 + hp]

                            def mms(e, pb=pb, hp=hp, tsl=tsl):
                                r = None
                                for hh in range(2):
                                    h = hp * 2 + hh
                                    r = e.matmul(pb[:, hh * 256:(hh + 1) * 256], lhsT=qT[:, h, tsl], rhs=bd[:, h, :], start=True, stop=True)
                                return r
                            S.op("pe", mms, reads=[qT_k, bd_k], writes=[pk])
                            S.op("act", lambda e, pb=pb, hp=hp: e.copy(s_sb[:, hp * 2:hp * 2 + 2, :], pb[:, :].rearrange("p (a b) -> p a b", b=256)), reads=[pk], writes=[s_k])
                        if ti < TB // 128:
                            g_ = select_tile(ti, src_=(s_sb, s_k), trbank=banks[6 + ti % 2], ijg_=ijgs[ti], ex_=exs[ti])
                            for r_ in g_:
                                if r_ == "act":
                                    break
                            early_sel.append(g_)
                        else:
                            S.dma("sp", s_k, s_d[ti], s_sb[:].rearrange("p h k -> p (h k)"), reads=[s_k])
                for g_ in early_sel:
                    for _ in g_:
                        pass
            S.barrier()
            if "D" in phases:
              with ExitStack() as esD:
                lnb2 = load_lnb(esD, 2)
                Gh = [sb(esD, "G%d" % i, [128, TB, 64], BF16) for i in range(2)]
                iotb, iotb_k = sb(esD, "iotb", [128, 128], BF16)
                S.op("dve", lambda e: e.tensor_copy(iotb[:], iot[:]), reads=[iot_k], writes=[iotb_k])
                GT = 8
                AB = Rot([sb(esD, "AB%d" % i, [128, GT, 192], BF16) + (Tok("ABb%d" % i),) for i in range(4)])
                xb_ = Rot([sb(esD, "xD%d" % i, [128, 8, TB], F32R) for i in range(2)])
                CH = 2
                ub_ = Rot([sb(esD, "ub%d" % i, [128, CH, DM], F32R) for i in range(3)])
                vb_ = Rot([sb(esD, "vb%d" % i, [128, CH, DM], F32R) for i in range(3)])
                gel_ = Rot([sb(esD, "gel%d" % i, [128, TB]) for i in range(2)])
                W_ = Rot([sb(esD, "W%d" % i, [128, TB], F32R) for i in range(2)])
                accb = Rot([sb(esD, "accb%d" % i, [128, DM]) for i in range(1)])
                yb_ = Rot([sb(esD, "yD%d" % i, [128, DM]) for i in range(2)])
                outb = [banks[0], banks[1], banks[2], banks[3]]
                hb = Rot([banks[4], banks[5]])
                NBLK = SEQ // TB

                def g_onehots_a(blk, half, q):
                    ab, ab_k, abb_k = AB.next()
                    sl = blk % NSEL
                    iT_k, jT_k, gT_k = selk[sl]
                    ts_ = slice(sl * TB + q * GT, sl * TB + (q + 1) * GT)
                    S.op("dve", lambda e: e.tensor_tensor(ab[:, :, 0:64], iotb[:, half * 64:(half + 1) * 64].unsqueeze(1).to_broadcast([128, GT, 64]),
                                                          iT[:, ts_].unsqueeze(2).to_broadcast([128, GT, 64]), ALU.is_equal), reads=[iotb_k, iT_k], writes=[ab_k])
                    S.op("pool", lambda e: e.tensor_tensor(ab[:, :, 0:64], ab[:, :, 0:64], gT[:, ts_].unsqueeze(2).to_broadcast([128, GT, 64]), ALU.mult), reads=[gT_k, ab_k], writes=[ab_k])
                    return (ab, ab_k, abb_k, half, q, ts_, jT_k)

                def g_onehots_b(st):
                    ab, ab_k, abb_k, half, q, ts_, jT_k = st
                    S.op("dve", lambda e: e.tensor_tensor(ab[:, :, 64:192], iotb[:].unsqueeze(1).to_broadcast([128, GT, 128]),
                                                          jT[:, ts_].unsqueeze(2).to_broadcast([128, GT, 128]), ALU.is_equal), reads=[iotb_k, jT_k], writes=[abb_k])
                    return st

                def g_matmul(st):
                    ab, ab_k, abb_k, half, q, ts_, jT_k = st
                    pb, pk = gbk.next()

                    def mmG(e):
                        r = None
                        for tl_ in range(GT):
                            r = e.matmul(pb[:, tl_ * 64:(tl_ + 1) * 64], lhsT=ab[:, tl_, 64:192], rhs=ab[:, tl_, 0:64], start=True, stop=True)
                        return r
                    S.op("pe", mmG, reads=[ab_k, abb_k], writes=[pk])
                    g_, gk_ = Gh[half]
                    S.op("act", lambda e: e.copy(g_[:, q * GT:(q + 1) * GT, :], pb[:, 0:GT * 64].rearrange("p (t i) -> p t i", i=64)), reads=[pk], writes=[gk_])

                for q in range(TB // GT):
                    g_matmul(g_onehots_b(g_onehots_a(0, 0, q)))

                def mmV(e, Wt, vb, cc, ci):
                    r = None
                    for tt in range(TB // 128):
                        for half in range(2):
                            r = e.matmul(outb[tt * 2 + half][0][:, :], lhsT=Wt[:, tt * 128:(tt + 1) * 128], rhs=vb[:, cc, half * 512:(half + 1) * 512], start=(ci == 0), stop=(ci == 127))
                    return r

                for blk in range(NBLK):
                    t0 = blk * TB
                    if blk == 0:
                        xnext = xb_.next()
                        S.dma("sp", xnext[1], xnext[0][:], x1T_d[:, 0:TB].rearrange("(k p) t -> p k t", p=128), writes=[xnext[1]])
                    xb, xk = xnext
                    if blk + 1 < NBLK:
                        xnext = xb_.next()
                        S.dma("sp", xnext[1], xnext[0][:], x1T_d[:, t0 + TB:t0 + 2 * TB].rearrange("(k p) t -> p k t", p=128), writes=[xnext[1]])
                    pend = None
                    if blk == 0:
                        def _chain():
                            yield from select_block(1)
                            yield from select_block(2)
                        selgen = _chain()
                        selrate = 6
                    else:
                        selgen = select_block(blk + 2) if blk + 2 < NBLK else None
                        selrate = 3
                    gcur = None
                    gq = []
                    for cg in range(128 // CH):
                        ub, uk = ub_.next()
                        vb, vk = vb_.next()
                        S.dma("sp", uk, ub[:], uT_d[cg * CH:(cg + 1) * CH].rearrange("c p f -> p c f"), writes=[uk])
                        S.dma("act", vk, vb[:], v_d[cg * CH * 128:(cg + 1) * CH * 128, :].rearrange("(c p) d -> p c d", p=128), writes=[vk])
                        if cg == 2:
                            S.flush()
                        for cc in range(CH):
                            ci = cg * CH + cc
                            pb, pk = hb.next()

                            def mmH(e, pb=pb, cc=cc, ub=ub, xb=xb):
                                r = None
                                for kc in range(8):
                                    r = e.matmul(pb[:, 0:TB], lhsT=ub[:, cc, kc * 128:(kc + 1) * 128], rhs=xb[:, kc, :], start=(kc == 0), stop=(kc == 7))
                                return r
                            S.op("pe", mmH, reads=[uk, xk], writes=[pk])
                            gl, gl_k = gel_.next()
                            S.op("act", lambda e, pb=pb, gl=gl: e.activation(gl[:], pb[:, 0:TB], AF.Gelu), reads=[pk], writes=[gl_k])
                            if pend is not None:
                                pW, pWk, pvb, pvk, pcc, pci = pend
                                S.op("pe", lambda e: mmV(e, pW, pvb, pcc, pci), reads=[pWk, pvk], writes=[outb[i][1] for i in range(4)])
                            Wt, W_k = W_.next()
                            g_, gk_ = Gh[ci // 64]
                            S.op("pool", lambda e, gl=gl, Wt=Wt, ci=ci: e.tensor_tensor(Wt[:], gl[:], g_[:, :, ci % 64], ALU.mult), reads=[gl_k, gk_], writes=[W_k])
                            pend = (Wt, W_k, vb, vk, cc, ci)
                            if selgen is not None:
                                for _ in range(selrate if ci < 127 else 100000):
                                    if next(selgen, "done") == "done":
                                        selgen = None
                                        break
                            if ci % 2 == 0:
                                if len(gq) >= 2:
                                    g_matmul(gq.pop(0))
                                if ci < 64:
                                    gcur = g_onehots_a(blk, 1, ci // 2)
                                elif blk + 1 < NBLK:
                                    gcur = g_onehots_a(blk + 1, 0, (ci - 64) // 2)
                            else:
                                if gcur is not None:
                                    gq.append(g_onehots_b(gcur))
                                    gcur = None
                                if ci == 63 or ci == 127:
                                    while gq:
                                        g_matmul(gq.pop(0))
                    pW, pWk, pvb, pvk, pcc, pci = pend
                    S.op("pe", lambda e: mmV(e, pW, pvb, pcc, pci), reads=[pWk, pvk], writes=[outb[i][1] for i in range(4)])
                    for tt in range(TB // 128):
                        ti = blk * (TB // 128) + tt
                        ab2, ab2_k = accb.next()
                        S.dma("act", ab2_k, ab2[:], acc_d[ti * 128:(ti + 1) * 128, :], writes=[ab2_k])
                        yb, yk = yb_.next()
                        for half in range(2):
                            hs = slice(half * 512, (half + 1) * 512)
                            pb, pk = outb[tt * 2 + half]
                            S.op("dve", lambda e, pb=pb, hs=hs: e.tensor_tensor(yb[:, hs], ab2[:, hs], pb[:, :], ALU.add), reads=[ab2_k, pk], writes=[yk])
                        ob, ok_ = yb, yk
                        layer_norm(yb, yk, ob, ok_, lnb2, lntmp)
                        S.defer("sp", ok_, out_d[ti * 128:(ti + 1) * 128, :], ob[:], reads=[ok_])
                S.barrier()
        S.barrier()
    return nc


def prep_inputs(inp, b):
    f = lambda a: np.ascontiguousarray(a, dtype=np.float32)
    x = inp["x"][b]
    d = {}
    d["xT"] = f(x.T)
    d["x"] = f(x)
    d["pT"] = f(inp["p"][0, b].T)
    w_in = inp["w_in"][0]
    d["wch"] = f(np.stack([w_in[:, c:c + 128].reshape(8, 128, 128).transpose(1, 0, 2).reshape(128, 1024) for c in W_CHUNK_COLS]))
    d["wvh"] = f(np.stack([w_in[:, OFF_V + h * 256: OFF_V + (h + 1) * 256].reshape(8, 128, 256).transpose(1, 0, 2).reshape(128, 2048) for h in range(4)]))
    d["convw"] = f(inp["conv_w"][0].reshape(4, 8, 128).transpose(2, 1, 0))
    pv = np.stack([inp["conv_b"][0], inp["lru_lambda"][0], inp["lru_br"][0].reshape(-1), inp["lru_bi"][0].reshape(-1), inp["gla_norm_g"][0]], axis=-1)
    d["pvec"] = f(pv.reshape(8, 128, 5).transpose(1, 0, 2))
    d["bf"] = f(inp["gla_bf"][0].reshape(4, 128).T)
    d["wr"] = f(inp["lru_wr"][0])
    d["wi"] = f(inp["lru_wi"][0])
    d["wf2"] = f(inp["gla_wf2"][0])
    d["w_out"] = f(inp["w_out"][0])
    d["wq"] = f(inp["peer_wq"][0])
    d["wpg"] = f(inp["ple_gate_w"][0])
    d["wpp"] = f(inp["ple_proj_w"][0])
    d["ln"] = f(np.stack([inp["ln1_g"][0], inp["ln1_b"][0], inp["ln2_g"][0], inp["ln2_b"][0]]))
    d["skT"] = f(inp["peer_subkeys"][0].transpose(1, 3, 0, 2).reshape(128, 8, 128))
    d["uT"] = f(inp["peer_u"][0].reshape(128, 128, 8, 128).transpose(0, 3, 2, 1).reshape(128, 128, 1024))
    d["v"] = f(inp["peer_v"][0])
    return d


def kernel(**inputs):
    inp = {k: np.asarray(v) for k, v in inputs.items()}
    n = 8
    nc = build_nc()
    shared = prep_inputs(inp, 0)
    in_maps = []
    for b in range(n):
        d = dict(shared)
        x = inp["x"][b]
        d["xT"] = np.ascontiguousarray(x.T, dtype=np.float32)
        d["x"] = np.ascontiguousarray(x, dtype=np.float32)
        d["pT"] = np.ascontiguousarray(inp["p"][0, b].T, dtype=np.float32)
        in_maps.append(d)
    res = run_bass_kernel_spmd(nc, in_maps, core_ids=list(range(n)))
    out = np.stack([np.asarray(r["out"]) for r in res.results], axis=0)
    return out.astype(np.float32)
```

```python
import numpy as np
from contextlib import ExitStack
import concourse.bass as bass
import concourse.mybir as mybir
from concourse.bass_utils import run_bass_kernel_spmd

F32 = mybir.dt.float32
F32R = mybir.dt.float32r
BF16 = mybir.dt.bfloat16
U32 = mybir.dt.uint32
AF = mybir.ActivationFunctionType
ALU = mybir.AluOpType
AX = mybir.AxisListType

SEQ = 2048
DM = 1024
NT = SEQ // 128
ALPHA = 2.0 ** 0.25
LN_EPS = 1e-5
RMS_EPS = 1e-6
OFF_X, OFF_Y, OFF_Q, OFF_K, OFF_V, OFF_R, OFF_F, OFF_GA, OFF_GB = 0, 1024, 2048, 2560, 3072, 4096, 5120, 5136, 6160
IN_W = 7184
NE = 16384
TB = 256
W_CHUNK_COLS = ([OFF_X + n * 128 for n in range(8)] + [OFF_Y + n * 128 for n in range(8)] + [OFF_Q + h * 128 for h in range(4)]
                + [OFF_K + h * 128 for h in range(4)] + [OFF_R + n * 128 for n in range(8)] + [OFF_F]
                + [OFF_GA + n * 128 for n in range(8)] + [OFF_GB + n * 128 for n in range(8)])
W_CHUNK_IDX = {c: i for i, c in enumerate(W_CHUNK_COLS)}


class Tok:
    __slots__ = ("name", "w", "r")

    def __init__(self, name=""):
        self.name = name
        self.w = []
        self.r = []


class Sched:
    def __init__(self, nc, es):
        self.nc = nc
        self.es = es
        self.eng = {"pe": nc.tensor, "act": nc.scalar, "dve": nc.vector, "pool": nc.gpsimd, "sp": nc.sync}
        self.sem = {k: es.enter_context(nc.semaphore("s_" + k)) for k in self.eng}
        self.cnt = {k: 0 for k in self.eng}
        self.known = {k: {} for k in self.eng}
        self.dsem = {}
        self.dcnt = {}

    def _wait(self, ek, deps):
        e = self.eng[ek]
        best = {}
        for kind, key, val in deps:
            if kind == "c" and key == ek and ek == "pe":
                continue
            k2 = (kind, key)
            if best.get(k2, 0) < val:
                best[k2] = val
        for (kind, key), val in best.items():
            if self.known[ek].get((kind, key), 0) >= val:
                continue
            e.wait_ge(self.sem[key] if kind == "c" else self.dsem[key], val)
            self.known[ek][(kind, key)] = val

    @staticmethod
    def _deps(reads, writes):
        deps = []
        for b in reads:
            deps += b.w
        for b in writes:
            deps += b.w
            deps += b.r
        return deps

    def _guard(self, writes):
        pend = getattr(self, "_deferred", None)
        if pend:
            ids = set(id(b) for b in writes)
            for a, kw in pend:
                if any(id(b) in ids for b in kw.get("reads", ())):
                    self.flush()
                    return

    def op(self, ek, fn, reads=(), writes=()):
        self._guard(writes)
        self._wait(ek, self._deps(reads, writes))
        inst = fn(self.eng[ek])
        self.cnt[ek] += 1
        inst.then_inc(self.sem[ek], 1)
        me = ("c", ek, self.cnt[ek])
        for b in reads:
            b.r.append(me)
        for b in writes:
            b.w = [me]
            b.r = []
        return me

    def dma(self, qk, tok, out, in_, reads=(), writes=(), append=False):
        name = tok.name
        if name not in self.dsem:
            self.dsem[name] = self.es.enter_context(self.nc.semaphore("d_" + name))
            self.dcnt[name] = 0
        self._guard(writes)
        self._wait(qk, self._deps(reads, writes))
        inst = self.eng[qk].dma_start(out=out, in_=in_)
        self.dcnt[name] += 16
        inst.then_inc(self.dsem[name], 16)
        me = ("d", name, self.dcnt[name])
        for b in reads:
            b.r.append(me)
        for b in writes:
            b.w = (b.w + [me]) if append else [me]
            b.r = []
        return me

    def defer(self, *a, **kw):
        if not hasattr(self, "_deferred"):
            self._deferred = []
        self._deferred.append((a, kw))

    def flush(self):
        for a, kw in getattr(self, "_deferred", []):
            self.dma(*a, **kw)
        self._deferred = []

    def barrier(self, toks=()):
        self.flush()
        deps = [("c", k, v) for k, v in self.cnt.items() if v > 0]
        deps += [("d", k, v) for k, v in self.dcnt.items() if v > 0]
        for ek in self.eng:
            self._wait(ek, [d for d in deps if not (d[0] == "c" and d[1] == ek)])


class Rot:
    def __init__(self, items):
        self.items = items
        self.i = 0

    def next(self):
        it = self.items[self.i % len(self.items)]
        self.i += 1
        return it


def build_nc(dbg=False, phases="ABCD"):
    nc = bass.Bass("TRN2", target_bir_lowering=False)
    nc.dge_precook = False

    def dram(name, shape, dtype=F32, kind="ExternalInput"):
        return nc.dram_tensor(name, shape, dtype, kind=kind).ap()

    xT_d = dram("xT", [DM, SEQ], F32R)
    x_d = dram("x", [SEQ, DM])
    pT_d = dram("pT", [256, SEQ], F32R)
    wch_d = dram("wch", [len(W_CHUNK_COLS), 128, DM], F32R)
    wv_d = dram("wvh", [4, 128, 8 * 256], F32R)
    convw_d = dram("convw", [128, 8, 4])
    pvec_d = dram("pvec", [128, 8, 5])
    bf_d = dram("bf", [128, 4])
    wr_d = dram("wr", [8, 128, 128], F32R)
    wi_d = dram("wi", [8, 128, 128], F32R)
    wf2_d = dram("wf2", [16, 512])
    w_out_d = dram("w_out", [DM, DM], F32R)
    wq_d = dram("wq", [DM, DM], F32R)
    wpg_d = dram("wpg", [DM, DM], F32R)
    wpp_d = dram("wpp", [256, DM], F32R)
    ln_d = dram("ln", [4, DM])
    skT_d = dram("skT", [128, 8, 128], F32R)
    uT_d = dram("uT", [128, 128, DM], F32R)
    v_d = dram("v", [NE, DM], F32R)
    okind = "ExternalOutput"
    out_d = dram("out", [SEQ, DM], F32, okind)
    skind = okind if dbg else "Internal"
    mT_d = dram("mT_s", [DM, SEQ], F32R, skind)
    x1_d = dram("x1_s", [SEQ, DM], F32, skind)
    x1T_d = dram("x1T_s", [DM, SEQ], F32R, skind)
    acc_d = dram("acc_s", [SEQ, DM], F32, skind)
    s_d = dram("sc_s", [NT, 128, 8 * 256], F32, skind)
    if dbg:
        sel_d = dram("sel_s", [3, 128, SEQ], F32, okind)

    with ExitStack() as es:
        S = Sched(nc, es)

        def sb(stk, name, shape, dtype=F32):
            return stk.enter_context(nc.sbuf_tensor("sb_" + name, shape, dtype)), Tok(name)

        banks = []
        for i in range(8):
            banks.append((es.enter_context(nc.psum_tensor("ps%d" % i, [128, 512], F32)), Tok("ps%d" % i)))
        es.enter_context(nc.Block())

        iot, iot_k = sb(es, "iot", [128, 128])
        pid, pid_k = sb(es, "pid", [128, 1])
        ident, ident_k = sb(es, "ident", [128, 128])
        triu, triu_k = sb(es, "triu", [128, 128])
        S.op("pool", lambda e: e.iota(iot[:], [[1, 128]], base=0, channel_multiplier=0, allow_small_or_imprecise_dtypes=True), writes=[iot_k])
        S.op("pool", lambda e: e.iota(pid[:], [[0, 1]], base=0, channel_multiplier=1, allow_small_or_imprecise_dtypes=True), writes=[pid_k])
        S.op("dve", lambda e: e.tensor_scalar(ident[:], iot[:], pid[:, 0:1], None, op0=ALU.is_equal), reads=[iot_k, pid_k], writes=[ident_k])
        S.op("dve", lambda e: e.tensor_scalar(triu[:], iot[:], pid[:, 0:1], None, op0=ALU.is_ge), reads=[iot_k, pid_k], writes=[triu_k])
        def load_lnb(stk, gi):
            lnb, lnb_k = sb(stk, "lnb%d" % gi, [128, 2, DM])
            for i in range(2):
                S.dma("sp", lnb_k, lnb[:, i, :], ln_d[gi + i].partition_broadcast(128), writes=[lnb_k], append=True)
            return lnb, lnb_k

        def layer_norm(src, src_k, dst, dst_k, lnb_, tmp):
            lnb, lnb_k = lnb_
            st, st_k = tmp["st"]
            mv, mv_k = tmp["mv"]
            for c in range(2):
                S.op("dve", lambda e, c=c: e.bn_stats(st[:, c, :], src[:, c * 512:(c + 1) * 512]), reads=[src_k], writes=[st_k])
            S.op("dve", lambda e: e.bn_aggr(mv[:, 0:2], st[:].rearrange("p a b -> p (a b)")), reads=[st_k], writes=[mv_k])
            S.op("act", lambda e: e.activation(mv[:, 2:3], mv[:, 1:2], AF.Sqrt, bias=tmp["eps"][0][:, 0:1]), reads=[mv_k, tmp["eps"][1]], writes=[mv_k])
            S.op("dve", lambda e: e.reciprocal(mv[:, 3:4], mv[:, 2:3]), reads=[mv_k], writes=[mv_k])
            S.op("dve", lambda e: e.tensor_scalar(dst[:], src[:], mv[:, 0:1], mv[:, 3:4], op0=ALU.subtract, op1=ALU.mult), reads=[src_k, mv_k], writes=[dst_k])
            S.op("dve", lambda e: e.tensor_tensor(dst[:], dst[:], lnb[:, 0, :], ALU.mult), reads=[dst_k, lnb_k], writes=[dst_k])
            S.op("dve", lambda e: e.tensor_tensor(dst[:], dst[:], lnb[:, 1, :], ALU.add), reads=[dst_k, lnb_k], writes=[dst_k])

        epsln, epsln_k = sb(es, "epsln", [128, 1])
        epsrms, epsrms_k = sb(es, "epsrms", [128, 1])
        S.op("dve", lambda e: e.memset(epsln[:], LN_EPS), writes=[epsln_k])
        S.op("dve", lambda e: e.memset(epsrms[:], RMS_EPS), writes=[epsrms_k])
        lnst = sb(es, "lnst", [128, 2, 6])
        lnmv = sb(es, "lnmv", [128, 4])
        lntmp = {"st": lnst, "mv": lnmv, "eps": (epsln, epsln_k)}

        if "A" in phases:
          with ExitStack() as esA:
            xT, xT_k = sb(esA, "xTs", [128, 8, SEQ], F32R)
            for kc in range(8):
                S.dma("sp" if kc % 2 == 0 else "act", xT_k, xT[:, kc, :], xT_d[kc * 128:(kc + 1) * 128, :], writes=[xT_k], append=True)
            wbufs = Rot([sb(esA, "wb%d" % i, [128, 8, 128], F32R) for i in range(2)])
            cw, cw_k = sb(esA, "cw", [128, 8, 4])
            pv, pv_k = sb(esA, "pv", [128, 8, 5])
            bfs, bfs_k = sb(esA, "bfs", [128, 4])
            nbf, nbf_k = sb(esA, "nbf", [128, 4])
            nsp, nsp_k = sb(esA, "nsp", [128, 8])
            wr, wr_k = sb(esA, "wrs", [128, 8, 128], F32R)
            wi, wi_k = sb(esA, "wis", [128, 8, 128], F32R)
            wf2, wf2_k = sb(esA, "wf2s", [16, 512])
            S.dma("sp", cw_k, cw[:], convw_d, writes=[cw_k])
            S.dma("sp", pv_k, pv[:], pvec_d, writes=[pv_k])
            S.dma("sp", bfs_k, bfs[:], bf_d, writes=[bfs_k])
            S.dma("sp", wr_k, wr[:], wr_d.rearrange("n c d -> c n d"), writes=[wr_k])
            S.dma("sp", wi_k, wi[:], wi_d.rearrange("n c d -> c n d"), writes=[wi_k])
            S.dma("sp", wf2_k, wf2[:], wf2_d, writes=[wf2_k])
            S.op("act", lambda e: e.activation(nsp[:], pv[:, :, 1], AF.Exp, scale=-1.0), reads=[pv_k], writes=[nsp_k])
            S.op("act", lambda e: e.activation(nsp[:], nsp[:], AF.Ln, bias=1.0), reads=[nsp_k], writes=[nsp_k])
            S.op("dve", lambda e: e.tensor_scalar(nsp[:], nsp[:], -8.0, None, op0=ALU.mult), reads=[nsp_k], writes=[nsp_k])
            S.op("dve", lambda e: e.tensor_scalar(nbf[:], bfs[:], -1.0, None, op0=ALU.mult), reads=[bfs_k], writes=[nbf_k])
            tl = [sb(esA, "tl%d" % i, [128, SEQ + 4]) for i in range(5)]
            zbank = Rot(banks[0:4])

            def zsection(col, evac, flush=True):
                wb, wk = wbufs.next()
                S.dma("sp", wk, wb[:].rearrange("p k c -> p (k c)"), wch_d[W_CHUNK_IDX[col]], writes=[wk])
                if flush:
                    S.flush()
                for tt in range(4):
                    pb, pk = zbank.next()

                    def mm(e, pb=pb, tt=tt, wb=wb):
                        r = None
                        for kc in range(8):
                            r = e.matmul(pb[:, :], lhsT=wb[:, kc, :], rhs=xT[:, kc, tt * 512:(tt + 1) * 512], start=(kc == 0), stop=(kc == 7))
                        return r
                    S.op("pe", mm, reads=[wk, xT_k], writes=[pk])
                    evac(tt, pb, pk)

            def act_evac(dst, dst_k, func, off=0, **kw):
                def f(tt, pb, pk):
                    S.op("act", lambda e: e.activation(dst[:, off + tt * 512: off + (tt + 1) * 512], pb[:, :], func, **kw), reads=[pk], writes=[dst_k])
                return f


            with ExitStack() as esG:
                flT, flT_k = sb(esG, "flT", [16, SEQ])
                ones1, ones1_k = sb(esG, "ones1", [128, 128])
                S.op("pool", lambda e: e.memset(ones1[:], 1.0), writes=[ones1_k])
                v_sb, v_k = sb(esG, "v_sb", [128, NT, 256], F32R)
                khat, khat_k = sb(esG, "khat", [128, NT, 128], F32R)
                oT, oT_k = sb(esG, "oT", [128, 2, SEQ])
                S_sb, S_k = sb(esG, "S_sb", [128, 256], F32R)
                attn = Rot([sb(esG, "attn%d" % i, [128, 128], F32R) for i in range(2)])
                on_ = Rot([sb(esG, "on%d" % i, [128, 256]) for i in range(2)])
                ss_ = Rot([sb(esG, "ss%d" % i, [128, 2]) for i in range(2)])
                junk, junk_k = sb(esG, "junk", [128, 256])
                wv = Rot([sb(esG, "wv%d" % i, [128, 8, 256], F32R) for i in range(1)])
                qt, qt_k = sb(esG, "qtR", [128, SEQ], F32R)
                kt, kt_k = sb(esG, "ktR", [128, SEQ], F32R)
                (lB, lB_k), (Eb, Eb_k), (Ei, Ei_k) = tl[0:3]
                (kh, kh_k), (sr, sr_k), (sg, sg_k) = tl[0], tl[3], tl[4]
                pk_tr, pk_dS, pk_at, pk_ms = banks[2], banks[3], banks[4], banks[7]
                obank = Rot([banks[5], banks[6]])
                zsection(OFF_F, lambda tt, pb, pk: S.op("act", lambda e: e.copy(flT[:, tt * 512:(tt + 1) * 512], pb[0:16, :]), reads=[pk], writes=[flT_k]))
                for h in range(4):
                    for tt in range(4):
                        pb, pk = zbank.next()
                        S.op("pe", lambda e, pb=pb, tt=tt: e.matmul(pb[:, :], lhsT=wf2[:, h * 128:(h + 1) * 128], rhs=flT[:, tt * 512:(tt + 1) * 512], start=True, stop=True),
                             reads=[wf2_k, flT_k], writes=[pk])
                        S.op("act", lambda e, pb=pb, tt=tt: e.activation(lB[:, tt * 512:(tt + 1) * 512], pb[:, :], AF.Exp, scale=-1.0, bias=nbf[:, h:h + 1]), reads=[pk, nbf_k], writes=[lB_k])
                    S.op("act", lambda e: e.activation(lB[:, 0:SEQ], lB[:, 0:SEQ], AF.Ln, bias=1.0), reads=[lB_k], writes=[lB_k])
                    for c in range(NT):
                        S.op("dve", lambda e, c=c: e.tensor_tensor_scan(Ei[:, c * 128:(c + 1) * 128], ones1[:], lB[:, c * 128:(c + 1) * 128], 0.0, ALU.mult, ALU.add), reads=[lB_k, ones1_k], writes=[Ei_k])
                    S.op("act", lambda e: e.activation(Eb[:, 0:SEQ], Ei[:, 0:SEQ], AF.Exp, scale=-1.0 / 16.0), reads=[Ei_k], writes=[Eb_k])
                    S.op("act", lambda e: e.activation(Ei[:, 0:SEQ], Ei[:, 0:SEQ], AF.Exp, scale=1.0 / 16.0), reads=[Ei_k], writes=[Ei_k])
                    zsection(OFF_Q + h * 128, lambda tt, pb, pk: S.op("dve", lambda e: e.scalar_tensor_tensor(out=qt[:, tt * 512:(tt + 1) * 512], in0=pb[:, :], scalar=128.0 ** -0.5, in1=Eb[:, tt * 512:(tt + 1) * 512], op0=ALU.mult, op1=ALU.mult), reads=[pk, Eb_k], writes=[qt_k]))
                    zsection(OFF_K + h * 128, lambda tt, pb, pk: S.op("dve", lambda e: e.tensor_tensor(kt[:, tt * 512:(tt + 1) * 512], pb[:, :], Ei[:, tt * 512:(tt + 1) * 512], ALU.mult), reads=[pk, Ei_k], writes=[kt_k]))
                    S.op("dve", lambda e: e.tensor_tensor(kh[:, 0:SEQ].rearrange("p (c k) -> p c k", k=128), kt[:, 0:SEQ].bitcast(F32).rearrange("p (c k) -> p c k", k=128),
                                                          Eb[:, 127:SEQ:128].unsqueeze(2).to_broadcast([128, NT, 128]), ALU.mult), reads=[kt_k, Eb_k], writes=[kh_k])
                    wvb, wvk = wv.next()
                    S.dma("sp", wvk, wvb[:].rearrange("p k c -> p (k c)"), wv_d[h], writes=[wvk])
                    for ti in range(NT):
                        pb, pk = zbank.next()

                        def mmv(e, pb=pb, ti=ti, wvb=wvb):
                            r = None
                            for kc in range(8):
                                r = e.matmul(pb[:, 0:256], lhsT=xT[:, kc, ti * 128:(ti + 1) * 128], rhs=wvb[:, kc, :], start=(kc == 0), stop=(kc == 7))
                            return r
                        S.op("pe", mmv, reads=[wvk, xT_k], writes=[pk])
                        S.op("act", lambda e, pb=pb, ti=ti: e.copy(v_sb[:, ti, :], pb[:, 0:256]), reads=[pk], writes=[v_k])
                    for g in range(NT // 4):
                        pb, pk = pk_tr

                        def trs(e, g=g, pb=pb):
                            r = None
                            for j in range(4):
                                c = g * 4 + j
                                r = e.matmul(pb[:, j * 128:(j + 1) * 128], lhsT=kh[:, c * 128:(c + 1) * 128], rhs=ident[:], start=True, stop=True)
                            return r
                        S.op("pe", trs, reads=[kh_k, ident_k], writes=[pk])
                        S.op("act", lambda e, g=g, pb=pb: e.copy(khat[:, g * 4:(g + 1) * 4, :], pb[:, :].rearrange("p (j d) -> p j d", d=128)), reads=[pk], writes=[khat_k])
                    pend_tr = None

                    def do_tr(st):
                        on, on_k, cs_ = st

                        def trs(e):
                            r = None
                            for ec in range(2):
                                r = e.matmul(pk_ms[0][:, ec * 128:(ec + 1) * 128], lhsT=on[:, ec * 128:(ec + 1) * 128], rhs=ident[:], start=True, stop=True)
                            return r
                        S.op("pe", trs, reads=[on_k, ident_k], writes=[pk_ms[1]])
                        S.op("act", lambda e: e.copy(oT[:, :, cs_], pk_ms[0][:, 0:256].rearrange("p (a b) -> p a b", b=128)), reads=[pk_ms[1]], writes=[oT_k])

                    pend_rms = None

                    def do_rms(st):
                        ob, ok_, cs_ = st
                        ssb, ss_k = ss_.next()
                        S.op("act", lambda e: e.activation(junk[:], ob[:, 0:256], AF.Square, accum_out=ssb[:, 0:1]), reads=[ok_], writes=[ss_k, junk_k])
                        S.op("act", lambda e: e.activation(ssb[:, 1:2], ssb[:, 0:1], AF.Sqrt, scale=1.0 / 256.0, bias=epsrms[:, 0:1]), reads=[ss_k, epsrms_k], writes=[ss_k])
                        S.op("dve", lambda e: e.reciprocal(ssb[:, 1:2], ssb[:, 1:2]), reads=[ss_k], writes=[ss_k])
                        on, on_k = on_.next()
                        S.op("dve", lambda e: e.tensor_scalar(on[:], ob[:, 0:256], ssb[:, 1:2], None, op0=ALU.mult), reads=[ok_, ss_k], writes=[on_k])
                        return (on, on_k, cs_)

                    for c in range(NT):
                        cs = slice(c * 128, (c + 1) * 128)
                        at_sb, at_k = attn.next()
                        S.op("pe", lambda e: e.matmul(pk_at[0][:, 0:128], lhsT=kt[:, cs], rhs=qt[:, cs], start=True, stop=True), reads=[kt_k, qt_k], writes=[pk_at[1]])
                        S.op("dve", lambda e: e.tensor_tensor(at_sb[:], pk_at[0][:, 0:128], triu[:], ALU.mult), reads=[pk_at[1], triu_k], writes=[at_k])
                        ob, ok_ = obank.next()

                        def mmo(e, ob=ob, c=c, at_sb=at_sb, cs=cs):
                            r = e.matmul(ob[:, 0:256], lhsT=at_sb[:], rhs=v_sb[:, c, :], start=True, stop=(c == 0))
                            if c > 0:
                                r = e.matmul(ob[:, 0:256], lhsT=qt[:, cs], rhs=S_sb[:], start=False, stop=True)
                            return r
                        S.op("pe", mmo, reads=[v_k, at_k, S_k, qt_k], writes=[ok_])
                        if c < NT - 1:
                            S.op("pe", lambda e: e.matmul(pk_dS[0][:, 0:256], lhsT=khat[:, c, :], rhs=v_sb[:, c, :], start=True, stop=True), reads=[khat_k, v_k], writes=[pk_dS[1]])
                            if c == 0:
                                S.op("dve", lambda e: e.tensor_copy(S_sb[:], pk_dS[0][:, 0:256]), reads=[pk_dS[1]], writes=[S_k])
                            else:
                                S.op("dve", lambda e: e.scalar_tensor_tensor(out=S_sb[:], in0=S_sb[:].bitcast(F32), scalar=Eb[:, c * 128 + 127: c * 128 + 128], in1=pk_dS[0][:, 0:256], op0=ALU.mult, op1=ALU.add),
                                     reads=[S_k, Eb_k, pk_dS[1]], writes=[S_k])
                        new_tr = do_rms(pend_rms) if pend_rms is not None else None
                        if pend_tr is not None:
                            do_tr(pend_tr)
                        pend_tr = new_tr
                        pend_rms = (ob, ok_, cs)
                    new_tr = do_rms(pend_rms)
                    if pend_tr is not None:
                        do_tr(pend_tr)
                    do_tr(new_tr)
                    for ec in range(2):
                        n = 2 * h + ec
                        zsection(OFF_R + n * 128, act_evac(sr, sr_k, AF.Silu))
                        zsection(OFF_GB + n * 128, act_evac(sg, sg_k, AF.Sigmoid))
                        S.op("dve", lambda e, ec=ec, n=n: e.scalar_tensor_tensor(out=sr[:, 0:SEQ], in0=oT[:, ec, :], scalar=pv[:, n, 4:5], in1=sr[:, 0:SEQ], op0=ALU.mult, op1=ALU.mult),
                             reads=[oT_k, pv_k, sr_k], writes=[sr_k])
                        S.op("dve", lambda e: e.tensor_tensor(sg[:, 0:SEQ].bitcast(F32R), sr[:, 0:SEQ], sg[:, 0:SEQ], ALU.mult), reads=[sr_k, sg_k], writes=[sg_k])
                        S.defer("sp", sg_k, mT_d[n * 128:(n + 1) * 128, :], sg[:, 0:SEQ].bitcast(F32R), reads=[sg_k])
            S.barrier()
            tl = tl + [sb(esA, "tl%d" % i, [128, SEQ + 4]) for i in range(5, 11)]
            wbufs.items.append(sb(esA, "wb2", [128, 8, 128], F32R))
            zbank.items = [banks[i] for i in (0, 1, 2, 3, 6, 7)]
            (zxp, zxp_k), (u32, u32_k), (t3, t3_k) = tl[0:3]
            ra_ = [tl[3], tl[4]]
            ia_ = [tl[5], tl[6]]
            gy_ = [tl[7], tl[8]]
            sga_ = [tl[9], tl[10]]
            ur_ = [sb(esA, "ur%d" % i, [128, SEQ], F32R) for i in range(2)]
            mb, mb_k = sb(esA, "mb", [128, SEQ], F32R)
            S.op("pool", lambda e: e.memset(zxp[:, 0:3], 0.0), writes=[zxp_k])
            pr, pi = banks[4], banks[5]
            for n in range(8):
                (ra, ra_k), (ia, ia_k), (gy, gy_k), (sga, sga_k), (ur, ur_k) = ra_[n % 2], ia_[n % 2], gy_[n % 2], sga_[n % 2], ur_[n % 2]
                zsection(OFF_X + n * 128, lambda tt, pb, pk: S.op("act", lambda e: e.copy(zxp[:, 3 + tt * 512: 3 + (tt + 1) * 512], pb[:, :]), reads=[pk], writes=[zxp_k]), flush=False)
                zsection(OFF_Y + n * 128, act_evac(gy, gy_k, AF.Gelu), flush=False)
                zsection(OFF_GA + n * 128, act_evac(sga, sga_k, AF.Sigmoid), flush=False)
                S.flush()
                S.dma("sp", mb_k, mb[:], mT_d[n * 128:(n + 1) * 128, :], writes=[mb_k])
                S.op("dve", lambda e: e.tensor_scalar(u32[:, 0:SEQ], zxp[:, 3:3 + SEQ], cw[:, n, 3:4], pv[:, n, 0:1], op0=ALU.mult, op1=ALU.add), reads=[zxp_k, cw_k, pv_k], writes=[u32_k])
                for j in (2, 1):
                    S.op("dve", lambda e, j=j: e.scalar_tensor_tensor(out=u32[:, 0:SEQ], in0=zxp[:, j:j + SEQ], scalar=cw[:, n, j:j + 1], in1=u32[:, 0:SEQ], op0=ALU.mult, op1=ALU.add),
                         reads=[zxp_k, cw_k, u32_k], writes=[u32_k])
                S.op("dve", lambda e: e.scalar_tensor_tensor(out=ur[:, 0:SEQ], in0=zxp[:, 0:SEQ], scalar=cw[:, n, 0:1], in1=u32[:, 0:SEQ], op0=ALU.mult, op1=ALU.add),
                     reads=[zxp_k, cw_k, u32_k], writes=[ur_k])
                for tt in range(4):
                    ts_ = slice(tt * 512, (tt + 1) * 512)
                    S.op("pe", lambda e: e.matmul(pr[0][:, :], lhsT=wr[:, n, :], rhs=ur[:, ts_], start=True, stop=True), reads=[ur_k, wr_k], writes=[pr[1]])
                    S.op("act", lambda e: e.activation(ra[:, ts_], pr[0][:, :], AF.Sigmoid, bias=pv[:, n, 2:3]), reads=[pr[1], pv_k], writes=[ra_k])
                    S.op("pe", lambda e: e.matmul(pi[0][:, :], lhsT=wi[:, n, :], rhs=ur[:, ts_], start=True, stop=True), reads=[ur_k, wi_k], writes=[pi[1]])
                    S.op("act", lambda e: e.activation(ia[:, ts_], pi[0][:, :], AF.Sigmoid, bias=pv[:, n, 3:4]), reads=[pi[1], pv_k], writes=[ia_k])
                S.op("act", lambda e: e.activation(ra[:, 0:SEQ], ra[:, 0:SEQ], AF.Exp, scale=nsp[:, n:n + 1]), reads=[ra_k, nsp_k], writes=[ra_k])
                S.op("act", lambda e: e.activation(t3[:, 0:SEQ], ra[:, 0:SEQ], AF.Square), reads=[ra_k], writes=[t3_k])
                S.op("act", lambda e: e.activation(t3[:, 0:SEQ], t3[:, 0:SEQ], AF.Sqrt, scale=-1.0, bias=1.0), reads=[t3_k], writes=[t3_k])
                S.op("dve", lambda e: e.tensor_tensor(ia[:, 0:SEQ], ia[:, 0:SEQ], ur[:, 0:SEQ].bitcast(F32), ALU.mult), reads=[ia_k, ur_k], writes=[ia_k])
                S.op("dve", lambda e: e.tensor_tensor(ia[:, 0:SEQ], ia[:, 0:SEQ], t3[:, 0:SEQ], ALU.mult), reads=[ia_k, t3_k], writes=[ia_k])
                S.op("dve", lambda e: e.tensor_tensor_scan(t3[:, 0:SEQ], ra[:, 0:SEQ], ia[:, 0:SEQ], 0.0, ALU.mult, ALU.add), reads=[ra_k, ia_k, t3_k], writes=[t3_k])
                S.op("dve", lambda e: e.tensor_tensor(gy[:, 0:SEQ], gy[:, 0:SEQ], sga[:, 0:SEQ], ALU.mult), reads=[gy_k, sga_k], writes=[gy_k])
                S.op("dve", lambda e: e.tensor_tensor(gy[:, 0:SEQ], gy[:, 0:SEQ], t3[:, 0:SEQ], ALU.mult), reads=[gy_k, t3_k], writes=[gy_k])
                S.op("dve", lambda e: e.tensor_tensor(mb[:], gy[:, 0:SEQ], mb[:].bitcast(F32), ALU.add), reads=[gy_k, mb_k], writes=[mb_k])
                S.defer("sp", mb_k, mT_d[n * 128:(n + 1) * 128, :], mb[:], reads=[mb_k])
          S.barrier()

        if "B" in phases:
          with ExitStack() as esB:
            lnb1 = load_lnb(esB, 0)
            wo, wo_k = sb(esB, "wo", [128, 8, DM], F32R)
            for kc in range(8):
                S.dma("sp", wo_k, wo[:, kc, :], w_out_d[kc * 128:(kc + 1) * 128, :], writes=[wo_k], append=True)
            wpg, wpg_k = sb(esB, "wpg", [128, 8, DM], F32R)
            wpp, wpp_k = sb(esB, "wpp", [128, 2, DM], F32R)
            pg = Rot([sb(esB, "pg%d" % i, [128, 2, 512], F32R) for i in range(2)])
            sgt = Rot([sb(esB, "sgt%d" % i, [128, DM]) for i in range(2)])
            pg_cur = [None]
            mg = Rot([sb(esB, "mg%d" % i, [128, 8, 512], F32R) for i in range(2)])
            xt = Rot([sb(esB, "xt%d" % i, [128, DM]) for i in range(2)])
            yt = Rot([sb(esB, "yt%d" % i, [128, DM]) for i in range(2)])
            x1t = Rot([sb(esB, "x1t%d" % i, [128, DM]) for i in range(3)])
            x1Tt = Rot([sb(esB, "x1Tt%d" % i, [128, 8, 128], F32R) for i in range(3)])
            mixb = Rot([(banks[0], banks[1])])
            trb = Rot([(banks[2], banks[3])])
            gateb = (banks[4], banks[5])
            ppb_ = (banks[6], banks[7])
            mg_cur = [None]

            def b_front(ti):
                gi, tj = divmod(ti, 4)
                if tj == 0:
                    mgb, mgk = mg.next()
                    S.dma("sp", mgk, mgb[:], mT_d[:, gi * 512:(gi + 1) * 512].rearrange("(k p) t -> p k t", p=128), writes=[mgk])
                    mg_cur[0] = (mgb, mgk)
                    pgb, pgk = pg.next()
                    S.dma("sp", pgk, pgb[:], pT_d[:, gi * 512:(gi + 1) * 512].rearrange("(k p) t -> p k t", p=128), writes=[pgk])
                    pg_cur[0] = (pgb, pgk)
                    if ti == 0:
                        for kc in range(8):
                            S.dma("sp", wpg_k, wpg[:, kc, :], wpg_d[kc * 128:(kc + 1) * 128, :], writes=[wpg_k], append=True)
                        for kc in range(2):
                            S.dma("sp", wpp_k, wpp[:, kc, :], wpp_d[kc * 128:(kc + 1) * 128, :], writes=[wpp_k], append=True)
                mgb, mgk = mg_cur[0]
                xb, xk = xt.next()
                S.dma("act", xk, xb[:], x_d[ti * 128:(ti + 1) * 128, :], writes=[xk])
                S.flush()
                (b0, b1) = mixb.next()
                for half, (pb, pk) in enumerate((b0, b1)):
                    def mm(e, pb=pb, half=half):
                        r = None
                        for kc in range(8):
                            r = e.matmul(pb[:, :], lhsT=mgb[:, kc, tj * 128:(tj + 1) * 128], rhs=wo[:, kc, half * 512:(half + 1) * 512], start=(kc == 0), stop=(kc == 7))
                        return r
                    S.op("pe", mm, reads=[mgk, wo_k], writes=[pk])
                yb, yk = yt.next()
                for half, (pb, pk) in enumerate((b0, b1)):
                    hs = slice(half * 512, (half + 1) * 512)
                    S.op("dve", lambda e, pb=pb, hs=hs: e.scalar_tensor_tensor(out=yb[:, hs], in0=xb[:, hs], scalar=ALPHA, in1=pb[:, :], op0=ALU.mult, op1=ALU.add), reads=[xk, pk], writes=[yk])
                x1b, x1k = x1t.next()
                layer_norm(yb, yk, x1b, x1k, lnb1, lntmp)
                return (ti, x1b, x1k, pg_cur[0])

            def b_back1(st):
                ti, x1b, x1k, (pgb, pgk) = st
                (t0, t1) = trb.next()
                x1Tb, x1Tk = x1Tt.next()
                for half, (pb, pk) in enumerate((t0, t1)):
                    def trs(e, pb=pb, half=half):
                        r = None
                        for j in range(4):
                            kc = half * 4 + j
                            r = e.matmul(pb[:, j * 128:(j + 1) * 128], lhsT=x1b[:, kc * 128:(kc + 1) * 128], rhs=ident[:], start=True, stop=True)
                        return r
                    S.op("pe", trs, reads=[x1k, ident_k], writes=[pk])
                    S.op("act", lambda e, pb=pb, half=half: e.copy(x1Tb[:, half * 4:(half + 1) * 4, :], pb[:, :].rearrange("p (j d) -> p j d", d=128)), reads=[pk], writes=[x1Tk])
                S.defer("sp", x1Tk, x1T_d[:, ti * 128:(ti + 1) * 128].rearrange("(k p) t -> p k t", p=128), x1Tb[:], reads=[x1Tk])
                return st + (x1Tb, x1Tk)

            def b_back2(st):
                ti, x1b, x1k, (pgb, pgk), x1Tb, x1Tk = st
                tsl = slice((ti % 4) * 128, (ti % 4 + 1) * 128)
                sgb, sgk = sgt.next()
                for half in range(2):
                    hs = slice(half * 512, (half + 1) * 512)
                    pbk, pkk = gateb[half]

                    def mmg(e, pbk=pbk, hs=hs):
                        r = None
                        for kc in range(8):
                            r = e.matmul(pbk[:, :], lhsT=x1Tb[:, kc, :], rhs=wpg[:, kc, hs], start=(kc == 0), stop=(kc == 7))
                        return r
                    S.op("pe", mmg, reads=[x1Tk, wpg_k], writes=[pkk])
                    S.op("act", lambda e, pbk=pbk, hs=hs: e.activation(sgb[:, hs], pbk[:, :], AF.Sigmoid), reads=[pkk], writes=[sgk])
                    ppb, ppk = ppb_[half]

                    def mmp(e, ppb=ppb, hs=hs):
                        r = None
                        for kc in range(2):
                            r = e.matmul(ppb[:, :], lhsT=pgb[:, kc, tsl], rhs=wpp[:, kc, hs], start=(kc == 0), stop=(kc == 1))
                        return r
                    S.op("pe", mmp, reads=[pgk, wpp_k], writes=[ppk])
                    S.op("dve", lambda e, ppb=ppb, hs=hs: e.tensor_tensor(sgb[:, hs], sgb[:, hs], ppb[:, :], ALU.mult), reads=[sgk, ppk], writes=[sgk])
                S.op("dve", lambda e: e.scalar_tensor_tensor(out=sgb[:], in0=x1b[:], scalar=ALPHA, in1=sgb[:], op0=ALU.mult, op1=ALU.add), reads=[x1k, sgk], writes=[sgk])
                S.defer("sp", sgk, acc_d[ti * 128:(ti + 1) * 128, :], sgb[:], reads=[sgk])

            st1 = None
            st2 = None
            for ti in range(NT):
                cur = b_front(ti)
                if st2 is not None:
                    b_back2(st2)
                st2 = b_back1(st1) if st1 is not None else None
                st1 = cur
            S.flush()
            if st2 is not None:
                b_back2(st2)
            st2 = b_back1(st1)
            S.flush()
            b_back2(st2)
          S.barrier()

        if "C" in phases:
          with ExitStack() as esCD:
            NSEL = 3
            iT, _ = sb(esCD, "iT", [128, NSEL * TB])
            jT, _ = sb(esCD, "jT", [128, NSEL * TB])
            gT, _ = sb(esCD, "gT", [128, NSEL * TB])
            selk = [(Tok("iT%d" % i), Tok("jT%d" % i), Tok("gT%d" % i)) for i in range(NSEL)]
            s_sbD, s_kD = sb(esCD, "s_sbD", [128, 8, 256])
            sw, sw_k = sb(esCD, "sw", [128, 256])
            top, top_k = sb(esCD, "top", [128, 8, 2, 16])
            idxu, idxu_k = sb(esCD, "idxu", [128, 8, 2, 16], U32)
            idxf, idxf_k = sb(esCD, "idxf", [128, 8, 2, 16])
            cand, cand_k = sb(esCD, "cand", [128, 8, 256])
            c16, c16_k = sb(esCD, "c16", [128, 8, 16])
            posu, posu_k = sb(esCD, "posu", [128, 8, 16], U32)
            abu, abu_k = sb(esCD, "abu", [128, 2, 8, 16], U32)
            abf, abf_k = sb(esCD, "abf", [128, 2, 8, 16])
            eq, eq_k = cand[:].rearrange("p h (a b) -> p h a b", b=16), cand_k
            exs = [sb(esCD, "ex%d" % i, [128, 8, 16]) for i in range(4)]
            zz, zz_k = sb(esCD, "zz", [128, 8])
            ijgs = [sb(esCD, "ijg%d" % i, [128, 3, 128]) for i in range(4)]

            SELPAD = 6

            def select_tile(ti, src_=None, trbank=None, ijg_=None, ex_=None):
                blk_ = ti // (TB // 128)
                sl = blk_ % NSEL
                col = sl * TB + (ti % (TB // 128)) * 128
                ijg, ijg_k = ijg_ if ijg_ is not None else ijgs[0]
                ex, ex_k = ex_ if ex_ is not None else exs[0]
                if src_ is None:
                    s_sb, s_k = s_sbD, s_kD
                    S.dma("sp", s_k, s_sb[:].rearrange("p h k -> p (h k)"), s_d[ti], writes=[s_k])
                    yield
                else:
                    s_sb, s_k = src_
                for h in range(8):
                    for p in range(2):
                        src = s_sb[:, h, p * 128:(p + 1) * 128]
                        S.op("dve", lambda e: e.max(top[:, h, p, 0:8], src), reads=[s_k], writes=[top_k])
                        yield
                        S.op("dve", lambda e: e.max_index(idxu[:, h, p, 0:8], top[:, h, p, 0:8], src), reads=[s_k, top_k], writes=[idxu_k])
                        yield
                        S.op("dve", lambda e: e.match_replace(sw[:, 0:128], top[:, h, p, 0:8], src, -1e30), reads=[s_k, top_k], writes=[sw_k])
                        yield
                        S.op("dve", lambda e: e.max(top[:, h, p, 8:16], sw[:, 0:128]), reads=[sw_k], writes=[top_k])
                        yield
                        S.op("dve", lambda e: e.max_index(idxu[:, h, p, 8:16], top[:, h, p, 8:16], sw[:, 0:128]), reads=[sw_k, top_k], writes=[idxu_k])
                        yield
                S.op("dve", lambda e: e.tensor_copy(idxf[:], idxu[:]), reads=[idxu_k], writes=[idxf_k])
                yield
                S.op("dve", lambda e: e.tensor_tensor(cand[:].rearrange("p h (a b) -> p h a b", b=16), top[:, :, 0, :].unsqueeze(3).to_broadcast([128, 8, 16, 16]),
                                                      top[:, :, 1, :].unsqueeze(2).to_broadcast([128, 8, 16, 16]), ALU.add), reads=[top_k], writes=[cand_k])
                yield
                for h in range(8):
                    src = cand[:, h, :]
                    S.op("dve", lambda e: e.max(c16[:, h, 0:8], src), reads=[cand_k], writes=[c16_k])
                    yield
                    S.op("dve", lambda e: e.max_index(posu[:, h, 0:8], c16[:, h, 0:8], src), reads=[cand_k, c16_k], writes=[posu_k])
                    yield
                    S.op("dve", lambda e: e.match_replace(sw[:], c16[:, h, 0:8], src, -1e30), reads=[cand_k, c16_k], writes=[sw_k])
                    yield
                    S.op("dve", lambda e: e.max(c16[:, h, 8:16], sw[:]), reads=[sw_k], writes=[c16_k])
                    yield
                    S.op("dve", lambda e: e.max_index(posu[:, h, 8:16], c16[:, h, 8:16], sw[:]), reads=[sw_k, c16_k], writes=[posu_k])
                    yield
                S.op("dve", lambda e: e.tensor_tensor(ex[:], c16[:], c16[:, :, 0:1].to_broadcast([128, 8, 16]), ALU.subtract), reads=[c16_k], writes=[ex_k])
                yield
                S.op("dve", lambda e: e.tensor_single_scalar(abu[:, 0, :, :], posu[:], 4, ALU.logical_shift_right), reads=[posu_k], writes=[abu_k])
                yield
                S.op("dve", lambda e: e.tensor_single_scalar(abu[:, 1, :, :], posu[:], 15, ALU.bitwise_and), reads=[posu_k], writes=[abu_k])
                yield
                S.op("dve", lambda e: e.tensor_copy(abf[:], abu[:]), reads=[abu_k], writes=[abf_k])
                yield
                for p in range(2):
                    S.op("dve", lambda e: e.tensor_tensor(eq, abf[:, p, :, :].unsqueeze(3).to_broadcast([128, 8, 16, 16]),
                                                          iot[:, 0:16].unsqueeze(1).unsqueeze(1).to_broadcast([128, 8, 16, 16]), ALU.is_equal), reads=[abf_k, iot_k], writes=[eq_k])
                    yield
                    S.op("dve", lambda e: e.tensor_tensor(eq, eq, idxf[:, :, p, :].unsqueeze(2).to_broadcast([128, 8, 16, 16]), ALU.mult), reads=[eq_k, idxf_k], writes=[eq_k])
                    yield
                    S.op("dve", lambda e: e.tensor_reduce(ijg[:, p, :], eq.rearrange("p h k a -> p (h k) a"), AX.X, ALU.add), reads=[eq_k], writes=[ijg_k])
                    yield
                yield "act"
                for _ in range(SELPAD):
                    yield
                S.op("act", lambda e: e.activation(ex[:], ex[:], AF.Exp), reads=[ex_k], writes=[ex_k])
                for _ in range(SELPAD):
                    yield
                S.op("dve", lambda e: e.tensor_reduce(zz[:], ex[:], AX.X, ALU.add), reads=[ex_k], writes=[zz_k])
                yield
                S.op("dve", lambda e: e.reciprocal(zz[:], zz[:]), reads=[zz_k], writes=[zz_k])
                yield
                S.op("dve", lambda e: e.tensor_tensor(ijg[:, 2, :].rearrange("p (h k) -> p h k", k=16), ex[:], zz[:].unsqueeze(2).to_broadcast([128, 8, 16]), ALU.mult), reads=[ex_k, zz_k], writes=[ijg_k])
                yield
                yield "pe"
                for _ in range(SELPAD):
                    yield
                pb, pk = trbank if trbank is not None else gbk.next()

                def trs(e):
                    r = None
                    for j in range(3):
                        r = e.transpose(pb[:, j * 128:(j + 1) * 128], ijg[:, j, :], ident[:])
                    return r
                S.op("pe", trs, reads=[ijg_k, ident_k], writes=[pk])
                for j, dst in enumerate((iT, jT, gT)):
                    S.op("act", lambda e: e.copy(dst[:, col:col + 128], pb[:, j * 128:(j + 1) * 128]), reads=[pk], writes=[selk[sl][j]])
                yield
                if dbg:
                    for j, dst in enumerate((iT, jT, gT)):
                        S.dma("sp", selk[sl][j], sel_d[j, :, ti * 128:(ti + 1) * 128], dst[:, col:col + 128], reads=[selk[sl][j]])

            def select_block(blk_):
                for tt_ in range(TB // 128):
                    yield from select_tile(blk_ * (TB // 128) + tt_)

            gbk = Rot([banks[6], banks[7]])

            with ExitStack() as esC:
                wq, wq_k = sb(esC, "wq", [128, 8, DM], F32R)
                for kc in range(8):
                    S.dma("sp", wq_k, wq[:, kc, :], wq_d[kc * 128:(kc + 1) * 128, :], writes=[wq_k], append=True)
                bd, bd_k = sb(esC, "bd", [128, 8, 256], F32R)
                S.op("dve", lambda e: e.tensor_scalar(bd[:].rearrange("p h k -> p (h k)"), iot[:, 0:1].to_broadcast([128, 2048]), 0.0, None, op0=ALU.mult), reads=[iot_k], writes=[bd_k])
                S.dma("sp", bd_k, bd[0:64, :, 0:128], skT_d[0:64], writes=[bd_k], append=True)
                S.dma("sp", bd_k, bd[64:128, :, 128:256], skT_d[64:128], writes=[bd_k], append=True)
                xg = Rot([sb(esC, "xq%d" % i, [128, 8, 512], F32R) for i in range(2)])
                qT, qT_k = sb(esC, "qT", [128, 8, 512], F32R)
                ssb = Rot([sb(esC, "s_sb%d" % i, [128, 8, 256]) for i in range(2)])
                ssb_early = [sb(esC, "s_sbe%d" % i, [128, 8, 256]) for i in range(TB // 128)]
                zb = Rot(banks[0:2])
                early_sel = []
                for gi in range(4):
                    xgb, xgk = xg.next()
                    S.dma("sp", xgk, xgb[:], x1T_d[:, gi * 512:(gi + 1) * 512].rearrange("(k p) t -> p k t", p=128), writes=[xgk])
                    for h in range(8):
                        pb, pk = zb.next()

                        def mmq(e, pb=pb, h=h, xgb=xgb):
                            r = None
                            for kc in range(8):
                                r = e.matmul(pb[:, :], lhsT=wq[:, kc, h * 128:(h + 1) * 128], rhs=xgb[:, kc, :], start=(kc == 0), stop=(kc == 7))
                            return r
                        S.op("pe", mmq, reads=[wq_k, xgk], writes=[pk])
                        S.op("act", lambda e, pb=pb, h=h: e.copy(qT[:, h, :], pb[:, :]), reads=[pk], writes=[qT_k])
                    for tj in range(4):
                        ti = gi * 4 + tj
                        tsl = slice(tj * 128, (tj + 1) * 128)
                        s_sb, s_k = ssb_early[ti] if ti < TB // 128 else ssb.next()
                        for hp in range(4):
                            pb, pk = banks[2 + hp]

                            def mms(e, pb=pb, hp=hp, tsl=tsl):
                                r = None
                                for hh in range(2):
                                    h = hp * 2 + hh
                                    r = e.matmul(pb[:, hh * 256:(hh + 1) * 256], lhsT=qT[:, h, tsl], rhs=bd[:, h, :], start=True, stop=True)
                                return r
                            S.op("pe", mms, reads=[qT_k, bd_k], writes=[pk])
                            S.op("act", lambda e, pb=pb, hp=hp: e.copy(s_sb[:, hp * 2:hp * 2 + 2, :], pb[:, :].rearrange("p (a b) -> p a b", b=256)), reads=[pk], writes=[s_k])
                        if ti < TB // 128:
                            g_ = select_tile(ti, src_=(s_sb, s_k), trbank=banks[6 + ti % 2], ijg_=ijgs[ti], ex_=exs[ti])
                            for r_ in g_:
                                if r_ == "act":
                                    break
                            early_sel.append(g_)
                        else:
                            S.dma("sp", s_k, s_d[ti], s_sb[:].rearrange("p h k -> p (h k)"), reads=[s_k])
                for g_ in early_sel:
                    for _ in g_:
                        pass
            S.barrier()
            if "D" in phases:
              with ExitStack() as esD:
                lnb2 = load_lnb(esD, 2)
                Gh = [sb(esD, "G%d" % i, [128, TB, 64], BF16) for i in range(2)]
                iotb, iotb_k = sb(esD, "iotb", [128, 128], BF16)
                S.op("dve", lambda e: e.tensor_copy(iotb[:], iot[:]), reads=[iot_k], writes=[iotb_k])
                GT = 8
                AB = Rot([sb(esD, "AB%d" % i, [128, GT, 192], BF16) + (Tok("ABb%d" % i),) for i in range(4)])
                xb_ = Rot([sb(esD, "xD%d" % i, [128, 8, TB], F32R) for i in range(2)])
                CH = 2
                ub_ = Rot([sb(esD, "ub%d" % i, [128, CH, DM], F32R) for i in range(3)])
                vb_ = Rot([sb(esD, "vb%d" % i, [128, CH, DM], F32R) for i in range(3)])
                gel_ = Rot([sb(esD, "gel%d" % i, [128, TB]) for i in range(2)])
                W_ = Rot([sb(esD, "W%d" % i, [128, TB], F32R) for i in range(2)])
                accb = Rot([sb(esD, "accb%d" % i, [128, DM]) for i in range(1)])
                yb_ = Rot([sb(esD, "yD%d" % i, [128, DM]) for i in range(2)])
                outb = [banks[0], banks[1], banks[2], banks[3]]
                hb = Rot([banks[4], banks[5]])
                NBLK = SEQ // TB

                def g_onehots_a(blk, half, q):
                    ab, ab_k, abb_k = AB.next()
                    sl = blk % NSEL
                    iT_k, jT_k, gT_k = selk[sl]
                    ts_ = slice(sl * TB + q * GT, sl * TB + (q + 1) * GT)
                    S.op("dve", lambda e: e.tensor_tensor(ab[:, :, 0:64], iotb[:, half * 64:(half + 1) * 64].unsqueeze(1).to_broadcast([128, GT, 64]),
                                                          iT[:, ts_].unsqueeze(2).to_broadcast([128, GT, 64]), ALU.is_equal), reads=[iotb_k, iT_k], writes=[ab_k])
                    S.op("pool", lambda e: e.tensor_tensor(ab[:, :, 0:64], ab[:, :, 0:64], gT[:, ts_].unsqueeze(2).to_broadcast([128, GT, 64]), ALU.mult), reads=[gT_k, ab_k], writes=[ab_k])
                    return (ab, ab_k, abb_k, half, q, ts_, jT_k)

                def g_onehots_b(st):
                    ab, ab_k, abb_k, half, q, ts_, jT_k = st
                    S.op("dve", lambda e: e.tensor_tensor(ab[:, :, 64:192], iotb[:].unsqueeze(1).to_broadcast([128, GT, 128]),
                                                          jT[:, ts_].unsqueeze(2).to_broadcast([128, GT, 128]), ALU.is_equal), reads=[iotb_k, jT_k], writes=[abb_k])
                    return st

                def g_matmul(st):
                    ab, ab_k, abb_k, half, q, ts_, jT_k = st
                    pb, pk = gbk.next()

                    def mmG(e):
                        r = None
                        for tl_ in range(GT):
                            r = e.matmul(pb[:, tl_ * 64:(tl_ + 1) * 64], lhsT=ab[:, tl_, 64:192], rhs=ab[:, tl_, 0:64], start=True, stop=True)
                        return r
                    S.op("pe", mmG, reads=[ab_k, abb_k], writes=[pk])
                    g_, gk_ = Gh[half]
                    S.op("act", lambda e: e.copy(g_[:, q * GT:(q + 1) * GT, :], pb[:, 0:GT * 64].rearrange("p (t i) -> p t i", i=64)), reads=[pk], writes=[gk_])

                for q in range(TB // GT):
                    g_matmul(g_onehots_b(g_onehots_a(0, 0, q)))

                def mmV(e, Wt, vb, cc, ci):
                    r = None
                    for tt in range(TB // 128):
                        for half in range(2):
                            r = e.matmul(outb[tt * 2 + half][0][:, :], lhsT=Wt[:, tt * 128:(tt + 1) * 128], rhs=vb[:, cc, half * 512:(half + 1) * 512], start=(ci == 0), stop=(ci == 127))
                    return r

                for blk in range(NBLK):
                    t0 = blk * TB
                    if blk == 0:
                        xnext = xb_.next()
                        S.dma("sp", xnext[1], xnext[0][:], x1T_d[:, 0:TB].rearrange("(k p) t -> p k t", p=128), writes=[xnext[1]])
                    xb, xk = xnext
                    if blk + 1 < NBLK:
                        xnext = xb_.next()
                        S.dma("sp", xnext[1], xnext[0][:], x1T_d[:, t0 + TB:t0 + 2 * TB].rearrange("(k p) t -> p k t", p=128), writes=[xnext[1]])
                    pend = None
                    if blk == 0:
                        def _chain():
                            yield from select_block(1)
                            yield from select_block(2)
                        selgen = _chain()
                        selrate = 6
                    else:
                        selgen = select_block(blk + 2) if blk + 2 < NBLK else None
                        selrate = 3
                    gcur = None
                    gq = []
                    for cg in range(128 // CH):
                        ub, uk = ub_.next()
                        vb, vk = vb_.next()
                        S.dma("sp", uk, ub[:], uT_d[cg * CH:(cg + 1) * CH].rearrange("c p f -> p c f"), writes=[uk])
                        S.dma("sp", vk, vb[:], v_d[cg * CH * 128:(cg + 1) * CH * 128, :].rearrange("(c p) d -> p c d", p=128), writes=[vk])
                        if cg == 2:
                            S.flush()
                        for cc in range(CH):
                            ci = cg * CH + cc
                            pb, pk = hb.next()

                            def mmH(e, pb=pb, cc=cc, ub=ub, xb=xb):
                                r = None
                                for kc in range(8):
                                    r = e.matmul(pb[:, 0:TB], lhsT=ub[:, cc, kc * 128:(kc + 1) * 128], rhs=xb[:, kc, :], start=(kc == 0), stop=(kc == 7))
                                return r
                            S.op("pe", mmH, reads=[uk, xk], writes=[pk])
                            gl, gl_k = gel_.next()
                            S.op("act", lambda e, pb=pb, gl=gl: e.activation(gl[:], pb[:, 0:TB], AF.Gelu), reads=[pk], writes=[gl_k])
                            if pend is not None:
                                pW, pWk, pvb, pvk, pcc, pci = pend
                                S.op("pe", lambda e: mmV(e, pW, pvb, pcc, pci), reads=[pWk, pvk], writes=[outb[i][1] for i in range(4)])
                            Wt, W_k = W_.next()
                            g_, gk_ = Gh[ci // 64]
                            S.op("pool", lambda e, gl=gl, Wt=Wt, ci=ci: e.tensor_tensor(Wt[:], gl[:], g_[:, :, ci % 64], ALU.mult), reads=[gl_k, gk_], writes=[W_k])
                            pend = (Wt, W_k, vb, vk, cc, ci)
                            if selgen is not None:
                                for _ in range(selrate if ci < 127 else 100000):
                                    if next(selgen, "done") == "done":
                                        selgen = None
                                        break
                            if ci % 2 == 0:
                                if len(gq) >= 2:
                                    g_matmul(gq.pop(0))
                                if ci < 64:
                                    gcur = g_onehots_a(blk, 1, ci // 2)
                                elif blk + 1 < NBLK:
                                    gcur = g_onehots_a(blk + 1, 0, (ci - 64) // 2)
                            else:
                                if gcur is not None:
                                    gq.append(g_onehots_b(gcur))
                                    gcur = None
                                if ci == 63 or ci == 127:
                                    while gq:
                                        g_matmul(gq.pop(0))
                    pW, pWk, pvb, pvk, pcc, pci = pend
                    S.op("pe", lambda e: mmV(e, pW, pvb, pcc, pci), reads=[pWk, pvk], writes=[outb[i][1] for i in range(4)])
                    for tt in range(TB // 128):
                        ti = blk * (TB // 128) + tt
                        ab2, ab2_k = accb.next()
                        S.dma("act", ab2_k, ab2[:], acc_d[ti * 128:(ti + 1) * 128, :], writes=[ab2_k])
                        yb, yk = yb_.next()
                        for half in range(2):
                            hs = slice(half * 512, (half + 1) * 512)
                            pb, pk = outb[tt * 2 + half]
                            S.op("dve", lambda e, pb=pb, hs=hs: e.tensor_tensor(yb[:, hs], ab2[:, hs], pb[:, :], ALU.add), reads=[ab2_k, pk], writes=[yk])
                        ob, ok_ = yb, yk
                        layer_norm(yb, yk, ob, ok_, lnb2, lntmp)
                        S.defer("sp", ok_, out_d[ti * 128:(ti + 1) * 128, :], ob[:], reads=[ok_])
                S.barrier()
        S.barrier()
    return nc


def prep_inputs(inp, b):
    f = lambda a: np.ascontiguousarray(a, dtype=np.float32)
    x = inp["x"][b]
    d = {}
    d["xT"] = f(x.T)
    d["x"] = f(x)
    d["pT"] = f(inp["p"][0, b].T)
    w_in = inp["w_in"][0]
    d["wch"] = f(np.stack([w_in[:, c:c + 128].reshape(8, 128, 128).transpose(1, 0, 2).reshape(128, 1024) for c in W_CHUNK_COLS]))
    d["wvh"] = f(np.stack([w_in[:, OFF_V + h * 256: OFF_V + (h + 1) * 256].reshape(8, 128, 256).transpose(1, 0, 2).reshape(128, 2048) for h in range(4)]))
    d["convw"] = f(inp["conv_w"][0].reshape(4, 8, 128).transpose(2, 1, 0))
    pv = np.stack([inp["conv_b"][0], inp["lru_lambda"][0], inp["lru_br"][0].reshape(-1), inp["lru_bi"][0].reshape(-1), inp["gla_norm_g"][0]], axis=-1)
    d["pvec"] = f(pv.reshape(8, 128, 5).transpose(1, 0, 2))
    d["bf"] = f(inp["gla_bf"][0].reshape(4, 128).T)
    d["wr"] = f(inp["lru_wr"][0])
    d["wi"] = f(inp["lru_wi"][0])
    d["wf2"] = f(inp["gla_wf2"][0])
    d["w_out"] = f(inp["w_out"][0])
    d["wq"] = f(inp["peer_wq"][0])
    d["wpg"] = f(inp["ple_gate_w"][0])
    d["wpp"] = f(inp["ple_proj_w"][0])
    d["ln"] = f(np.stack([inp["ln1_g"][0], inp["ln1_b"][0], inp["ln2_g"][0], inp["ln2_b"][0]]))
    d["skT"] = f(inp["peer_subkeys"][0].transpose(1, 3, 0, 2).reshape(128, 8, 128))
    d["uT"] = f(inp["peer_u"][0].reshape(128, 128, 8, 128).transpose(0, 3, 2, 1).reshape(128, 128, 1024))
    d["v"] = f(inp["peer_v"][0])
    return d


def kernel(**inputs):
    inp = {k: np.asarray(v) for k, v in inputs.items()}
    n = 8
    nc = build_nc()
    shared = prep_inputs(inp, 0)
    in_maps = []
    for b in range(n):
        d = dict(shared)
        x = inp["x"][b]
        d["xT"] = np.ascontiguousarray(x.T, dtype=np.float32)
        d["x"] = np.ascontiguousarray(x, dtype=np.float32)
        d["pT"] = np.ascontiguousarray(inp["p"][0, b].T, dtype=np.float32)
        in_maps.append(d)
    res = run_bass_kernel_spmd(nc, in_maps, core_ids=list(range(n)))
    out = np.stack([np.asarray(r["out"]) for r in res.results], axis=0)
    return out.astype(np.float32)
```

```python
import numpy as np
from contextlib import ExitStack
import concourse.bass as bass
import concourse.mybir as mybir
from concourse.bass_utils import run_bass_kernel_spmd

F32 = mybir.dt.float32
F32R = mybir.dt.float32r
BF16 = mybir.dt.bfloat16
U32 = mybir.dt.uint32
AF = mybir.ActivationFunctionType
ALU = mybir.AluOpType
AX = mybir.AxisListType

SEQ = 2048
DM = 1024
NT = SEQ // 128
ALPHA = 2.0 ** 0.25
LN_EPS = 1e-5
RMS_EPS = 1e-6
OFF_X, OFF_Y, OFF_Q, OFF_K, OFF_V, OFF_R, OFF_F, OFF_GA, OFF_GB = 0, 1024, 2048, 2560, 3072, 4096, 5120, 5136, 6160
IN_W = 7184
NE = 16384
TB = 256
W_CHUNK_COLS = ([OFF_X + n * 128 for n in range(8)] + [OFF_Y + n * 128 for n in range(8)] + [OFF_Q + h * 128 for h in range(4)]
                + [OFF_K + h * 128 for h in range(4)] + [OFF_R + n * 128 for n in range(8)] + [OFF_F]
                + [OFF_GA + n * 128 for n in range(8)] + [OFF_GB + n * 128 for n in range(8)])
W_CHUNK_IDX = {c: i for i, c in enumerate(W_CHUNK_COLS)}


class Tok:
    __slots__ = ("name", "w", "r")

    def __init__(self, name=""):
        self.name = name
        self.w = []
        self.r = []


class Sched:
    def __init__(self, nc, es):
        self.nc = nc
        self.es = es
        self.eng = {"pe": nc.tensor, "act": nc.scalar, "dve": nc.vector, "pool": nc.gpsimd, "sp": nc.sync}
        self.sem = {k: es.enter_context(nc.semaphore("s_" + k)) for k in self.eng}
        self.cnt = {k: 0 for k in self.eng}
        self.known = {k: {} for k in self.eng}
        self.dsem = {}
        self.dcnt = {}

    def _wait(self, ek, deps):
        e = self.eng[ek]
        best = {}
        for kind, key, val in deps:
            if kind == "c" and key == ek and ek == "pe":
                continue
            k2 = (kind, key)
            if best.get(k2, 0) < val:
                best[k2] = val
        for (kind, key), val in best.items():
            if self.known[ek].get((kind, key), 0) >= val:
                continue
            e.wait_ge(self.sem[key] if kind == "c" else self.dsem[key], val)
            self.known[ek][(kind, key)] = val

    @staticmethod
    def _deps(reads, writes):
        deps = []
        for b in reads:
            deps += b.w
        for b in writes:
            deps += b.w
            deps += b.r
        return deps

    def _guard(self, writes):
        pend = getattr(self, "_deferred", None)
        if pend:
            ids = set(id(b) for b in writes)
            for a, kw in pend:
                if any(id(b) in ids for b in kw.get("reads", ())):
                    self.flush()
                    return

    def op(self, ek, fn, reads=(), writes=()):
        self._guard(writes)
        self._wait(ek, self._deps(reads, writes))
        inst = fn(self.eng[ek])
        self.cnt[ek] += 1
        inst.then_inc(self.sem[ek], 1)
        me = ("c", ek, self.cnt[ek])
        for b in reads:
            b.r.append(me)
        for b in writes:
            b.w = [me]
            b.r = []
        return me

    def dma(self, qk, tok, out, in_, reads=(), writes=(), append=False):
        name = tok.name
        if name not in self.dsem:
            self.dsem[name] = self.es.enter_context(self.nc.semaphore("d_" + name))
            self.dcnt[name] = 0
        self._guard(writes)
        self._wait(qk, self._deps(reads, writes))
        inst = self.eng[qk].dma_start(out=out, in_=in_)
        self.dcnt[name] += 16
        inst.then_inc(self.dsem[name], 16)
        me = ("d", name, self.dcnt[name])
        for b in reads:
            b.r.append(me)
        for b in writes:
            b.w = (b.w + [me]) if append else [me]
            b.r = []
        return me

    def defer(self, *a, **kw):
        if not hasattr(self, "_deferred"):
            self._deferred = []
        self._deferred.append((a, kw))

    def flush(self):
        for a, kw in getattr(self, "_deferred", []):
            self.dma(*a, **kw)
        self._deferred = []

    def barrier(self, toks=()):
        self.flush()
        deps = [("c", k, v) for k, v in self.cnt.items() if v > 0]
        deps += [("d", k, v) for k, v in self.dcnt.items() if v > 0]
        for ek in self.eng:
            self._wait(ek, [d for d in deps if not (d[0] == "c" and d[1] == ek)])


class Rot:
    def __init__(self, items):
        self.items = items
        self.i = 0

    def next(self):
        it = self.items[self.i % len(self.items)]
        self.i += 1
        return it


def build_nc(dbg=False, phases="ABCD"):
    nc = bass.Bass("TRN2", target_bir_lowering=False)
    nc.dge_precook = False

    def dram(name, shape, dtype=F32, kind="ExternalInput"):
        return nc.dram_tensor(name, shape, dtype, kind=kind).ap()

    xT_d = dram("xT", [DM, SEQ], F32R)
    x_d = dram("x", [SEQ, DM])
    pT_d = dram("pT", [256, SEQ], F32R)
    wch_d = dram("wch", [len(W_CHUNK_COLS), 128, DM], F32R)
    wv_d = dram("wvh", [4, 128, 8 * 256], F32R)
    convw_d = dram("convw", [128, 8, 4])
    pvec_d = dram("pvec", [128, 8, 5])
    bf_d = dram("bf", [128, 4])
    wr_d = dram("wr", [8, 128, 128], F32R)
    wi_d = dram("wi", [8, 128, 128], F32R)
    wf2_d = dram("wf2", [16, 512])
    w_out_d = dram("w_out", [DM, DM], F32R)
    wq_d = dram("wq", [DM, DM], F32R)
    wpg_d = dram("wpg", [DM, DM], F32R)
    wpp_d = dram("wpp", [256, DM], F32R)
    ln_d = dram("ln", [4, DM])
    skT_d = dram("skT", [128, 8, 128], F32R)
    uT_d = dram("uT", [128, 128, DM], F32R)
    v_d = dram("v", [NE, DM], F32R)
    okind = "ExternalOutput"
    out_d = dram("out", [SEQ, DM], F32, okind)
    skind = okind if dbg else "Internal"
    mT_d = dram("mT_s", [DM, SEQ], F32R, skind)
    x1_d = dram("x1_s", [SEQ, DM], F32, skind)
    x1T_d = dram("x1T_s", [DM, SEQ], F32R, skind)
    acc_d = dram("acc_s", [SEQ, DM], F32, skind)
    s_d = dram("sc_s", [NT, 128, 8 * 256], F32, skind)
    if dbg:
        sel_d = dram("sel_s", [3, 128, SEQ], F32, okind)

    with ExitStack() as es:
        S = Sched(nc, es)

        def sb(stk, name, shape, dtype=F32):
            return stk.enter_context(nc.sbuf_tensor("sb_" + name, shape, dtype)), Tok(name)

        banks = []
        for i in range(8):
            banks.append((es.enter_context(nc.psum_tensor("ps%d" % i, [128, 512], F32)), Tok("ps%d" % i)))
        es.enter_context(nc.Block())

        iot, iot_k = sb(es, "iot", [128, 128])
        pid, pid_k = sb(es, "pid", [128, 1])
        ident, ident_k = sb(es, "ident", [128, 128])
        triu, triu_k = sb(es, "triu", [128, 128])
        S.op("pool", lambda e: e.iota(iot[:], [[1, 128]], base=0, channel_multiplier=0, allow_small_or_imprecise_dtypes=True), writes=[iot_k])
        S.op("pool", lambda e: e.iota(pid[:], [[0, 1]], base=0, channel_multiplier=1, allow_small_or_imprecise_dtypes=True), writes=[pid_k])
        S.op("dve", lambda e: e.tensor_scalar(ident[:], iot[:], pid[:, 0:1], None, op0=ALU.is_equal), reads=[iot_k, pid_k], writes=[ident_k])
        S.op("dve", lambda e: e.tensor_scalar(triu[:], iot[:], pid[:, 0:1], None, op0=ALU.is_ge), reads=[iot_k, pid_k], writes=[triu_k])
        def load_lnb(stk, gi):
            lnb, lnb_k = sb(stk, "lnb%d" % gi, [128, 2, DM])
            for i in range(2):
                S.dma("sp", lnb_k, lnb[:, i, :], ln_d[gi + i].partition_broadcast(128), writes=[lnb_k], append=True)
            return lnb, lnb_k

        def layer_norm(src, src_k, dst, dst_k, lnb_, tmp):
            lnb, lnb_k = lnb_
            st, st_k = tmp["st"]
            mv, mv_k = tmp["mv"]
            for c in range(2):
                S.op("dve", lambda e, c=c: e.bn_stats(st[:, c, :], src[:, c * 512:(c + 1) * 512]), reads=[src_k], writes=[st_k])
            S.op("dve", lambda e: e.bn_aggr(mv[:, 0:2], st[:].rearrange("p a b -> p (a b)")), reads=[st_k], writes=[mv_k])
            S.op("act", lambda e: e.activation(mv[:, 2:3], mv[:, 1:2], AF.Sqrt, bias=tmp["eps"][0][:, 0:1]), reads=[mv_k, tmp["eps"][1]], writes=[mv_k])
            S.op("dve", lambda e: e.reciprocal(mv[:, 3:4], mv[:, 2:3]), reads=[mv_k], writes=[mv_k])
            S.op("dve", lambda e: e.tensor_scalar(dst[:], src[:], mv[:, 0:1], mv[:, 3:4], op0=ALU.subtract, op1=ALU.mult), reads=[src_k, mv_k], writes=[dst_k])
            S.op("dve", lambda e: e.tensor_tensor(dst[:], dst[:], lnb[:, 0, :], ALU.mult), reads=[dst_k, lnb_k], writes=[dst_k])
            S.op("dve", lambda e: e.tensor_tensor(dst[:], dst[:], lnb[:, 1, :], ALU.add), reads=[dst_k, lnb_k], writes=[dst_k])

        epsln, epsln_k = sb(es, "epsln", [128, 1])
        epsrms, epsrms_k = sb(es, "epsrms", [128, 1])
        S.op("dve", lambda e: e.memset(epsln[:], LN_EPS), writes=[epsln_k])
        S.op("dve", lambda e: e.memset(epsrms[:], RMS_EPS), writes=[epsrms_k])
        lnst = sb(es, "lnst", [128, 2, 6])
        lnmv = sb(es, "lnmv", [128, 4])
        lntmp = {"st": lnst, "mv": lnmv, "eps": (epsln, epsln_k)}

        if "A" in phases:
          with ExitStack() as esA:
            xT, xT_k = sb(esA, "xTs", [128, 8, SEQ], F32R)
            for kc in range(8):
                S.dma("sp" if kc % 2 == 0 else "act", xT_k, xT[:, kc, :], xT_d[kc * 128:(kc + 1) * 128, :], writes=[xT_k], append=True)
            wbufs = Rot([sb(esA, "wb%d" % i, [128, 8, 128], F32R) for i in range(2)])
            cw, cw_k = sb(esA, "cw", [128, 8, 4])
            pv, pv_k = sb(esA, "pv", [128, 8, 5])
            bfs, bfs_k = sb(esA, "bfs", [128, 4])
            nbf, nbf_k = sb(esA, "nbf", [128, 4])
            nsp, nsp_k = sb(esA, "nsp", [128, 8])
            wr, wr_k = sb(esA, "wrs", [128, 8, 128], F32R)
            wi, wi_k = sb(esA, "wis", [128, 8, 128], F32R)
            wf2, wf2_k = sb(esA, "wf2s", [16, 512])
            S.dma("sp", cw_k, cw[:], convw_d, writes=[cw_k])
            S.dma("sp", pv_k, pv[:], pvec_d, writes=[pv_k])
            S.dma("sp", bfs_k, bfs[:], bf_d, writes=[bfs_k])
            S.dma("sp", wr_k, wr[:], wr_d.rearrange("n c d -> c n d"), writes=[wr_k])
            S.dma("sp", wi_k, wi[:], wi_d.rearrange("n c d -> c n d"), writes=[wi_k])
            S.dma("sp", wf2_k, wf2[:], wf2_d, writes=[wf2_k])
            S.op("act", lambda e: e.activation(nsp[:], pv[:, :, 1], AF.Exp, scale=-1.0), reads=[pv_k], writes=[nsp_k])
            S.op("act", lambda e: e.activation(nsp[:], nsp[:], AF.Ln, bias=1.0), reads=[nsp_k], writes=[nsp_k])
            S.op("dve", lambda e: e.tensor_scalar(nsp[:], nsp[:], -8.0, None, op0=ALU.mult), reads=[nsp_k], writes=[nsp_k])
            S.op("dve", lambda e: e.tensor_scalar(nbf[:], bfs[:], -1.0, None, op0=ALU.mult), reads=[bfs_k], writes=[nbf_k])
            tl = [sb(esA, "tl%d" % i, [128, SEQ + 4]) for i in range(5)]
            zbank = Rot(banks[0:4])

            def zsection(col, evac, flush=True):
                wb, wk = wbufs.next()
                S.dma("sp", wk, wb[:].rearrange("p k c -> p (k c)"), wch_d[W_CHUNK_IDX[col]], writes=[wk])
                if flush:
                    S.flush()
                for tt in range(4):
                    pb, pk = zbank.next()

                    def mm(e, pb=pb, tt=tt, wb=wb):
                        r = None
                        for kc in range(8):
                            r = e.matmul(pb[:, :], lhsT=wb[:, kc, :], rhs=xT[:, kc, tt * 512:(tt + 1) * 512], start=(kc == 0), stop=(kc == 7))
                        return r
                    S.op("pe", mm, reads=[wk, xT_k], writes=[pk])
                    evac(tt, pb, pk)

            def act_evac(dst, dst_k, func, off=0, **kw):
                def f(tt, pb, pk):
                    S.op("act", lambda e: e.activation(dst[:, off + tt * 512: off + (tt + 1) * 512], pb[:, :], func, **kw), reads=[pk], writes=[dst_k])
                return f


            with ExitStack() as esG:
                flT, flT_k = sb(esG, "flT", [16, SEQ])
                ones1, ones1_k = sb(esG, "ones1", [128, 128])
                S.op("pool", lambda e: e.memset(ones1[:], 1.0), writes=[ones1_k])
                v_sb, v_k = sb(esG, "v_sb", [128, NT, 256], F32R)
                khat, khat_k = sb(esG, "khat", [128, NT, 128], F32R)
                oT, oT_k = sb(esG, "oT", [128, 2, SEQ])
                S_sb, S_k = sb(esG, "S_sb", [128, 256], F32R)
                attn = Rot([sb(esG, "attn%d" % i, [128, 128], F32R) for i in range(2)])
                on_ = Rot([sb(esG, "on%d" % i, [128, 256]) for i in range(2)])
                ss_ = Rot([sb(esG, "ss%d" % i, [128, 2]) for i in range(2)])
                junk, junk_k = sb(esG, "junk", [128, 256])
                wv = Rot([sb(esG, "wv%d" % i, [128, 8, 256], F32R) for i in range(1)])
                qt, qt_k = sb(esG, "qtR", [128, SEQ], F32R)
                kt, kt_k = sb(esG, "ktR", [128, SEQ], F32R)
                (lB, lB_k), (Eb, Eb_k), (Ei, Ei_k) = tl[0:3]
                (kh, kh_k), (sr, sr_k), (sg, sg_k) = tl[0], tl[3], tl[4]
                pk_tr, pk_dS, pk_at, pk_ms = banks[2], banks[3], banks[4], banks[7]
                obank = Rot([banks[5], banks[6]])
                zsection(OFF_F, lambda tt, pb, pk: S.op("act", lambda e: e.copy(flT[:, tt * 512:(tt + 1) * 512], pb[0:16, :]), reads=[pk], writes=[flT_k]))
                for h in range(4):
                    for tt in range(4):
                        pb, pk = zbank.next()
                        S.op("pe", lambda e, pb=pb, tt=tt: e.matmul(pb[:, :], lhsT=wf2[:, h * 128:(h + 1) * 128], rhs=flT[:, tt * 512:(tt + 1) * 512], start=True, stop=True),
                             reads=[wf2_k, flT_k], writes=[pk])
                        S.op("act", lambda e, pb=pb, tt=tt: e.activation(lB[:, tt * 512:(tt + 1) * 512], pb[:, :], AF.Exp, scale=-1.0, bias=nbf[:, h:h + 1]), reads=[pk, nbf_k], writes=[lB_k])
                    S.op("act", lambda e: e.activation(lB[:, 0:SEQ], lB[:, 0:SEQ], AF.Ln, bias=1.0), reads=[lB_k], writes=[lB_k])
                    for c in range(NT):
                        S.op("dve", lambda e, c=c: e.tensor_tensor_scan(Ei[:, c * 128:(c + 1) * 128], ones1[:], lB[:, c * 128:(c + 1) * 128], 0.0, ALU.mult, ALU.add), reads=[lB_k, ones1_k], writes=[Ei_k])
                    S.op("act", lambda e: e.activation(Eb[:, 0:SEQ], Ei[:, 0:SEQ], AF.Exp, scale=-1.0 / 16.0), reads=[Ei_k], writes=[Eb_k])
                    S.op("act", lambda e: e.activation(Ei[:, 0:SEQ], Ei[:, 0:SEQ], AF.Exp, scale=1.0 / 16.0), reads=[Ei_k], writes=[Ei_k])
                    zsection(OFF_Q + h * 128, lambda tt, pb, pk: S.op("dve", lambda e: e.scalar_tensor_tensor(out=qt[:, tt * 512:(tt + 1) * 512], in0=pb[:, :], scalar=128.0 ** -0.5, in1=Eb[:, tt * 512:(tt + 1) * 512], op0=ALU.mult, op1=ALU.mult), reads=[pk, Eb_k], writes=[qt_k]))
                    zsection(OFF_K + h * 128, lambda tt, pb, pk: S.op("dve", lambda e: e.tensor_tensor(kt[:, tt * 512:(tt + 1) * 512], pb[:, :], Ei[:, tt * 512:(tt + 1) * 512], ALU.mult), reads=[pk, Ei_k], writes=[kt_k]))
                    S.op("dve", lambda e: e.tensor_tensor(kh[:, 0:SEQ].rearrange("p (c k) -> p c k", k=128), kt[:, 0:SEQ].bitcast(F32).rearrange("p (c k) -> p c k", k=128),
                                                          Eb[:, 127:SEQ:128].unsqueeze(2).to_broadcast([128, NT, 128]), ALU.mult), reads=[kt_k, Eb_k], writes=[kh_k])
                    wvb, wvk = wv.next()
                    S.dma("sp", wvk, wvb[:].rearrange("p k c -> p (k c)"), wv_d[h], writes=[wvk])
                    for ti in range(NT):
                        pb, pk = zbank.next()

                        def mmv(e, pb=pb, ti=ti, wvb=wvb):
                            r = None
                            for kc in range(8):
                                r = e.matmul(pb[:, 0:256], lhsT=xT[:, kc, ti * 128:(ti + 1) * 128], rhs=wvb[:, kc, :], start=(kc == 0), stop=(kc == 7))
                            return r
                        S.op("pe", mmv, reads=[wvk, xT_k], writes=[pk])
                        S.op("act", lambda e, pb=pb, ti=ti: e.copy(v_sb[:, ti, :], pb[:, 0:256]), reads=[pk], writes=[v_k])
                    for g in range(NT // 4):
                        pb, pk = pk_tr

                        def trs(e, g=g, pb=pb):
                            r = None
                            for j in range(4):
                                c = g * 4 + j
                                r = e.transpose(pb[:, j * 128:(j + 1) * 128], kh[:, c * 128:(c + 1) * 128], ident[:])
                            return r
                        S.op("pe", trs, reads=[kh_k, ident_k], writes=[pk])
                        S.op("act", lambda e, g=g, pb=pb: e.copy(khat[:, g * 4:(g + 1) * 4, :], pb[:, :].rearrange("p (j d) -> p j d", d=128)), reads=[pk], writes=[khat_k])
                    pend_tr = None

                    def do_tr(st):
                        on, on_k, cs_ = st

                        def trs(e):
                            r = None
                            for ec in range(2):
                                r = e.transpose(pk_ms[0][:, ec * 128:(ec + 1) * 128], on[:, ec * 128:(ec + 1) * 128], ident[:])
                            return r
                        S.op("pe", trs, reads=[on_k, ident_k], writes=[pk_ms[1]])
                        S.op("act", lambda e: e.copy(oT[:, :, cs_], pk_ms[0][:, 0:256].rearrange("p (a b) -> p a b", b=128)), reads=[pk_ms[1]], writes=[oT_k])

                    pend_rms = None

                    def do_rms(st):
                        ob, ok_, cs_ = st
                        ssb, ss_k = ss_.next()
                        S.op("act", lambda e: e.activation(junk[:], ob[:, 0:256], AF.Square, accum_out=ssb[:, 0:1]), reads=[ok_], writes=[ss_k, junk_k])
                        S.op("act", lambda e: e.activation(ssb[:, 1:2], ssb[:, 0:1], AF.Sqrt, scale=1.0 / 256.0, bias=epsrms[:, 0:1]), reads=[ss_k, epsrms_k], writes=[ss_k])
                        S.op("dve", lambda e: e.reciprocal(ssb[:, 1:2], ssb[:, 1:2]), reads=[ss_k], writes=[ss_k])
                        on, on_k = on_.next()
                        S.op("dve", lambda e: e.tensor_scalar(on[:], ob[:, 0:256], ssb[:, 1:2], None, op0=ALU.mult), reads=[ok_, ss_k], writes=[on_k])
                        return (on, on_k, cs_)

                    for c in range(NT):
                        cs = slice(c * 128, (c + 1) * 128)
                        at_sb, at_k = attn.next()
                        S.op("pe", lambda e: e.matmul(pk_at[0][:, 0:128], lhsT=kt[:, cs], rhs=qt[:, cs], start=True, stop=True), reads=[kt_k, qt_k], writes=[pk_at[1]])
                        S.op("dve", lambda e: e.tensor_tensor(at_sb[:], pk_at[0][:, 0:128], triu[:], ALU.mult), reads=[pk_at[1], triu_k], writes=[at_k])
                        ob, ok_ = obank.next()

                        def mmo(e, ob=ob, c=c, at_sb=at_sb, cs=cs):
                            r = e.matmul(ob[:, 0:256], lhsT=at_sb[:], rhs=v_sb[:, c, :], start=True, stop=(c == 0))
                            if c > 0:
                                r = e.matmul(ob[:, 0:256], lhsT=qt[:, cs], rhs=S_sb[:], start=False, stop=True)
                            return r
                        S.op("pe", mmo, reads=[v_k, at_k, S_k, qt_k], writes=[ok_])
                        if c < NT - 1:
                            S.op("pe", lambda e: e.matmul(pk_dS[0][:, 0:256], lhsT=khat[:, c, :], rhs=v_sb[:, c, :], start=True, stop=True), reads=[khat_k, v_k], writes=[pk_dS[1]])
                            if c == 0:
                                S.op("dve", lambda e: e.tensor_copy(S_sb[:], pk_dS[0][:, 0:256]), reads=[pk_dS[1]], writes=[S_k])
                            else:
                                S.op("dve", lambda e: e.scalar_tensor_tensor(out=S_sb[:], in0=S_sb[:].bitcast(F32), scalar=Eb[:, c * 128 + 127: c * 128 + 128], in1=pk_dS[0][:, 0:256], op0=ALU.mult, op1=ALU.add),
                                     reads=[S_k, Eb_k, pk_dS[1]], writes=[S_k])
                        new_tr = do_rms(pend_rms) if pend_rms is not None else None
                        if pend_tr is not None:
                            do_tr(pend_tr)
                        pend_tr = new_tr
                        pend_rms = (ob, ok_, cs)
                    new_tr = do_rms(pend_rms)
                    if pend_tr is not None:
                        do_tr(pend_tr)
                    do_tr(new_tr)
                    for ec in range(2):
                        n = 2 * h + ec
                        zsection(OFF_R + n * 128, act_evac(sr, sr_k, AF.Silu))
                        zsection(OFF_GB + n * 128, act_evac(sg, sg_k, AF.Sigmoid))
                        S.op("dve", lambda e, ec=ec, n=n: e.scalar_tensor_tensor(out=sr[:, 0:SEQ], in0=oT[:, ec, :], scalar=pv[:, n, 4:5], in1=sr[:, 0:SEQ], op0=ALU.mult, op1=ALU.mult),
                             reads=[oT_k, pv_k, sr_k], writes=[sr_k])
                        S.op("dve", lambda e: e.tensor_tensor(sg[:, 0:SEQ].bitcast(F32R), sr[:, 0:SEQ], sg[:, 0:SEQ], ALU.mult), reads=[sr_k, sg_k], writes=[sg_k])
                        S.defer("sp", sg_k, mT_d[n * 128:(n + 1) * 128, :], sg[:, 0:SEQ].bitcast(F32R), reads=[sg_k])
            S.barrier()
            tl = tl + [sb(esA, "tl%d" % i, [128, SEQ + 4]) for i in range(5, 11)]
            wbufs.items.append(sb(esA, "wb2", [128, 8, 128], F32R))
            zbank.items = [banks[i] for i in (0, 1, 2, 3, 6, 7)]
            (zxp, zxp_k), (u32, u32_k), (t3, t3_k) = tl[0:3]
            ra_ = [tl[3], tl[4]]
            ia_ = [tl[5], tl[6]]
            gy_ = [tl[7], tl[8]]
            sga_ = [tl[9], tl[10]]
            ur_ = [sb(esA, "ur%d" % i, [128, SEQ], F32R) for i in range(2)]
            mb, mb_k = sb(esA, "mb", [128, SEQ], F32R)
            S.op("pool", lambda e: e.memset(zxp[:, 0:3], 0.0), writes=[zxp_k])
            pr, pi = banks[4], banks[5]
            for n in range(8):
                (ra, ra_k), (ia, ia_k), (gy, gy_k), (sga, sga_k), (ur, ur_k) = ra_[n % 2], ia_[n % 2], gy_[n % 2], sga_[n % 2], ur_[n % 2]
                zsection(OFF_X + n * 128, lambda tt, pb, pk: S.op("act", lambda e: e.copy(zxp[:, 3 + tt * 512: 3 + (tt + 1) * 512], pb[:, :]), reads=[pk], writes=[zxp_k]), flush=False)
                zsection(OFF_Y + n * 128, act_evac(gy, gy_k, AF.Gelu), flush=False)
                zsection(OFF_GA + n * 128, act_evac(sga, sga_k, AF.Sigmoid), flush=False)
                S.flush()
                S.dma("sp", mb_k, mb[:], mT_d[n * 128:(n + 1) * 128, :], writes=[mb_k])
                S.op("dve", lambda e: e.tensor_scalar(u32[:, 0:SEQ], zxp[:, 3:3 + SEQ], cw[:, n, 3:4], pv[:, n, 0:1], op0=ALU.mult, op1=ALU.add), reads=[zxp_k, cw_k, pv_k], writes=[u32_k])
                for j in (2, 1):
                    S.op("dve", lambda e, j=j: e.scalar_tensor_tensor(out=u32[:, 0:SEQ], in0=zxp[:, j:j + SEQ], scalar=cw[:, n, j:j + 1], in1=u32[:, 0:SEQ], op0=ALU.mult, op1=ALU.add),
                         reads=[zxp_k, cw_k, u32_k], writes=[u32_k])
                S.op("dve", lambda e: e.scalar_tensor_tensor(out=ur[:, 0:SEQ], in0=zxp[:, 0:SEQ], scalar=cw[:, n, 0:1], in1=u32[:, 0:SEQ], op0=ALU.mult, op1=ALU.add),
                     reads=[zxp_k, cw_k, u32_k], writes=[ur_k])
                for tt in range(4):
                    ts_ = slice(tt * 512, (tt + 1) * 512)
                    S.op("pe", lambda e: e.matmul(pr[0][:, :], lhsT=wr[:, n, :], rhs=ur[:, ts_], start=True, stop=True), reads=[ur_k, wr_k], writes=[pr[1]])
                    S.op("act", lambda e: e.activation(ra[:, ts_], pr[0][:, :], AF.Sigmoid, bias=pv[:, n, 2:3]), reads=[pr[1], pv_k], writes=[ra_k])
                    S.op("pe", lambda e: e.matmul(pi[0][:, :], lhsT=wi[:, n, :], rhs=ur[:, ts_], start=True, stop=True), reads=[ur_k, wi_k], writes=[pi[1]])
                    S.op("act", lambda e: e.activation(ia[:, ts_], pi[0][:, :], AF.Sigmoid, bias=pv[:, n, 3:4]), reads=[pi[1], pv_k], writes=[ia_k])
                S.op("act", lambda e: e.activation(ra[:, 0:SEQ], ra[:, 0:SEQ], AF.Exp, scale=nsp[:, n:n + 1]), reads=[ra_k, nsp_k], writes=[ra_k])
                S.op("act", lambda e: e.activation(t3[:, 0:SEQ], ra[:, 0:SEQ], AF.Square), reads=[ra_k], writes=[t3_k])
                S.op("act", lambda e: e.activation(t3[:, 0:SEQ], t3[:, 0:SEQ], AF.Sqrt, scale=-1.0, bias=1.0), reads=[t3_k], writes=[t3_k])
                S.op("dve", lambda e: e.tensor_tensor(ia[:, 0:SEQ], ia[:, 0:SEQ], ur[:, 0:SEQ].bitcast(F32), ALU.mult), reads=[ia_k, ur_k], writes=[ia_k])
                S.op("dve", lambda e: e.tensor_tensor(ia[:, 0:SEQ], ia[:, 0:SEQ], t3[:, 0:SEQ], ALU.mult), reads=[ia_k, t3_k], writes=[ia_k])
                S.op("dve", lambda e: e.tensor_tensor_scan(t3[:, 0:SEQ], ra[:, 0:SEQ], ia[:, 0:SEQ], 0.0, ALU.mult, ALU.add), reads=[ra_k, ia_k, t3_k], writes=[t3_k])
                S.op("dve", lambda e: e.tensor_tensor(gy[:, 0:SEQ], gy[:, 0:SEQ], sga[:, 0:SEQ], ALU.mult), reads=[gy_k, sga_k], writes=[gy_k])
                S.op("dve", lambda e: e.tensor_tensor(gy[:, 0:SEQ], gy[:, 0:SEQ], t3[:, 0:SEQ], ALU.mult), reads=[gy_k, t3_k], writes=[gy_k])
                S.op("dve", lambda e: e.tensor_tensor(mb[:], gy[:, 0:SEQ], mb[:].bitcast(F32), ALU.add), reads=[gy_k, mb_k], writes=[mb_k])
                S.defer("sp", mb_k, mT_d[n * 128:(n + 1) * 128, :], mb[:], reads=[mb_k])
          S.barrier()

        if "B" in phases:
          with ExitStack() as esB:
            lnb1 = load_lnb(esB, 0)
            wo, wo_k = sb(esB, "wo", [128, 8, DM], F32R)
            for kc in range(8):
                S.dma("sp", wo_k, wo[:, kc, :], w_out_d[kc * 128:(kc + 1) * 128, :], writes=[wo_k], append=True)
            wpg, wpg_k = sb(esB, "wpg", [128, 8, DM], F32R)
            wpp, wpp_k = sb(esB, "wpp", [128, 2, DM], F32R)
            pg = Rot([sb(esB, "pg%d" % i, [128, 2, 512], F32R) for i in range(2)])
            sgt = Rot([sb(esB, "sgt%d" % i, [128, DM]) for i in range(2)])
            pg_cur = [None]
            mg = Rot([sb(esB, "mg%d" % i, [128, 8, 512], F32R) for i in range(2)])
            xt = Rot([sb(esB, "xt%d" % i, [128, DM]) for i in range(2)])
            yt = Rot([sb(esB, "yt%d" % i, [128, DM]) for i in range(2)])
            x1t = Rot([sb(esB, "x1t%d" % i, [128, DM]) for i in range(3)])
            x1Tt = Rot([sb(esB, "x1Tt%d" % i, [128, 8, 128], F32R) for i in range(3)])
            mixb = Rot([(banks[0], banks[1])])
            trb = Rot([(banks[2], banks[3])])
            gateb = (banks[4], banks[5])
            ppb_ = (banks[6], banks[7])
            mg_cur = [None]

            def b_front(ti):
                gi, tj = divmod(ti, 4)
                if tj == 0:
                    mgb, mgk = mg.next()
                    S.dma("sp", mgk, mgb[:], mT_d[:, gi * 512:(gi + 1) * 512].rearrange("(k p) t -> p k t", p=128), writes=[mgk])
                    mg_cur[0] = (mgb, mgk)
                    pgb, pgk = pg.next()
                    S.dma("sp", pgk, pgb[:], pT_d[:, gi * 512:(gi + 1) * 512].rearrange("(k p) t -> p k t", p=128), writes=[pgk])
                    pg_cur[0] = (pgb, pgk)
                    if ti == 0:
                        for kc in range(8):
                            S.dma("sp", wpg_k, wpg[:, kc, :], wpg_d[kc * 128:(kc + 1) * 128, :], writes=[wpg_k], append=True)
                        for kc in range(2):
                            S.dma("sp", wpp_k, wpp[:, kc, :], wpp_d[kc * 128:(kc + 1) * 128, :], writes=[wpp_k], append=True)
                mgb, mgk = mg_cur[0]
                xb, xk = xt.next()
                S.dma("act", xk, xb[:], x_d[ti * 128:(ti + 1) * 128, :], writes=[xk])
                S.flush()
                (b0, b1) = mixb.next()
                for half, (pb, pk) in enumerate((b0, b1)):
                    def mm(e, pb=pb, half=half):
                        r = None
                        for kc in range(8):
                            r = e.matmul(pb[:, :], lhsT=mgb[:, kc, tj * 128:(tj + 1) * 128], rhs=wo[:, kc, half * 512:(half + 1) * 512], start=(kc == 0), stop=(kc == 7))
                        return r
                    S.op("pe", mm, reads=[mgk, wo_k], writes=[pk])
                yb, yk = yt.next()
                for half, (pb, pk) in enumerate((b0, b1)):
                    hs = slice(half * 512, (half + 1) * 512)
                    S.op("dve", lambda e, pb=pb, hs=hs: e.scalar_tensor_tensor(out=yb[:, hs], in0=xb[:, hs], scalar=ALPHA, in1=pb[:, :], op0=ALU.mult, op1=ALU.add), reads=[xk, pk], writes=[yk])
                x1b, x1k = x1t.next()
                layer_norm(yb, yk, x1b, x1k, lnb1, lntmp)
                return (ti, x1b, x1k, pg_cur[0])

            def b_back1(st):
                ti, x1b, x1k, (pgb, pgk) = st
                (t0, t1) = trb.next()
                x1Tb, x1Tk = x1Tt.next()
                for half, (pb, pk) in enumerate((t0, t1)):
                    def trs(e, pb=pb, half=half):
                        r = None
                        for j in range(4):
                            kc = half * 4 + j
                            r = e.transpose(pb[:, j * 128:(j + 1) * 128], x1b[:, kc * 128:(kc + 1) * 128], ident[:])
                        return r
                    S.op("pe", trs, reads=[x1k, ident_k], writes=[pk])
                    S.op("act", lambda e, pb=pb, half=half: e.copy(x1Tb[:, half * 4:(half + 1) * 4, :], pb[:, :].rearrange("p (j d) -> p j d", d=128)), reads=[pk], writes=[x1Tk])
                S.defer("sp", x1Tk, x1T_d[:, ti * 128:(ti + 1) * 128].rearrange("(k p) t -> p k t", p=128), x1Tb[:], reads=[x1Tk])
                return st + (x1Tb, x1Tk)

            def b_back2(st):
                ti, x1b, x1k, (pgb, pgk), x1Tb, x1Tk = st
                tsl = slice((ti % 4) * 128, (ti % 4 + 1) * 128)
                sgb, sgk = sgt.next()
                for half in range(2):
                    hs = slice(half * 512, (half + 1) * 512)
                    pbk, pkk = gateb[half]

                    def mmg(e, pbk=pbk, hs=hs):
                        r = None
                        for kc in range(8):
                            r = e.matmul(pbk[:, :], lhsT=x1Tb[:, kc, :], rhs=wpg[:, kc, hs], start=(kc == 0), stop=(kc == 7))
                        return r
                    S.op("pe", mmg, reads=[x1Tk, wpg_k], writes=[pkk])
                    S.op("act", lambda e, pbk=pbk, hs=hs: e.activation(sgb[:, hs], pbk[:, :], AF.Sigmoid), reads=[pkk], writes=[sgk])
                    ppb, ppk = ppb_[half]

                    def mmp(e, ppb=ppb, hs=hs):
                        r = None
                        for kc in range(2):
                            r = e.matmul(ppb[:, :], lhsT=pgb[:, kc, tsl], rhs=wpp[:, kc, hs], start=(kc == 0), stop=(kc == 1))
                        return r
                    S.op("pe", mmp, reads=[pgk, wpp_k], writes=[ppk])
                    S.op("dve", lambda e, ppb=ppb, hs=hs: e.tensor_tensor(sgb[:, hs], sgb[:, hs], ppb[:, :], ALU.mult), reads=[sgk, ppk], writes=[sgk])
                S.op("dve", lambda e: e.scalar_tensor_tensor(out=sgb[:], in0=x1b[:], scalar=ALPHA, in1=sgb[:], op0=ALU.mult, op1=ALU.add), reads=[x1k, sgk], writes=[sgk])
                S.defer("sp", sgk, acc_d[ti * 128:(ti + 1) * 128, :], sgb[:], reads=[sgk])

            st1 = None
            st2 = None
            for ti in range(NT):
                cur = b_front(ti)
                if st2 is not None:
                    b_back2(st2)
                st2 = b_back1(st1) if st1 is not None else None
                st1 = cur
            S.flush()
            if st2 is not None:
                b_back2(st2)
            st2 = b_back1(st1)
            S.flush()
            b_back2(st2)
          S.barrier()

        if "C" in phases:
          with ExitStack() as esCD:
            NSEL = 3
            iT, _ = sb(esCD, "iT", [128, NSEL * TB])
            jT, _ = sb(esCD, "jT", [128, NSEL * TB])
            gT, _ = sb(esCD, "gT", [128, NSEL * TB])
            selk = [(Tok("iT%d" % i), Tok("jT%d" % i), Tok("gT%d" % i)) for i in range(NSEL)]
            sw, sw_k = sb(esCD, "sw", [128, 256])
            top, top_k = sb(esCD, "top", [128, 8, 2, 16])
            idxu, idxu_k = sb(esCD, "idxu", [128, 8, 2, 16], U32)
            idxf, idxf_k = sb(esCD, "idxf", [128, 8, 2, 16])
            cand, cand_k = sb(esCD, "cand", [128, 8, 256])
            s_sbD, s_kD = cand, cand_k
            c16, c16_k = sb(esCD, "c16", [128, 8, 16])
            posu, posu_k = sb(esCD, "posu", [128, 8, 16], U32)
            abu, abu_k = sb(esCD, "abu", [128, 2, 8, 16], U32)
            abf, abf_k = sb(esCD, "abf", [128, 2, 8, 16])
            eq, eq_k = cand[:].rearrange("p h (a b) -> p h a b", b=16), cand_k
            exs = [sb(esCD, "ex%d" % i, [128, 8, 16]) for i in range(2)]
            zz, zz_k = sb(esCD, "zz", [128, 8])
            ijgs = [sb(esCD, "ijg%d" % i, [128, 3, 128]) for i in range(2)]

            SELPAD = 6

            def select_tile(ti, src_=None, trbank=None, ijg_=None, ex_=None):
                blk_ = ti // (TB // 128)
                sl = blk_ % NSEL
                col = sl * TB + (ti % (TB // 128)) * 128
                ijg, ijg_k = ijg_ if ijg_ is not None else ijgs[0]
                ex, ex_k = ex_ if ex_ is not None else exs[0]
                if src_ is None:
                    s_sb, s_k = s_sbD, s_kD
                    S.dma("sp", s_k, s_sb[:].rearrange("p h k -> p (h k)"), s_d[ti], writes=[s_k])
                    yield
                else:
                    s_sb, s_k = src_
                for h in range(8):
                    for p in range(2):
                        src = s_sb[:, h, p * 128:(p + 1) * 128]
                        S.op("dve", lambda e: e.max(top[:, h, p, 0:8], src), reads=[s_k], writes=[top_k])
                        yield
                        S.op("dve", lambda e: e.max_index(idxu[:, h, p, 0:8], top[:, h, p, 0:8], src), reads=[s_k, top_k], writes=[idxu_k])
                        yield
                        S.op("dve", lambda e: e.match_replace(sw[:, 0:128], top[:, h, p, 0:8], src, -1e30), reads=[s_k, top_k], writes=[sw_k])
                        yield
                        S.op("dve", lambda e: e.max(top[:, h, p, 8:16], sw[:, 0:128]), reads=[sw_k], writes=[top_k])
                        yield
                        S.op("dve", lambda e: e.max_index(idxu[:, h, p, 8:16], top[:, h, p, 8:16], sw[:, 0:128]), reads=[sw_k, top_k], writes=[idxu_k])
                        yield
                S.op("dve", lambda e: e.tensor_copy(idxf[:], idxu[:]), reads=[idxu_k], writes=[idxf_k])
                yield
                S.op("dve", lambda e: e.tensor_tensor(cand[:].rearrange("p h (a b) -> p h a b", b=16), top[:, :, 0, :].unsqueeze(3).to_broadcast([128, 8, 16, 16]),
                                                      top[:, :, 1, :].unsqueeze(2).to_broadcast([128, 8, 16, 16]), ALU.add), reads=[top_k], writes=[cand_k])
                yield
                for h in range(8):
                    src = cand[:, h, :]
                    S.op("dve", lambda e: e.max(c16[:, h, 0:8], src), reads=[cand_k], writes=[c16_k])
                    yield
                    S.op("dve", lambda e: e.max_index(posu[:, h, 0:8], c16[:, h, 0:8], src), reads=[cand_k, c16_k], writes=[posu_k])
                    yield
                    S.op("dve", lambda e: e.match_replace(sw[:], c16[:, h, 0:8], src, -1e30), reads=[cand_k, c16_k], writes=[sw_k])
                    yield
                    S.op("dve", lambda e: e.max(c16[:, h, 8:16], sw[:]), reads=[sw_k], writes=[c16_k])
                    yield
                    S.op("dve", lambda e: e.max_index(posu[:, h, 8:16], c16[:, h, 8:16], sw[:]), reads=[sw_k, c16_k], writes=[posu_k])
                    yield
                S.op("dve", lambda e: e.tensor_tensor(ex[:], c16[:], c16[:, :, 0:1].to_broadcast([128, 8, 16]), ALU.subtract), reads=[c16_k], writes=[ex_k])
                yield
                S.op("dve", lambda e: e.tensor_single_scalar(abu[:, 0, :, :], posu[:], 4, ALU.logical_shift_right), reads=[posu_k], writes=[abu_k])
                yield
                S.op("dve", lambda e: e.tensor_single_scalar(abu[:, 1, :, :], posu[:], 15, ALU.bitwise_and), reads=[posu_k], writes=[abu_k])
                yield
                S.op("dve", lambda e: e.tensor_copy(abf[:], abu[:]), reads=[abu_k], writes=[abf_k])
                yield
                for p in range(2):
                    S.op("dve", lambda e: e.tensor_tensor(eq, abf[:, p, :, :].unsqueeze(3).to_broadcast([128, 8, 16, 16]),
                                                          iot[:, 0:16].unsqueeze(1).unsqueeze(1).to_broadcast([128, 8, 16, 16]), ALU.is_equal), reads=[abf_k, iot_k], writes=[eq_k])
                    yield
                    S.op("dve", lambda e: e.tensor_tensor(eq, eq, idxf[:, :, p, :].unsqueeze(2).to_broadcast([128, 8, 16, 16]), ALU.mult), reads=[eq_k, idxf_k], writes=[eq_k])
                    yield
                    S.op("dve", lambda e: e.tensor_reduce(ijg[:, p, :], eq.rearrange("p h k a -> p (h k) a"), AX.X, ALU.add), reads=[eq_k], writes=[ijg_k])
                    yield
                yield "act"
                for _ in range(SELPAD):
                    yield
                S.op("act", lambda e: e.activation(ex[:], ex[:], AF.Exp), reads=[ex_k], writes=[ex_k])
                for _ in range(SELPAD):
                    yield
                S.op("dve", lambda e: e.tensor_reduce(zz[:], ex[:], AX.X, ALU.add), reads=[ex_k], writes=[zz_k])
                yield
                S.op("dve", lambda e: e.reciprocal(zz[:], zz[:]), reads=[zz_k], writes=[zz_k])
                yield
                S.op("dve", lambda e: e.tensor_tensor(ijg[:, 2, :].rearrange("p (h k) -> p h k", k=16), ex[:], zz[:].unsqueeze(2).to_broadcast([128, 8, 16]), ALU.mult), reads=[ex_k, zz_k], writes=[ijg_k])
                yield
                yield "pe"
                for _ in range(SELPAD):
                    yield
                pb, pk = trbank if trbank is not None else gbk.next()

                def trs(e):
                    r = None
                    for j in range(3):
                        r = e.transpose(pb[:, j * 128:(j + 1) * 128], ijg[:, j, :], ident[:])
                    return r
                S.op("pe", trs, reads=[ijg_k, ident_k], writes=[pk])
                for j, dst in enumerate((iT, jT, gT)):
                    S.op("act", lambda e: e.copy(dst[:, col:col + 128], pb[:, j * 128:(j + 1) * 128]), reads=[pk], writes=[selk[sl][j]])
                yield
                if dbg:
                    for j, dst in enumerate((iT, jT, gT)):
                        S.dma("sp", selk[sl][j], sel_d[j, :, ti * 128:(ti + 1) * 128], dst[:, col:col + 128], reads=[selk[sl][j]])

            def select_block(blk_):
                for tt_ in range(TB // 128):
                    yield from select_tile(blk_ * (TB // 128) + tt_)

            gbk = Rot([banks[6], banks[7]])

            with ExitStack() as esC:
                wq, wq_k = sb(esC, "wq", [128, 8, DM], F32R)
                for kc in range(8):
                    S.dma("sp", wq_k, wq[:, kc, :], wq_d[kc * 128:(kc + 1) * 128, :], writes=[wq_k], append=True)
                bd, bd_k = sb(esC, "bd", [128, 8, 256], F32R)
                S.op("dve", lambda e: e.tensor_scalar(bd[:].rearrange("p h k -> p (h k)"), iot[:, 0:1].to_broadcast([128, 2048]), 0.0, None, op0=ALU.mult), reads=[iot_k], writes=[bd_k])
                S.dma("sp", bd_k, bd[0:64, :, 0:128], skT_d[0:64], writes=[bd_k], append=True)
                S.dma("sp", bd_k, bd[64:128, :, 128:256], skT_d[64:128], writes=[bd_k], append=True)
                xg = Rot([sb(esC, "xq%d" % i, [128, 8, 512], F32R) for i in range(2)])
                qT, qT_k = sb(esC, "qT", [128, 8, 512], F32R)
                ssb = Rot([sb(esC, "s_sb%d" % i, [128, 8, 256]) for i in range(2)])
                ssb_early = [sb(esC, "s_sbe%d" % i, [128, 8, 256]) for i in range(TB // 128)]
                zb = Rot(banks[0:2])
                early_sel = []
                for gi in range(4):
                    xgb, xgk = xg.next()
                    S.dma("sp", xgk, xgb[:], x1T_d[:, gi * 512:(gi + 1) * 512].rearrange("(k p) t -> p k t", p=128), writes=[xgk])
                    for h in range(8):
                        pb, pk = zb.next()

                        def mmq(e, pb=pb, h=h, xgb=xgb):
                            r = None
                            for kc in range(8):
                                r = e.matmul(pb[:, :], lhsT=wq[:, kc, h * 128:(h + 1) * 128], rhs=xgb[:, kc, :], start=(kc == 0), stop=(kc == 7))
                            return r
                        S.op("pe", mmq, reads=[wq_k, xgk], writes=[pk])
                        S.op("act", lambda e, pb=pb, h=h: e.copy(qT[:, h, :], pb[:, :]), reads=[pk], writes=[qT_k])
                    for tj in range(4):
                        ti = gi * 4 + tj
                        tsl = slice(tj * 128, (tj + 1) * 128)
                        s_sb, s_k = ssb_early[ti] if ti < TB // 128 else ssb.next()
                        for hp in range(4):
                            pb, pk = banks[2 + hp]

                            def mms(e, pb=pb, hp=hp, tsl=tsl):
                                r = None
                                for hh in range(2):
                                    h = hp * 2 + hh
                                    r = e.matmul(pb[:, hh * 256:(hh + 1) * 256], lhsT=qT[:, h, tsl], rhs=bd[:, h, :], start=True, stop=True)
                                return r
                            S.op("pe", mms, reads=[qT_k, bd_k], writes=[pk])
                            S.op("act", lambda e, pb=pb, hp=hp: e.copy(s_sb[:, hp * 2:hp * 2 + 2, :], pb[:, :].rearrange("p (a b) -> p a b", b=256)), reads=[pk], writes=[s_k])
                        if ti < TB // 128:
                            g_ = select_tile(ti, src_=(s_sb, s_k), trbank=banks[6 + ti % 2], ijg_=ijgs[ti], ex_=exs[ti])
                            for r_ in g_:
                                if r_ == "act":
                                    break
                            early_sel.append(g_)
                        else:
                            S.dma("sp", s_k, s_d[ti], s_sb[:].rearrange("p h k -> p (h k)"), reads=[s_k])
                for g_ in early_sel:
                    for _ in g_:
                        pass
            S.barrier()
            if "D" in phases:
              with ExitStack() as esD:
                lnb2 = load_lnb(esD, 2)
                Gh = [sb(esD, "G%d" % i, [128, TB, 64], BF16) for i in range(2)]
                iotb, iotb_k = sb(esD, "iotb", [128, 128], BF16)
                S.op("dve", lambda e: e.tensor_copy(iotb[:], iot[:]), reads=[iot_k], writes=[iotb_k])
                GT = 8
                AB = Rot([sb(esD, "AB%d" % i, [128, GT, 192], BF16) + (Tok("ABb%d" % i),) for i in range(3)])
                xb_ = Rot([sb(esD, "xD%d" % i, [128, 8, TB], F32R) for i in range(2)])
                CH = 2
                ub_ = Rot([sb(esD, "ub%d" % i, [128, CH, DM], F32R) for i in range(4)])
                vb_ = Rot([sb(esD, "vb%d" % i, [128, CH, DM], F32R) for i in range(4)])
                gel_ = Rot([sb(esD, "gel%d" % i, [128, TB]) for i in range(2)])
                W_ = Rot([sb(esD, "W%d" % i, [128, TB], F32R) for i in range(2)])
                accb = Rot([sb(esD, "accb%d" % i, [128, DM]) for i in range(1)])
                yb_ = Rot([sb(esD, "yD%d" % i, [128, DM]) for i in range(2)])
                outb = [banks[0], banks[1], banks[2], banks[3]]
                hb = Rot([banks[4], banks[5]])
                NBLK = SEQ // TB

                def g_onehots_a(blk, half, q):
                    ab, ab_k, abb_k = AB.next()
                    sl = blk % NSEL
                    iT_k, jT_k, gT_k = selk[sl]
                    ts_ = slice(sl * TB + q * GT, sl * TB + (q + 1) * GT)
                    S.op("dve", lambda e: e.tensor_tensor(ab[:, :, 0:64], iotb[:, half * 64:(half + 1) * 64].unsqueeze(1).to_broadcast([128, GT, 64]),
                                                          iT[:, ts_].unsqueeze(2).to_broadcast([128, GT, 64]), ALU.is_equal), reads=[iotb_k, iT_k], writes=[ab_k])
                    S.op("pool", lambda e: e.tensor_tensor(ab[:, :, 0:64], ab[:, :, 0:64], gT[:, ts_].unsqueeze(2).to_broadcast([128, GT, 64]), ALU.mult), reads=[gT_k, ab_k], writes=[ab_k])
                    return (ab, ab_k, abb_k, half, q, ts_, jT_k)

                def g_onehots_b(st):
                    ab, ab_k, abb_k, half, q, ts_, jT_k = st
                    S.op("dve", lambda e: e.tensor_tensor(ab[:, :, 64:192], iotb[:].unsqueeze(1).to_broadcast([128, GT, 128]),
                                                          jT[:, ts_].unsqueeze(2).to_broadcast([128, GT, 128]), ALU.is_equal), reads=[iotb_k, jT_k], writes=[abb_k])
                    return st

                def g_matmul(st):
                    ab, ab_k, abb_k, half, q, ts_, jT_k = st
                    pb, pk = gbk.next()

                    def mmG(e):
                        r = None
                        for tl_ in range(GT):
                            r = e.matmul(pb[:, tl_ * 64:(tl_ + 1) * 64], lhsT=ab[:, tl_, 64:192], rhs=ab[:, tl_, 0:64], start=True, stop=True)
                        return r
                    S.op("pe", mmG, reads=[ab_k, abb_k], writes=[pk])
                    g_, gk_ = Gh[half]
                    S.op("act", lambda e: e.copy(g_[:, q * GT:(q + 1) * GT, :], pb[:, 0:GT * 64].rearrange("p (t i) -> p t i", i=64)), reads=[pk], writes=[gk_])

                for q in range(TB // GT):
                    g_matmul(g_onehots_b(g_onehots_a(0, 0, q)))

                def mmV(e, Wt, vb, cc, ci):
                    r = None
                    for tt in range(TB // 128):
                        for half in range(2):
                            r = e.matmul(outb[tt * 2 + half][0][:, :], lhsT=Wt[:, tt * 128:(tt + 1) * 128], rhs=vb[:, cc, half * 512:(half + 1) * 512], start=(ci == 0), stop=(ci == 127))
                    return r

                for blk in range(NBLK):
                    t0 = blk * TB
                    if blk == 0:
                        xnext = xb_.next()
                        S.dma("sp", xnext[1], xnext[0][:], x1T_d[:, 0:TB].rearrange("(k p) t -> p k t", p=128), writes=[xnext[1]])
                    xb, xk = xnext
                    if blk + 1 < NBLK:
                        xnext = xb_.next()
                        S.dma("sp", xnext[1], xnext[0][:], x1T_d[:, t0 + TB:t0 + 2 * TB].rearrange("(k p) t -> p k t", p=128), writes=[xnext[1]])
                    pend = None
                    if blk == 0:
                        def _chain():
                            yield from select_block(1)
                            yield from select_block(2)
                        selgen = _chain()
                        selrate = 6
                    else:
                        selgen = select_block(blk + 2) if blk + 2 < NBLK else None
                        selrate = 3
                    gcur = None
                    gq = []
                    for cg in range(128 // CH):
                        ub, uk = ub_.next()
                        vb, vk = vb_.next()
                        S.dma("sp", uk, ub[:], uT_d[cg * CH:(cg + 1) * CH].rearrange("c p f -> p c f"), writes=[uk])
                        S.dma("sp", vk, vb[:], v_d[cg * CH * 128:(cg + 1) * CH * 128, :].rearrange("(c p) d -> p c d", p=128), writes=[vk])
                        if cg == 2:
                            S.flush()
                        for cc in range(CH):
                            ci = cg * CH + cc
                            pb, pk = hb.next()

                            def mmH(e, pb=pb, cc=cc, ub=ub, xb=xb):
                                r = None
                                for kc in range(8):
                                    r = e.matmul(pb[:, 0:TB], lhsT=ub[:, cc, kc * 128:(kc + 1) * 128], rhs=xb[:, kc, :], start=(kc == 0), stop=(kc == 7))
                                return r
                            S.op("pe", mmH, reads=[uk, xk], writes=[pk])
                            gl, gl_k = gel_.next()
                            S.op("act", lambda e, pb=pb, gl=gl: e.activation(gl[:], pb[:, 0:TB], AF.Gelu), reads=[pk], writes=[gl_k])
                            if pend is not None:
                                pW, pWk, pvb, pvk, pcc, pci = pend
                                S.op("pe", lambda e: mmV(e, pW, pvb, pcc, pci), reads=[pWk, pvk], writes=[outb[i][1] for i in range(4)])
                            Wt, W_k = W_.next()
                            g_, gk_ = Gh[ci // 64]
                            S.op("pool", lambda e, gl=gl, Wt=Wt, ci=ci: e.tensor_tensor(Wt[:], gl[:], g_[:, :, ci % 64], ALU.mult), reads=[gl_k, gk_], writes=[W_k])
                            pend = (Wt, W_k, vb, vk, cc, ci)
                            if selgen is not None:
                                for _ in range(selrate if ci < 127 else 100000):
                                    if next(selgen, "done") == "done":
                                        selgen = None
                                        break
                            if ci % 2 == 0:
                                if len(gq) >= 2:
                                    g_matmul(gq.pop(0))
                                if ci < 64:
                                    gcur = g_onehots_a(blk, 1, ci // 2)
                                elif blk + 1 < NBLK:
                                    gcur = g_onehots_a(blk + 1, 0, (ci - 64) // 2)
                            else:
                                if gcur is not None:
                                    gq.append(g_onehots_b(gcur))
                                    gcur = None
                                if ci == 63 or ci == 127:
                                    while gq:
                                        g_matmul(gq.pop(0))
                    pW, pWk, pvb, pvk, pcc, pci = pend
                    S.op("pe", lambda e: mmV(e, pW, pvb, pcc, pci), reads=[pWk, pvk], writes=[outb[i][1] for i in range(4)])
                    for tt in range(TB // 128):
                        ti = blk * (TB // 128) + tt
                        ab2, ab2_k = accb.next()
                        S.dma("act", ab2_k, ab2[:], acc_d[ti * 128:(ti + 1) * 128, :], writes=[ab2_k])
                        yb, yk = yb_.next()
                        for half in range(2):
                            hs = slice(half * 512, (half + 1) * 512)
                            pb, pk = outb[tt * 2 + half]
                            S.op("dve", lambda e, pb=pb, hs=hs: e.tensor_tensor(yb[:, hs], ab2[:, hs], pb[:, :], ALU.add), reads=[ab2_k, pk], writes=[yk])
                        ob, ok_ = yb, yk
                        layer_norm(yb, yk, ob, ok_, lnb2, lntmp)
                        S.defer("sp", ok_, out_d[ti * 128:(ti + 1) * 128, :], ob[:], reads=[ok_])
                S.barrier()
        S.barrier()
    return nc


def prep_inputs(inp, b):
    f = lambda a: np.ascontiguousarray(a, dtype=np.float32)
    x = inp["x"][b]
    d = {}
    d["xT"] = f(x.T)
    d["x"] = f(x)
    d["pT"] = f(inp["p"][0, b].T)
    w_in = inp["w_in"][0]
    d["wch"] = f(np.stack([w_in[:, c:c + 128].reshape(8, 128, 128).transpose(1, 0, 2).reshape(128, 1024) for c in W_CHUNK_COLS]))
    d["wvh"] = f(np.stack([w_in[:, OFF_V + h * 256: OFF_V + (h + 1) * 256].reshape(8, 128, 256).transpose(1, 0, 2).reshape(128, 2048) for h in range(4)]))
    d["convw"] = f(inp["conv_w"][0].reshape(4, 8, 128).transpose(2, 1, 0))
    pv = np.stack([inp["conv_b"][0], inp["lru_lambda"][0], inp["lru_br"][0].reshape(-1), inp["lru_bi"][0].reshape(-1), inp["gla_norm_g"][0]], axis=-1)
    d["pvec"] = f(pv.reshape(8, 128, 5).transpose(1, 0, 2))
    d["bf"] = f(inp["gla_bf"][0].reshape(4, 128).T)
    d["wr"] = f(inp["lru_wr"][0])
    d["wi"] = f(inp["lru_wi"][0])
    d["wf2"] = f(inp["gla_wf2"][0])
    d["w_out"] = f(inp["w_out"][0])
    d["wq"] = f(inp["peer_wq"][0])
    d["wpg"] = f(inp["ple_gate_w"][0])
    d["wpp"] = f(inp["ple_proj_w"][0])
    d["ln"] = f(np.stack([inp["ln1_g"][0], inp["ln1_b"][0], inp["ln2_g"][0], inp["ln2_b"][0]]))
    d["skT"] = f(inp["peer_subkeys"][0].transpose(1, 3, 0, 2).reshape(128, 8, 128))
    d["uT"] = f(inp["peer_u"][0].reshape(128, 128, 8, 128).transpose(0, 3, 2, 1).reshape(128, 128, 1024))
    d["v"] = f(inp["peer_v"][0])
    return d


def kernel(**inputs):
    inp = {k: np.asarray(v) for k, v in inputs.items()}
    n = 8
    nc = build_nc()
    shared = prep_inputs(inp, 0)
    in_maps = []
    for b in range(n):
        d = dict(shared)
        x = inp["x"][b]
        d["xT"] = np.ascontiguousarray(x.T, dtype=np.float32)
        d["x"] = np.ascontiguousarray(x, dtype=np.float32)
        d["pT"] = np.ascontiguousarray(inp["p"][0, b].T, dtype=np.float32)
        in_maps.append(d)
    res = run_bass_kernel_spmd(nc, in_maps, core_ids=list(range(n)))
    out = np.stack([np.asarray(r["out"]) for r in res.results], axis=0)
    return out.astype(np.float32)
```

```python
import numpy as np
from contextlib import ExitStack
import concourse.bass as bass
import concourse.mybir as mybir
from concourse.bass_utils import run_bass_kernel_spmd

F32 = mybir.dt.float32
F32R = mybir.dt.float32r
BF16 = mybir.dt.bfloat16
U32 = mybir.dt.uint32
AF = mybir.ActivationFunctionType
ALU = mybir.AluOpType
AX = mybir.AxisListType

SEQ = 2048
DM = 1024
NT = SEQ // 128
ALPHA = 2.0 ** 0.25
LN_EPS = 1e-5
RMS_EPS = 1e-6
OFF_X, OFF_Y, OFF_Q, OFF_K, OFF_V, OFF_R, OFF_F, OFF_GA, OFF_GB = 0, 1024, 2048, 2560, 3072, 4096, 5120, 5136, 6160
IN_W = 7184
NE = 16384
TB = 256
W_CHUNK_COLS = ([OFF_X + n * 128 for n in range(8)] + [OFF_Y + n * 128 for n in range(8)] + [OFF_Q + h * 128 for h in range(4)]
                + [OFF_K + h * 128 for h in range(4)] + [OFF_R + n * 128 for n in range(8)] + [OFF_F]
                + [OFF_GA + n * 128 for n in range(8)] + [OFF_GB + n * 128 for n in range(8)])
W_CHUNK_IDX = {c: i for i, c in enumerate(W_CHUNK_COLS)}


class Tok:
    __slots__ = ("name", "w", "r")

    def __init__(self, name=""):
        self.name = name
        self.w = []
        self.r = []


class Sched:
    def __init__(self, nc, es):
        self.nc = nc
        self.es = es
        self.eng = {"pe": nc.tensor, "act": nc.scalar, "dve": nc.vector, "pool": nc.gpsimd, "sp": nc.sync}
        self.sem = {k: es.enter_context(nc.semaphore("s_" + k)) for k in self.eng}
        self.cnt = {k: 0 for k in self.eng}
        self.known = {k: {} for k in self.eng}
        self.dsem = {}
        self.dcnt = {}

    def _wait(self, ek, deps):
        e = self.eng[ek]
        best = {}
        for kind, key, val in deps:
            if kind == "c" and key == ek and ek == "pe":
                continue
            k2 = (kind, key)
            if best.get(k2, 0) < val:
                best[k2] = val
        for (kind, key), val in best.items():
            if self.known[ek].get((kind, key), 0) >= val:
                continue
            e.wait_ge(self.sem[key] if kind == "c" else self.dsem[key], val)
            self.known[ek][(kind, key)] = val

    @staticmethod
    def _deps(reads, writes):
        deps = []
        for b in reads:
            deps += b.w
        for b in writes:
            deps += b.w
            deps += b.r
        return deps

    def _guard(self, writes):
        pend = getattr(self, "_deferred", None)
        if pend:
            ids = set(id(b) for b in writes)
            for a, kw in pend:
                if any(id(b) in ids for b in kw.get("reads", ())):
                    self.flush()
                    return

    def op(self, ek, fn, reads=(), writes=()):
        self._guard(writes)
        self._wait(ek, self._deps(reads, writes))
        inst = fn(self.eng[ek])
        self.cnt[ek] += 1
        inst.then_inc(self.sem[ek], 1)
        me = ("c", ek, self.cnt[ek])
        for b in reads:
            b.r.append(me)
        for b in writes:
            b.w = [me]
            b.r = []
        return me

    def dma(self, qk, tok, out, in_, reads=(), writes=(), append=False):
        name = tok.name
        if name not in self.dsem:
            self.dsem[name] = self.es.enter_context(self.nc.semaphore("d_" + name))
            self.dcnt[name] = 0
        self._guard(writes)
        self._wait(qk, self._deps(reads, writes))
        inst = self.eng[qk].dma_start(out=out, in_=in_)
        self.dcnt[name] += 16
        inst.then_inc(self.dsem[name], 16)
        me = ("d", name, self.dcnt[name])
        for b in reads:
            b.r.append(me)
        for b in writes:
            b.w = (b.w + [me]) if append else [me]
            b.r = []
        return me

    def defer(self, *a, **kw):
        if not hasattr(self, "_deferred"):
            self._deferred = []
        self._deferred.append((a, kw))

    def flush(self):
        for a, kw in getattr(self, "_deferred", []):
            self.dma(*a, **kw)
        self._deferred = []

    def barrier(self, toks=()):
        self.flush()
        deps = [("c", k, v) for k, v in self.cnt.items() if v > 0]
        deps += [("d", k, v) for k, v in self.dcnt.items() if v > 0]
        for ek in self.eng:
            self._wait(ek, [d for d in deps if not (d[0] == "c" and d[1] == ek)])


class Rot:
    def __init__(self, items):
        self.items = items
        self.i = 0

    def next(self):
        it = self.items[self.i % len(self.items)]
        self.i += 1
        return it


def build_nc(dbg=False, phases="ABCD"):
    nc = bass.Bass("TRN2", target_bir_lowering=False)
    nc.dge_precook = False

    def dram(name, shape, dtype=F32, kind="ExternalInput"):
        return nc.dram_tensor(name, shape, dtype, kind=kind).ap()

    xT_d = dram("xT", [DM, SEQ], F32R)
    x_d = dram("x", [SEQ, DM])
    pT_d = dram("pT", [256, SEQ], F32R)
    wch_d = dram("wch", [len(W_CHUNK_COLS), 128, DM], F32R)
    wv_d = dram("wvh", [4, 128, 8 * 256], F32R)
    convw_d = dram("convw", [128, 8, 4])
    pvec_d = dram("pvec", [128, 8, 5])
    bf_d = dram("bf", [128, 4])
    wr_d = dram("wr", [8, 128, 128], F32R)
    wi_d = dram("wi", [8, 128, 128], F32R)
    wf2_d = dram("wf2", [16, 512])
    w_out_d = dram("w_out", [DM, DM], F32R)
    wq_d = dram("wq", [DM, DM], F32R)
    wpg_d = dram("wpg", [DM, DM], F32R)
    wpp_d = dram("wpp", [256, DM], F32R)
    ln_d = dram("ln", [4, DM])
    skT_d = dram("skT", [128, 8, 128], F32R)
    uT_d = dram("uT", [128, 128, DM], F32R)
    v_d = dram("v", [NE, DM], F32R)
    okind = "ExternalOutput"
    out_d = dram("out", [SEQ, DM], F32, okind)
    skind = okind if dbg else "Internal"
    mT_d = dram("mT_s", [DM, SEQ], F32R, skind)
    x1_d = dram("x1_s", [SEQ, DM], F32, skind)
    x1T_d = dram("x1T_s", [DM, SEQ], F32R, skind)
    acc_d = dram("acc_s", [SEQ, DM], F32, skind)
    s_d = dram("sc_s", [NT, 128, 8 * 256], F32, skind)
    if dbg:
        sel_d = dram("sel_s", [3, 128, SEQ], F32, okind)

    with ExitStack() as es:
        S = Sched(nc, es)

        def sb(stk, name, shape, dtype=F32):
            return stk.enter_context(nc.sbuf_tensor("sb_" + name, shape, dtype)), Tok(name)

        banks = []
        for i in range(8):
            banks.append((es.enter_context(nc.psum_tensor("ps%d" % i, [128, 512], F32)), Tok("ps%d" % i)))
        es.enter_context(nc.Block())

        iot, iot_k = sb(es, "iot", [128, 128])
        pid, pid_k = sb(es, "pid", [128, 1])
        ident, ident_k = sb(es, "ident", [128, 128])
        triu, triu_k = sb(es, "triu", [128, 128])
        S.op("pool", lambda e: e.iota(iot[:], [[1, 128]], base=0, channel_multiplier=0, allow_small_or_imprecise_dtypes=True), writes=[iot_k])
        S.op("pool", lambda e: e.iota(pid[:], [[0, 1]], base=0, channel_multiplier=1, allow_small_or_imprecise_dtypes=True), writes=[pid_k])
        S.op("dve", lambda e: e.tensor_scalar(ident[:], iot[:], pid[:, 0:1], None, op0=ALU.is_equal), reads=[iot_k, pid_k], writes=[ident_k])
        S.op("dve", lambda e: e.tensor_scalar(triu[:], iot[:], pid[:, 0:1], None, op0=ALU.is_ge), reads=[iot_k, pid_k], writes=[triu_k])
        def load_lnb(stk, gi):
            lnb, lnb_k = sb(stk, "lnb%d" % gi, [128, 2, DM])
            for i in range(2):
                S.dma("sp", lnb_k, lnb[:, i, :], ln_d[gi + i].partition_broadcast(128), writes=[lnb_k], append=True)
            return lnb, lnb_k

        def layer_norm(src, src_k, dst, dst_k, lnb_, tmp):
            lnb, lnb_k = lnb_
            st, st_k = tmp["st"]
            mv, mv_k = tmp["mv"]
            for c in range(2):
                S.op("dve", lambda e, c=c: e.bn_stats(st[:, c, :], src[:, c * 512:(c + 1) * 512]), reads=[src_k], writes=[st_k])
            S.op("dve", lambda e: e.bn_aggr(mv[:, 0:2], st[:].rearrange("p a b -> p (a b)")), reads=[st_k], writes=[mv_k])
            S.op("act", lambda e: e.activation(mv[:, 2:3], mv[:, 1:2], AF.Sqrt, bias=tmp["eps"][0][:, 0:1]), reads=[mv_k, tmp["eps"][1]], writes=[mv_k])
            S.op("dve", lambda e: e.reciprocal(mv[:, 3:4], mv[:, 2:3]), reads=[mv_k], writes=[mv_k])
            S.op("dve", lambda e: e.tensor_scalar(dst[:], src[:], mv[:, 0:1], mv[:, 3:4], op0=ALU.subtract, op1=ALU.mult), reads=[src_k, mv_k], writes=[dst_k])
            S.op("dve", lambda e: e.tensor_tensor(dst[:], dst[:], lnb[:, 0, :], ALU.mult), reads=[dst_k, lnb_k], writes=[dst_k])
            S.op("dve", lambda e: e.tensor_tensor(dst[:], dst[:], lnb[:, 1, :], ALU.add), reads=[dst_k, lnb_k], writes=[dst_k])

        epsln, epsln_k = sb(es, "epsln", [128, 1])
        epsrms, epsrms_k = sb(es, "epsrms", [128, 1])
        S.op("dve", lambda e: e.memset(epsln[:], LN_EPS), writes=[epsln_k])
        S.op("dve", lambda e: e.memset(epsrms[:], RMS_EPS), writes=[epsrms_k])
        lnst = sb(es, "lnst", [128, 2, 6])
        lnmv = sb(es, "lnmv", [128, 4])
        lntmp = {"st": lnst, "mv": lnmv, "eps": (epsln, epsln_k)}

        if "A" in phases:
          with ExitStack() as esA:
            xT, xT_k = sb(esA, "xTs", [128, 8, SEQ], F32R)
            for kc in range(8):
                S.dma("sp" if kc % 2 == 0 else "act", xT_k, xT[:, kc, :], xT_d[kc * 128:(kc + 1) * 128, :], writes=[xT_k], append=True)
            wbufs = Rot([sb(esA, "wb%d" % i, [128, 8, 128], F32R) for i in range(2)])
            cw, cw_k = sb(esA, "cw", [128, 8, 4])
            pv, pv_k = sb(esA, "pv", [128, 8, 5])
            bfs, bfs_k = sb(esA, "bfs", [128, 4])
            nbf, nbf_k = sb(esA, "nbf", [128, 4])
            nsp, nsp_k = sb(esA, "nsp", [128, 8])
            wr, wr_k = sb(esA, "wrs", [128, 8, 128], F32R)
            wi, wi_k = sb(esA, "wis", [128, 8, 128], F32R)
            wf2, wf2_k = sb(esA, "wf2s", [16, 512])
            S.dma("sp", cw_k, cw[:], convw_d, writes=[cw_k])
            S.dma("sp", pv_k, pv[:], pvec_d, writes=[pv_k])
            S.dma("sp", bfs_k, bfs[:], bf_d, writes=[bfs_k])
            S.dma("sp", wr_k, wr[:], wr_d.rearrange("n c d -> c n d"), writes=[wr_k])
            S.dma("sp", wi_k, wi[:], wi_d.rearrange("n c d -> c n d"), writes=[wi_k])
            S.dma("sp", wf2_k, wf2[:], wf2_d, writes=[wf2_k])
            S.op("act", lambda e: e.activation(nsp[:], pv[:, :, 1], AF.Exp, scale=-1.0), reads=[pv_k], writes=[nsp_k])
            S.op("act", lambda e: e.activation(nsp[:], nsp[:], AF.Ln, bias=1.0), reads=[nsp_k], writes=[nsp_k])
            S.op("dve", lambda e: e.tensor_scalar(nsp[:], nsp[:], -8.0, None, op0=ALU.mult), reads=[nsp_k], writes=[nsp_k])
            S.op("dve", lambda e: e.tensor_scalar(nbf[:], bfs[:], -1.0, None, op0=ALU.mult), reads=[bfs_k], writes=[nbf_k])
            tl = [sb(esA, "tl%d" % i, [128, SEQ + 4]) for i in range(5)]
            zbank = Rot(banks[0:4])

            def zsection(col, evac, flush=True):
                wb, wk = wbufs.next()
                S.dma("sp", wk, wb[:].rearrange("p k c -> p (k c)"), wch_d[W_CHUNK_IDX[col]], writes=[wk])
                if flush:
                    S.flush()
                for tt in range(4):
                    pb, pk = zbank.next()

                    def mm(e, pb=pb, tt=tt, wb=wb):
                        r = None
                        for kc in range(8):
                            r = e.matmul(pb[:, :], lhsT=wb[:, kc, :], rhs=xT[:, kc, tt * 512:(tt + 1) * 512], start=(kc == 0), stop=(kc == 7))
                        return r
                    S.op("pe", mm, reads=[wk, xT_k], writes=[pk])
                    evac(tt, pb, pk)

            def act_evac(dst, dst_k, func, off=0, **kw):
                def f(tt, pb, pk):
                    S.op("act", lambda e: e.activation(dst[:, off + tt * 512: off + (tt + 1) * 512], pb[:, :], func, **kw), reads=[pk], writes=[dst_k])
                return f


            with ExitStack() as esG:
                flT, flT_k = sb(esG, "flT", [16, SEQ])
                ones1, ones1_k = sb(esG, "ones1", [128, 128])
                S.op("pool", lambda e: e.memset(ones1[:], 1.0), writes=[ones1_k])
                v_sb, v_k = sb(esG, "v_sb", [128, NT, 256], F32R)
                khat, khat_k = sb(esG, "khat", [128, NT, 128], F32R)
                oT, oT_k = sb(esG, "oT", [128, 2, SEQ])
                S_sb, S_k = sb(esG, "S_sb", [128, 256], F32R)
                attn = Rot([sb(esG, "attn%d" % i, [128, 128], F32R) for i in range(2)])
                on_ = Rot([sb(esG, "on%d" % i, [128, 256]) for i in range(2)])
                ss_ = Rot([sb(esG, "ss%d" % i, [128, 2]) for i in range(2)])
                junk, junk_k = sb(esG, "junk", [128, 256])
                wv = Rot([sb(esG, "wv%d" % i, [128, 8, 256], F32R) for i in range(1)])
                qt, qt_k = sb(esG, "qtR", [128, SEQ], F32R)
                kt, kt_k = sb(esG, "ktR", [128, SEQ], F32R)
                (lB, lB_k), (Eb, Eb_k), (Ei, Ei_k) = tl[0:3]
                (kh, kh_k), (sr, sr_k), (sg, sg_k) = tl[0], tl[3], tl[4]
                pk_tr, pk_dS, pk_at, pk_ms = banks[2], banks[3], banks[4], banks[7]
                obank = Rot([banks[5], banks[6]])
                zsection(OFF_F, lambda tt, pb, pk: S.op("act", lambda e: e.copy(flT[:, tt * 512:(tt + 1) * 512], pb[0:16, :]), reads=[pk], writes=[flT_k]))
                for h in range(4):
                    for tt in range(4):
                        pb, pk = zbank.next()
                        S.op("pe", lambda e, pb=pb, tt=tt: e.matmul(pb[:, :], lhsT=wf2[:, h * 128:(h + 1) * 128], rhs=flT[:, tt * 512:(tt + 1) * 512], start=True, stop=True),
                             reads=[wf2_k, flT_k], writes=[pk])
                        S.op("act", lambda e, pb=pb, tt=tt: e.activation(lB[:, tt * 512:(tt + 1) * 512], pb[:, :], AF.Exp, scale=-1.0, bias=nbf[:, h:h + 1]), reads=[pk, nbf_k], writes=[lB_k])
                    S.op("act", lambda e: e.activation(lB[:, 0:SEQ], lB[:, 0:SEQ], AF.Ln, bias=1.0), reads=[lB_k], writes=[lB_k])
                    for c in range(NT):
                        S.op("dve", lambda e, c=c: e.tensor_tensor_scan(Ei[:, c * 128:(c + 1) * 128], ones1[:], lB[:, c * 128:(c + 1) * 128], 0.0, ALU.mult, ALU.add), reads=[lB_k, ones1_k], writes=[Ei_k])
                    S.op("act", lambda e: e.activation(Eb[:, 0:SEQ], Ei[:, 0:SEQ], AF.Exp, scale=-1.0 / 16.0), reads=[Ei_k], writes=[Eb_k])
                    S.op("act", lambda e: e.activation(Ei[:, 0:SEQ], Ei[:, 0:SEQ], AF.Exp, scale=1.0 / 16.0), reads=[Ei_k], writes=[Ei_k])
                    zsection(OFF_Q + h * 128, lambda tt, pb, pk: S.op("dve", lambda e: e.scalar_tensor_tensor(out=qt[:, tt * 512:(tt + 1) * 512], in0=pb[:, :], scalar=128.0 ** -0.5, in1=Eb[:, tt * 512:(tt + 1) * 512], op0=ALU.mult, op1=ALU.mult), reads=[pk, Eb_k], writes=[qt_k]))
                    zsection(OFF_K + h * 128, lambda tt, pb, pk: S.op("dve", lambda e: e.tensor_tensor(kt[:, tt * 512:(tt + 1) * 512], pb[:, :], Ei[:, tt * 512:(tt + 1) * 512], ALU.mult), reads=[pk, Ei_k], writes=[kt_k]))
                    S.op("dve", lambda e: e.tensor_tensor(kh[:, 0:SEQ].rearrange("p (c k) -> p c k", k=128), kt[:, 0:SEQ].bitcast(F32).rearrange("p (c k) -> p c k", k=128),
                                                          Eb[:, 127:SEQ:128].unsqueeze(2).to_broadcast([128, NT, 128]), ALU.mult), reads=[kt_k, Eb_k], writes=[kh_k])
                    wvb, wvk = wv.next()
                    S.dma("sp", wvk, wvb[:].rearrange("p k c -> p (k c)"), wv_d[h], writes=[wvk])
                    for ti in range(NT):
                        pb, pk = zbank.next()

                        def mmv(e, pb=pb, ti=ti, wvb=wvb):
                            r = None
                            for kc in range(8):
                                r = e.matmul(pb[:, 0:256], lhsT=xT[:, kc, ti * 128:(ti + 1) * 128], rhs=wvb[:, kc, :], start=(kc == 0), stop=(kc == 7))
                            return r
                        S.op("pe", mmv, reads=[wvk, xT_k], writes=[pk])
                        S.op("act", lambda e, pb=pb, ti=ti: e.copy(v_sb[:, ti, :], pb[:, 0:256]), reads=[pk], writes=[v_k])
                    for g in range(NT // 4):
                        pb, pk = pk_tr

                        def trs(e, g=g, pb=pb):
                            r = None
                            for j in range(4):
                                c = g * 4 + j
                                r = e.transpose(pb[:, j * 128:(j + 1) * 128], kh[:, c * 128:(c + 1) * 128], ident[:])
                            return r
                        S.op("pe", trs, reads=[kh_k, ident_k], writes=[pk])
                        S.op("act", lambda e, g=g, pb=pb: e.copy(khat[:, g * 4:(g + 1) * 4, :], pb[:, :].rearrange("p (j d) -> p j d", d=128)), reads=[pk], writes=[khat_k])
                    pend_tr = None

                    def do_tr(st):
                        on, on_k, cs_ = st

                        def trs(e):
                            r = None
                            for ec in range(2):
                                r = e.transpose(pk_ms[0][:, ec * 128:(ec + 1) * 128], on[:, ec * 128:(ec + 1) * 128], ident[:])
                            return r
                        S.op("pe", trs, reads=[on_k, ident_k], writes=[pk_ms[1]])
                        S.op("act", lambda e: e.copy(oT[:, :, cs_], pk_ms[0][:, 0:256].rearrange("p (a b) -> p a b", b=128)), reads=[pk_ms[1]], writes=[oT_k])

                    pend_rms = None

                    def do_rms(st):
                        ob, ok_, cs_ = st
                        ssb, ss_k = ss_.next()
                        S.op("act", lambda e: e.activation(junk[:], ob[:, 0:256], AF.Square, accum_out=ssb[:, 0:1]), reads=[ok_], writes=[ss_k, junk_k])
                        S.op("act", lambda e: e.activation(ssb[:, 1:2], ssb[:, 0:1], AF.Sqrt, scale=1.0 / 256.0, bias=epsrms[:, 0:1]), reads=[ss_k, epsrms_k], writes=[ss_k])
                        S.op("dve", lambda e: e.reciprocal(ssb[:, 1:2], ssb[:, 1:2]), reads=[ss_k], writes=[ss_k])
                        on, on_k = on_.next()
                        S.op("dve", lambda e: e.tensor_scalar(on[:], ob[:, 0:256], ssb[:, 1:2], None, op0=ALU.mult), reads=[ok_, ss_k], writes=[on_k])
                        return (on, on_k, cs_)

                    for c in range(NT):
                        cs = slice(c * 128, (c + 1) * 128)
                        at_sb, at_k = attn.next()
                        S.op("pe", lambda e: e.matmul(pk_at[0][:, 0:128], lhsT=kt[:, cs], rhs=qt[:, cs], start=True, stop=True), reads=[kt_k, qt_k], writes=[pk_at[1]])
                        S.op("dve", lambda e: e.tensor_tensor(at_sb[:], pk_at[0][:, 0:128], triu[:], ALU.mult), reads=[pk_at[1], triu_k], writes=[at_k])
                        ob, ok_ = obank.next()

                        def mmo(e, ob=ob, c=c, at_sb=at_sb, cs=cs):
                            r = e.matmul(ob[:, 0:256], lhsT=at_sb[:], rhs=v_sb[:, c, :], start=True, stop=(c == 0))
                            if c > 0:
                                r = e.matmul(ob[:, 0:256], lhsT=qt[:, cs], rhs=S_sb[:], start=False, stop=True)
                            return r
                        S.op("pe", mmo, reads=[v_k, at_k, S_k, qt_k], writes=[ok_])
                        if c < NT - 1:
                            S.op("pe", lambda e: e.matmul(pk_dS[0][:, 0:256], lhsT=khat[:, c, :], rhs=v_sb[:, c, :], start=True, stop=True), reads=[khat_k, v_k], writes=[pk_dS[1]])
                            if c == 0:
                                S.op("dve", lambda e: e.tensor_copy(S_sb[:], pk_dS[0][:, 0:256]), reads=[pk_dS[1]], writes=[S_k])
                            else:
                                S.op("dve", lambda e: e.scalar_tensor_tensor(out=S_sb[:], in0=S_sb[:].bitcast(F32), scalar=Eb[:, c * 128 + 127: c * 128 + 128], in1=pk_dS[0][:, 0:256], op0=ALU.mult, op1=ALU.add),
                                     reads=[S_k, Eb_k, pk_dS[1]], writes=[S_k])
                        new_tr = do_rms(pend_rms) if pend_rms is not None else None
                        if pend_tr is not None:
                            do_tr(pend_tr)
                        pend_tr = new_tr
                        pend_rms = (ob, ok_, cs)
                    new_tr = do_rms(pend_rms)
                    if pend_tr is not None:
                        do_tr(pend_tr)
                    do_tr(new_tr)
                    for ec in range(2):
                        n = 2 * h + ec
                        zsection(OFF_R + n * 128, act_evac(sr, sr_k, AF.Silu))
                        zsection(OFF_GB + n * 128, act_evac(sg, sg_k, AF.Sigmoid))
                        S.op("dve", lambda e, ec=ec, n=n: e.scalar_tensor_tensor(out=sr[:, 0:SEQ], in0=oT[:, ec, :], scalar=pv[:, n, 4:5], in1=sr[:, 0:SEQ], op0=ALU.mult, op1=ALU.mult),
                             reads=[oT_k, pv_k, sr_k], writes=[sr_k])
                        S.op("dve", lambda e: e.tensor_tensor(sg[:, 0:SEQ].bitcast(F32R), sr[:, 0:SEQ], sg[:, 0:SEQ], ALU.mult), reads=[sr_k, sg_k], writes=[sg_k])
                        S.defer("sp", sg_k, mT_d[n * 128:(n + 1) * 128, :], sg[:, 0:SEQ].bitcast(F32R), reads=[sg_k])
            S.barrier()
            tl = tl + [sb(esA, "tl%d" % i, [128, SEQ + 4]) for i in range(5, 11)]
            wbufs.items.append(sb(esA, "wb2", [128, 8, 128], F32R))
            zbank.items = [banks[i] for i in (0, 1, 2, 3, 6, 7)]
            (zxp, zxp_k), (u32, u32_k), (t3, t3_k) = tl[0:3]
            ra_ = [tl[3], tl[4]]
            ia_ = [tl[5], tl[6]]
            gy_ = [tl[7], tl[8]]
            sga_ = [tl[9], tl[10]]
            ur_ = [sb(esA, "ur%d" % i, [128, SEQ], F32R) for i in range(2)]
            mb, mb_k = sb(esA, "mb", [128, SEQ], F32R)
            S.op("pool", lambda e: e.memset(zxp[:, 0:3], 0.0), writes=[zxp_k])
            pr, pi = banks[4], banks[5]
            for n in range(8):
                (ra, ra_k), (ia, ia_k), (gy, gy_k), (sga, sga_k), (ur, ur_k) = ra_[n % 2], ia_[n % 2], gy_[n % 2], sga_[n % 2], ur_[n % 2]
                zsection(OFF_X + n * 128, lambda tt, pb, pk: S.op("act", lambda e: e.copy(zxp[:, 3 + tt * 512: 3 + (tt + 1) * 512], pb[:, :]), reads=[pk], writes=[zxp_k]), flush=False)
                zsection(OFF_Y + n * 128, act_evac(gy, gy_k, AF.Gelu), flush=False)
                zsection(OFF_GA + n * 128, act_evac(sga, sga_k, AF.Sigmoid), flush=False)
                S.flush()
                S.dma("sp", mb_k, mb[:], mT_d[n * 128:(n + 1) * 128, :], writes=[mb_k])
                S.op("dve", lambda e: e.tensor_scalar(u32[:, 0:SEQ], zxp[:, 3:3 + SEQ], cw[:, n, 3:4], pv[:, n, 0:1], op0=ALU.mult, op1=ALU.add), reads=[zxp_k, cw_k, pv_k], writes=[u32_k])
                for j in (2, 1):
                    S.op("dve", lambda e, j=j: e.scalar_tensor_tensor(out=u32[:, 0:SEQ], in0=zxp[:, j:j + SEQ], scalar=cw[:, n, j:j + 1], in1=u32[:, 0:SEQ], op0=ALU.mult, op1=ALU.add),
                         reads=[zxp_k, cw_k, u32_k], writes=[u32_k])
                S.op("dve", lambda e: e.scalar_tensor_tensor(out=ur[:, 0:SEQ], in0=zxp[:, 0:SEQ], scalar=cw[:, n, 0:1], in1=u32[:, 0:SEQ], op0=ALU.mult, op1=ALU.add),
                     reads=[zxp_k, cw_k, u32_k], writes=[ur_k])
                for tt in range(4):
                    ts_ = slice(tt * 512, (tt + 1) * 512)
                    S.op("pe", lambda e: e.matmul(pr[0][:, :], lhsT=wr[:, n, :], rhs=ur[:, ts_], start=True, stop=True), reads=[ur_k, wr_k], writes=[pr[1]])
                    S.op("act", lambda e: e.activation(ra[:, ts_], pr[0][:, :], AF.Sigmoid, bias=pv[:, n, 2:3]), reads=[pr[1], pv_k], writes=[ra_k])
                    S.op("pe", lambda e: e.matmul(pi[0][:, :], lhsT=wi[:, n, :], rhs=ur[:, ts_], start=True, stop=True), reads=[ur_k, wi_k], writes=[pi[1]])
                    S.op("act", lambda e: e.activation(ia[:, ts_], pi[0][:, :], AF.Sigmoid, bias=pv[:, n, 3:4]), reads=[pi[1], pv_k], writes=[ia_k])
                S.op("act", lambda e: e.activation(ra[:, 0:SEQ], ra[:, 0:SEQ], AF.Exp, scale=nsp[:, n:n + 1]), reads=[ra_k, nsp_k], writes=[ra_k])
                S.op("act", lambda e: e.activation(t3[:, 0:SEQ], ra[:, 0:SEQ], AF.Square), reads=[ra_k], writes=[t3_k])
                S.op("act", lambda e: e.activation(t3[:, 0:SEQ], t3[:, 0:SEQ], AF.Sqrt, scale=-1.0, bias=1.0), reads=[t3_k], writes=[t3_k])
                S.op("dve", lambda e: e.tensor_tensor(ia[:, 0:SEQ], ia[:, 0:SEQ], ur[:, 0:SEQ].bitcast(F32), ALU.mult), reads=[ia_k, ur_k], writes=[ia_k])
                S.op("dve", lambda e: e.tensor_tensor(ia[:, 0:SEQ], ia[:, 0:SEQ], t3[:, 0:SEQ], ALU.mult), reads=[ia_k, t3_k], writes=[ia_k])
                S.op("dve", lambda e: e.tensor_tensor_scan(t3[:, 0:SEQ], ra[:, 0:SEQ], ia[:, 0:SEQ], 0.0, ALU.mult, ALU.add), reads=[ra_k, ia_k, t3_k], writes=[t3_k])
                S.op("dve", lambda e: e.tensor_tensor(gy[:, 0:SEQ], gy[:, 0:SEQ], sga[:, 0:SEQ], ALU.mult), reads=[gy_k, sga_k], writes=[gy_k])
                S.op("dve", lambda e: e.tensor_tensor(gy[:, 0:SEQ], gy[:, 0:SEQ], t3[:, 0:SEQ], ALU.mult), reads=[gy_k, t3_k], writes=[gy_k])
                S.op("dve", lambda e: e.tensor_tensor(mb[:], gy[:, 0:SEQ], mb[:].bitcast(F32), ALU.add), reads=[gy_k, mb_k], writes=[mb_k])
                S.defer("sp", mb_k, mT_d[n * 128:(n + 1) * 128, :], mb[:], reads=[mb_k])
          S.barrier()

        if "B" in phases:
          with ExitStack() as esB:
            lnb1 = load_lnb(esB, 0)
            wo, wo_k = sb(esB, "wo", [128, 8, DM], F32R)
            for kc in range(8):
                S.dma("sp", wo_k, wo[:, kc, :], w_out_d[kc * 128:(kc + 1) * 128, :], writes=[wo_k], append=True)
            wpg, wpg_k = sb(esB, "wpg", [128, 8, DM], F32R)
            wpp, wpp_k = sb(esB, "wpp", [128, 2, DM], F32R)
            pg = Rot([sb(esB, "pg%d" % i, [128, 2, 512], F32R) for i in range(2)])
            sgt = Rot([sb(esB, "sgt%d" % i, [128, DM]) for i in range(2)])
            pg_cur = [None]
            mg = Rot([sb(esB, "mg%d" % i, [128, 8, 512], F32R) for i in range(2)])
            xt = Rot([sb(esB, "xt%d" % i, [128, DM]) for i in range(2)])
            yt = Rot([sb(esB, "yt%d" % i, [128, DM]) for i in range(2)])
            x1t = Rot([sb(esB, "x1t%d" % i, [128, DM]) for i in range(3)])
            x1Tt = Rot([sb(esB, "x1Tt%d" % i, [128, 8, 128], F32R) for i in range(3)])
            mixb = Rot([(banks[0], banks[1])])
            trb = Rot([(banks[2], banks[3])])
            gateb = (banks[4], banks[5])
            ppb_ = (banks[6], banks[7])
            mg_cur = [None]

            def b_front(ti):
                gi, tj = divmod(ti, 4)
                if tj == 0:
                    mgb, mgk = mg.next()
                    S.dma("sp", mgk, mgb[:], mT_d[:, gi * 512:(gi + 1) * 512].rearrange("(k p) t -> p k t", p=128), writes=[mgk])
                    mg_cur[0] = (mgb, mgk)
                    pgb, pgk = pg.next()
                    S.dma("sp", pgk, pgb[:], pT_d[:, gi * 512:(gi + 1) * 512].rearrange("(k p) t -> p k t", p=128), writes=[pgk])
                    pg_cur[0] = (pgb, pgk)
                    if ti == 0:
                        for kc in range(8):
                            S.dma("sp", wpg_k, wpg[:, kc, :], wpg_d[kc * 128:(kc + 1) * 128, :], writes=[wpg_k], append=True)
                        for kc in range(2):
                            S.dma("sp", wpp_k, wpp[:, kc, :], wpp_d[kc * 128:(kc + 1) * 128, :], writes=[wpp_k], append=True)
                mgb, mgk = mg_cur[0]
                xb, xk = xt.next()
                S.dma("act", xk, xb[:], x_d[ti * 128:(ti + 1) * 128, :], writes=[xk])
                S.flush()
                (b0, b1) = mixb.next()
                for half, (pb, pk) in enumerate((b0, b1)):
                    def mm(e, pb=pb, half=half):
                        r = None
                        for kc in range(8):
                            r = e.matmul(pb[:, :], lhsT=mgb[:, kc, tj * 128:(tj + 1) * 128], rhs=wo[:, kc, half * 512:(half + 1) * 512], start=(kc == 0), stop=(kc == 7))
                        return r
                    S.op("pe", mm, reads=[mgk, wo_k], writes=[pk])
                yb, yk = yt.next()
                for half, (pb, pk) in enumerate((b0, b1)):
                    hs = slice(half * 512, (half + 1) * 512)
                    S.op("dve", lambda e, pb=pb, hs=hs: e.scalar_tensor_tensor(out=yb[:, hs], in0=xb[:, hs], scalar=ALPHA, in1=pb[:, :], op0=ALU.mult, op1=ALU.add), reads=[xk, pk], writes=[yk])
                x1b, x1k = x1t.next()
                layer_norm(yb, yk, x1b, x1k, lnb1, lntmp)
                return (ti, x1b, x1k, pg_cur[0])

            def b_back1(st):
                ti, x1b, x1k, (pgb, pgk) = st
                (t0, t1) = trb.next()
                x1Tb, x1Tk = x1Tt.next()
                for half, (pb, pk) in enumerate((t0, t1)):
                    def trs(e, pb=pb, half=half):
                        r = None
                        for j in range(4):
                            kc = half * 4 + j
                            r = e.transpose(pb[:, j * 128:(j + 1) * 128], x1b[:, kc * 128:(kc + 1) * 128], ident[:])
                        return r
                    S.op("pe", trs, reads=[x1k, ident_k], writes=[pk])
                    S.op("act", lambda e, pb=pb, half=half: e.copy(x1Tb[:, half * 4:(half + 1) * 4, :], pb[:, :].rearrange("p (j d) -> p j d", d=128)), reads=[pk], writes=[x1Tk])
                S.defer("sp", x1Tk, x1T_d[:, ti * 128:(ti + 1) * 128].rearrange("(k p) t -> p k t", p=128), x1Tb[:], reads=[x1Tk])
                return st + (x1Tb, x1Tk)

            def b_back2(st):
                ti, x1b, x1k, (pgb, pgk), x1Tb, x1Tk = st
                tsl = slice((ti % 4) * 128, (ti % 4 + 1) * 128)
                sgb, sgk = sgt.next()
                for half in range(2):
                    hs = slice(half * 512, (half + 1) * 512)
                    pbk, pkk = gateb[half]

                    def mmg(e, pbk=pbk, hs=hs):
                        r = None
                        for kc in range(8):
                            r = e.matmul(pbk[:, :], lhsT=x1Tb[:, kc, :], rhs=wpg[:, kc, hs], start=(kc == 0), stop=(kc == 7))
                        return r
                    S.op("pe", mmg, reads=[x1Tk, wpg_k], writes=[pkk])
                    S.op("act", lambda e, pbk=pbk, hs=hs: e.activation(sgb[:, hs], pbk[:, :], AF.Sigmoid), reads=[pkk], writes=[sgk])
                    ppb, ppk = ppb_[half]

                    def mmp(e, ppb=ppb, hs=hs):
                        r = None
                        for kc in range(2):
                            r = e.matmul(ppb[:, :], lhsT=pgb[:, kc, tsl], rhs=wpp[:, kc, hs], start=(kc == 0), stop=(kc == 1))
                        return r
                    S.op("pe", mmp, reads=[pgk, wpp_k], writes=[ppk])
                    S.op("dve", lambda e, ppb=ppb, hs=hs: e.tensor_tensor(sgb[:, hs], sgb[:, hs], ppb[:, :], ALU.mult), reads=[sgk, ppk], writes=[sgk])
                S.op("dve", lambda e: e.scalar_tensor_tensor(out=sgb[:], in0=x1b[:], scalar=ALPHA, in1=sgb[:], op0=ALU.mult, op1=ALU.add), reads=[x1k, sgk], writes=[sgk])
                S.defer("sp", sgk, acc_d[ti * 128:(ti + 1) * 128, :], sgb[:], reads=[sgk])

            st1 = None
            st2 = None
            for ti in range(NT):
                cur = b_front(ti)
                if st2 is not None:
                    b_back2(st2)
                st2 = b_back1(st1) if st1 is not None else None
                st1 = cur
            S.flush()
            if st2 is not None:
                b_back2(st2)
            st2 = b_back1(st1)
            S.flush()
            b_back2(st2)
          S.barrier()

        if "C" in phases:
          with ExitStack() as esCD:
            NSEL = 3
            iT, _ = sb(esCD, "iT", [128, NSEL * TB])
            jT, _ = sb(esCD, "jT", [128, NSEL * TB])
            gT, _ = sb(esCD, "gT", [128, NSEL * TB])
            selk = [(Tok("iT%d" % i), Tok("jT%d" % i), Tok("gT%d" % i)) for i in range(NSEL)]
            s_sbD, s_kD = sb(esCD, "s_sbD", [128, 8, 256])
            sw, sw_k = sb(esCD, "sw", [128, 256])
            top, top_k = sb(esCD, "top", [128, 8, 2, 16])
            idxu, idxu_k = sb(esCD, "idxu", [128, 8, 2, 16], U32)
            idxf, idxf_k = sb(esCD, "idxf", [128, 8, 2, 16])
            cand, cand_k = sb(esCD, "cand", [128, 8, 256])
            c16, c16_k = sb(esCD, "c16", [128, 8, 16])
            posu, posu_k = sb(esCD, "posu", [128, 8, 16], U32)
            abu, abu_k = sb(esCD, "abu", [128, 2, 8, 16], U32)
            abf, abf_k = sb(esCD, "abf", [128, 2, 8, 16])
            eq, eq_k = cand[:].rearrange("p h (a b) -> p h a b", b=16), cand_k
            exs = [sb(esCD, "ex%d" % i, [128, 8, 16]) for i in range(4)]
            zz, zz_k = sb(esCD, "zz", [128, 8])
            ijgs = [sb(esCD, "ijg%d" % i, [128, 3, 128]) for i in range(4)]

            SELPAD = 6

            def select_tile(ti, src_=None, trbank=None, ijg_=None, ex_=None):
                blk_ = ti // (TB // 128)
                sl = blk_ % NSEL
                col = sl * TB + (ti % (TB // 128)) * 128
                ijg, ijg_k = ijg_ if ijg_ is not None else ijgs[0]
                ex, ex_k = ex_ if ex_ is not None else exs[0]
                if src_ is None:
                    s_sb, s_k = s_sbD, s_kD
                    S.dma("sp", s_k, s_sb[:].rearrange("p h k -> p (h k)"), s_d[ti], writes=[s_k])
                    yield
                else:
                    s_sb, s_k = src_
                for h in range(8):
                    for p in range(2):
                        src = s_sb[:, h, p * 128:(p + 1) * 128]
                        S.op("dve", lambda e: e.max(top[:, h, p, 0:8], src), reads=[s_k], writes=[top_k])
                        yield
                        S.op("dve", lambda e: e.max_index(idxu[:, h, p, 0:8], top[:, h, p, 0:8], src), reads=[s_k, top_k], writes=[idxu_k])
                        yield
                        S.op("dve", lambda e: e.match_replace(sw[:, 0:128], top[:, h, p, 0:8], src, -1e30), reads=[s_k, top_k], writes=[sw_k])
                        yield
                        S.op("dve", lambda e: e.max(top[:, h, p, 8:16], sw[:, 0:128]), reads=[sw_k], writes=[top_k])
                        yield
                        S.op("dve", lambda e: e.max_index(idxu[:, h, p, 8:16], top[:, h, p, 8:16], sw[:, 0:128]), reads=[sw_k, top_k], writes=[idxu_k])
                        yield
                S.op("dve", lambda e: e.tensor_copy(idxf[:], idxu[:]), reads=[idxu_k], writes=[idxf_k])
                yield
                S.op("dve", lambda e: e.tensor_tensor(cand[:].rearrange("p h (a b) -> p h a b", b=16), top[:, :, 0, :].unsqueeze(3).to_broadcast([128, 8, 16, 16]),
                                                      top[:, :, 1, :].unsqueeze(2).to_broadcast([128, 8, 16, 16]), ALU.add), reads=[top_k], writes=[cand_k])
                yield
                for h in range(8):
                    src = cand[:, h, :]
                    S.op("dve", lambda e: e.max(c16[:, h, 0:8], src), reads=[cand_k], writes=[c16_k])
                    yield
                    S.op("dve", lambda e: e.max_index(posu[:, h, 0:8], c16[:, h, 0:8], src), reads=[cand_k, c16_k], writes=[posu_k])
                    yield
                    S.op("dve", lambda e: e.match_replace(sw[:], c16[:, h, 0:8], src, -1e30), reads=[cand_k, c16_k], writes=[sw_k])
                    yield
                    S.op("dve", lambda e: e.max(c16[:, h, 8:16], sw[:]), reads=[sw_k], writes=[c16_k])
                    yield
                    S.op("dve", lambda e: e.max_index(posu[:, h, 8:16], c16[:, h, 8:16], sw[:]), reads=[sw_k, c16_k], writes=[posu_k])
                    yield
                S.op("dve", lambda e: e.tensor_tensor(ex[:], c16[:], c16[:, :, 0:1].to_broadcast([128, 8, 16]), ALU.subtract), reads=[c16_k], writes=[ex_k])
                yield
                S.op("dve", lambda e: e.tensor_single_scalar(abu[:, 0, :, :], posu[:], 4, ALU.logical_shift_right), reads=[posu_k], writes=[abu_k])
                yield
                S.op("dve", lambda e: e.tensor_single_scalar(abu[:, 1, :, :], posu[:], 15, ALU.bitwise_and), reads=[posu_k], writes=[abu_k])
                yield
                S.op("dve", lambda e: e.tensor_copy(abf[:], abu[:]), reads=[abu_k], writes=[abf_k])
                yield
                for p in range(2):
                    S.op("dve", lambda e: e.tensor_tensor(eq, abf[:, p, :, :].unsqueeze(3).to_broadcast([128, 8, 16, 16]),
                                                          iot[:, 0:16].unsqueeze(1).unsqueeze(1).to_broadcast([128, 8, 16, 16]), ALU.is_equal), reads=[abf_k, iot_k], writes=[eq_k])
                    yield
                    S.op("dve", lambda e: e.tensor_tensor(eq, eq, idxf[:, :, p, :].unsqueeze(2).to_broadcast([128, 8, 16, 16]), ALU.mult), reads=[eq_k, idxf_k], writes=[eq_k])
                    yield
                    S.op("dve", lambda e: e.tensor_reduce(ijg[:, p, :], eq.rearrange("p h k a -> p (h k) a"), AX.X, ALU.add), reads=[eq_k], writes=[ijg_k])
                    yield
                yield "act"
                for _ in range(SELPAD):
                    yield
                S.op("act", lambda e: e.activation(ex[:], ex[:], AF.Exp), reads=[ex_k], writes=[ex_k])
                for _ in range(SELPAD):
                    yield
                S.op("dve", lambda e: e.tensor_reduce(zz[:], ex[:], AX.X, ALU.add), reads=[ex_k], writes=[zz_k])
                yield
                S.op("dve", lambda e: e.reciprocal(zz[:], zz[:]), reads=[zz_k], writes=[zz_k])
                yield
                S.op("dve", lambda e: e.tensor_tensor(ijg[:, 2, :].rearrange("p (h k) -> p h k", k=16), ex[:], zz[:].unsqueeze(2).to_broadcast([128, 8, 16]), ALU.mult), reads=[ex_k, zz_k], writes=[ijg_k])
                yield
                yield "pe"
                for _ in range(SELPAD):
                    yield
                pb, pk = trbank if trbank is not None else gbk.next()

                def trs(e):
                    r = None
                    for j in range(3):
                        r = e.transpose(pb[:, j * 128:(j + 1) * 128], ijg[:, j, :], ident[:])
                    return r
                S.op("pe", trs, reads=[ijg_k, ident_k], writes=[pk])
                for j, dst in enumerate((iT, jT, gT)):
                    S.op("act", lambda e: e.copy(dst[:, col:col + 128], pb[:, j * 128:(j + 1) * 128]), reads=[pk], writes=[selk[sl][j]])
                yield
                if dbg:
                    for j, dst in enumerate((iT, jT, gT)):
                        S.dma("sp", selk[sl][j], sel_d[j, :, ti * 128:(ti + 1) * 128], dst[:, col:col + 128], reads=[selk[sl][j]])

            def select_block(blk_):
                for tt_ in range(TB // 128):
                    yield from select_tile(blk_ * (TB // 128) + tt_)

            gbk = Rot([banks[6], banks[7]])

            with ExitStack() as esC:
                wq, wq_k = sb(esC, "wq", [128, 8, DM], F32R)
                for kc in range(8):
                    S.dma("sp", wq_k, wq[:, kc, :], wq_d[kc * 128:(kc + 1) * 128, :], writes=[wq_k], append=True)
                bd, bd_k = sb(esC, "bd", [128, 8, 256], F32R)
                S.op("dve", lambda e: e.tensor_scalar(bd[:].rearrange("p h k -> p (h k)"), iot[:, 0:1].to_broadcast([128, 2048]), 0.0, None, op0=ALU.mult), reads=[iot_k], writes=[bd_k])
                S.dma("sp", bd_k, bd[0:64, :, 0:128], skT_d[0:64], writes=[bd_k], append=True)
                S.dma("sp", bd_k, bd[64:128, :, 128:256], skT_d[64:128], writes=[bd_k], append=True)
                xg = Rot([sb(esC, "xq%d" % i, [128, 8, 512], F32R) for i in range(2)])
                qT, qT_k = sb(esC, "qT", [128, 8, 512], F32R)
                ssb = Rot([sb(esC, "s_sb%d" % i, [128, 8, 256]) for i in range(2)])
                ssb_early = [sb(esC, "s_sbe%d" % i, [128, 8, 256]) for i in range(TB // 128)]
                zb = Rot([banks[0], banks[1], banks[6], banks[7]])
                early_sel = []
                for gi in range(4):
                    xgb, xgk = xg.next()
                    S.dma("sp", xgk, xgb[:], x1T_d[:, gi * 512:(gi + 1) * 512].rearrange("(k p) t -> p k t", p=128), writes=[xgk])
                    for h in range(8):
                        pb, pk = zb.next()

                        def mmq(e, pb=pb, h=h, xgb=xgb):
                            r = None
                            for kc in range(8):
                                r = e.matmul(pb[:, :], lhsT=wq[:, kc, h * 128:(h + 1) * 128], rhs=xgb[:, kc, :], start=(kc == 0), stop=(kc == 7))
                            return r
                        S.op("pe", mmq, reads=[wq_k, xgk], writes=[pk])
                        S.op("act", lambda e, pb=pb, h=h: e.copy(qT[:, h, :], pb[:, :]), reads=[pk], writes=[qT_k])
                    for tj in range(4):
                        ti = gi * 4 + tj
                        tsl = slice(tj * 128, (tj + 1) * 128)
                        s_sb, s_k = ssb_early[ti] if ti < TB // 128 else ssb.next()
                        for hp in range(4):
                            pb, pk = banks[2 + hp]

                            def mms(e, pb=pb, hp=hp, tsl=tsl):
                                r = None
                                for hh in range(2):
                                    h = hp * 2 + hh
                                    r = e.matmul(pb[:, hh * 256:(hh + 1) * 256], lhsT=qT[:, h, tsl], rhs=bd[:, h, :], start=True, stop=True)
                                return r
                            S.op("pe", mms, reads=[qT_k, bd_k], writes=[pk])
                            S.op("act", lambda e, pb=pb, hp=hp: e.copy(s_sb[:, hp * 2:hp * 2 + 2, :], pb[:, :].rearrange("p (a b) -> p a b", b=256)), reads=[pk], writes=[s_k])
                        if ti < TB // 128:
                            g_ = select_tile(ti, src_=(s_sb, s_k), trbank=banks[6 + ti % 2], ijg_=ijgs[ti], ex_=exs[ti])
                            for r_ in g_:
                                if r_ == "act":
                                    break
                            early_sel.append(g_)
                        else:
                            S.dma("sp", s_k, s_d[ti], s_sb[:].rearrange("p h k -> p (h k)"), reads=[s_k])
                for g_ in early_sel:
                    for _ in g_:
                        pass
            S.barrier()
            if "D" in phases:
              with ExitStack() as esD:
                lnb2 = load_lnb(esD, 2)
                Gh = [sb(esD, "G%d" % i, [128, TB, 64], BF16) for i in range(2)]
                iotb, iotb_k = sb(esD, "iotb", [128, 128], BF16)
                S.op("dve", lambda e: e.tensor_copy(iotb[:], iot[:]), reads=[iot_k], writes=[iotb_k])
                GT = 8
                AB = Rot([sb(esD, "AB%d" % i, [128, GT, 192], BF16) + (Tok("ABb%d" % i),) for i in range(4)])
                xb_ = Rot([sb(esD, "xD%d" % i, [128, 8, TB], F32R) for i in range(2)])
                CH = 2
                ub_ = Rot([sb(esD, "ub%d" % i, [128, CH, DM], F32R) for i in range(3)])
                vb_ = Rot([sb(esD, "vb%d" % i, [128, CH, DM], F32R) for i in range(3)])
                gel_ = Rot([sb(esD, "gel%d" % i, [128, TB]) for i in range(2)])
                W_ = Rot([sb(esD, "W%d" % i, [128, TB], F32R) for i in range(2)])
                accb = Rot([sb(esD, "accb%d" % i, [128, DM]) for i in range(1)])
                yb_ = Rot([sb(esD, "yD%d" % i, [128, DM]) for i in range(2)])
                outb = [banks[0], banks[1], banks[2], banks[3]]
                hb = Rot([banks[4], banks[5]])
                NBLK = SEQ // TB

                def g_onehots_a(blk, half, q):
                    ab, ab_k, abb_k = AB.next()
                    sl = blk % NSEL
                    iT_k, jT_k, gT_k = selk[sl]
                    ts_ = slice(sl * TB + q * GT, sl * TB + (q + 1) * GT)
                    S.op("dve", lambda e: e.tensor_tensor(ab[:, :, 0:64], iotb[:, half * 64:(half + 1) * 64].unsqueeze(1).to_broadcast([128, GT, 64]),
                                                          iT[:, ts_].unsqueeze(2).to_broadcast([128, GT, 64]), ALU.is_equal), reads=[iotb_k, iT_k], writes=[ab_k])
                    S.op("pool", lambda e: e.tensor_tensor(ab[:, :, 0:64], ab[:, :, 0:64], gT[:, ts_].unsqueeze(2).to_broadcast([128, GT, 64]), ALU.mult), reads=[gT_k, ab_k], writes=[ab_k])
                    return (ab, ab_k, abb_k, half, q, ts_, jT_k)

                def g_onehots_b(st):
                    ab, ab_k, abb_k, half, q, ts_, jT_k = st
                    S.op("dve", lambda e: e.tensor_tensor(ab[:, :, 64:192], iotb[:].unsqueeze(1).to_broadcast([128, GT, 128]),
                                                          jT[:, ts_].unsqueeze(2).to_broadcast([128, GT, 128]), ALU.is_equal), reads=[iotb_k, jT_k], writes=[abb_k])
                    return st

                def g_matmul(st):
                    ab, ab_k, abb_k, half, q, ts_, jT_k = st
                    pb, pk = gbk.next()

                    def mmG(e):
                        r = None
                        for tl_ in range(GT):
                            r = e.matmul(pb[:, tl_ * 64:(tl_ + 1) * 64], lhsT=ab[:, tl_, 64:192], rhs=ab[:, tl_, 0:64], start=True, stop=True)
                        return r
                    S.op("pe", mmG, reads=[ab_k, abb_k], writes=[pk])
                    g_, gk_ = Gh[half]
                    S.op("act", lambda e: e.copy(g_[:, q * GT:(q + 1) * GT, :], pb[:, 0:GT * 64].rearrange("p (t i) -> p t i", i=64)), reads=[pk], writes=[gk_])

                for q in range(TB // GT):
                    g_matmul(g_onehots_b(g_onehots_a(0, 0, q)))

                def mmV(e, Wt, vb, cc, ci):
                    r = None
                    for tt in range(TB // 128):
                        for half in range(2):
                            r = e.matmul(outb[tt * 2 + half][0][:, :], lhsT=Wt[:, tt * 128:(tt + 1) * 128], rhs=vb[:, cc, half * 512:(half + 1) * 512], start=(ci == 0), stop=(ci == 127))
                    return r

                for blk in range(NBLK):
                    t0 = blk * TB
                    if blk == 0:
                        xnext = xb_.next()
                        S.dma("sp", xnext[1], xnext[0][:], x1T_d[:, 0:TB].rearrange("(k p) t -> p k t", p=128), writes=[xnext[1]])
                    xb, xk = xnext
                    if blk + 1 < NBLK:
                        xnext = xb_.next()
                        S.dma("sp", xnext[1], xnext[0][:], x1T_d[:, t0 + TB:t0 + 2 * TB].rearrange("(k p) t -> p k t", p=128), writes=[xnext[1]])
                    pend = None
                    if blk == 0:
                        def _chain():
                            yield from select_block(1)
                            yield from select_block(2)
                        selgen = _chain()
                        selrate = 6
                    else:
                        selgen = select_block(blk + 2) if blk + 2 < NBLK else None
                        selrate = 3
                    gcur = None
                    gq = []
                    for cg in range(128 // CH):
                        ub, uk = ub_.next()
                        vb, vk = vb_.next()
                        S.dma("sp", uk, ub[:], uT_d[cg * CH:(cg + 1) * CH].rearrange("c p f -> p c f"), writes=[uk])
                        S.dma("sp", vk, vb[:], v_d[cg * CH * 128:(cg + 1) * CH * 128, :].rearrange("(c p) d -> p c d", p=128), writes=[vk])
                        if cg == 2:
                            S.flush()
                        for cc in range(CH):
                            ci = cg * CH + cc
                            pb, pk = hb.next()

                            def mmH(e, pb=pb, cc=cc, ub=ub, xb=xb):
                                r = None
                                for kc in range(8):
                                    r = e.matmul(pb[:, 0:TB], lhsT=ub[:, cc, kc * 128:(kc + 1) * 128], rhs=xb[:, kc, :], start=(kc == 0), stop=(kc == 7))
                                return r
                            S.op("pe", mmH, reads=[uk, xk], writes=[pk])
                            gl, gl_k = gel_.next()
                            S.op("act", lambda e, pb=pb, gl=gl: e.activation(gl[:], pb[:, 0:TB], AF.Gelu), reads=[pk], writes=[gl_k])
                            if pend is not None:
                                pW, pWk, pvb, pvk, pcc, pci = pend
                                S.op("pe", lambda e: mmV(e, pW, pvb, pcc, pci), reads=[pWk, pvk], writes=[outb[i][1] for i in range(4)])
                            Wt, W_k = W_.next()
                            g_, gk_ = Gh[ci // 64]
                            S.op("pool", lambda e, gl=gl, Wt=Wt, ci=ci: e.tensor_tensor(Wt[:], gl[:], g_[:, :, ci % 64], ALU.mult), reads=[gl_k, gk_], writes=[W_k])
                            pend = (Wt, W_k, vb, vk, cc, ci)
                            if selgen is not None:
                                for _ in range(selrate if ci < 127 else 100000):
                                    if next(selgen, "done") == "done":
                                        selgen = None
                                        break
                            if ci % 2 == 0:
                                if len(gq) >= 2:
                                    g_matmul(gq.pop(0))
                                if ci < 64:
                                    gcur = g_onehots_a(blk, 1, ci // 2)
                                elif blk + 1 < NBLK:
                                    gcur = g_onehots_a(blk + 1, 0, (ci - 64) // 2)
                            else:
                                if gcur is not None:
                                    gq.append(g_onehots_b(gcur))
                                    gcur = None
                                if ci == 63 or ci == 127:
                                    while gq:
                                        g_matmul(gq.pop(0))
                    pW, pWk, pvb, pvk, pcc, pci = pend
                    S.op("pe", lambda e: mmV(e, pW, pvb, pcc, pci), reads=[pWk, pvk], writes=[outb[i][1] for i in range(4)])
                    for tt in range(TB // 128):
                        ti = blk * (TB // 128) + tt
                        ab2, ab2_k = accb.next()
                        S.dma("act", ab2_k, ab2[:], acc_d[ti * 128:(ti + 1) * 128, :], writes=[ab2_k])
                        yb, yk = yb_.next()
                        for half in range(2):
                            hs = slice(half * 512, (half + 1) * 512)
                            pb, pk = outb[tt * 2 + half]
                            S.op("dve", lambda e, pb=pb, hs=hs: e.tensor_tensor(yb[:, hs], ab2[:, hs], pb[:, :], ALU.add), reads=[ab2_k, pk], writes=[yk])
                        ob, ok_ = yb, yk
                        layer_norm(yb, yk, ob, ok_, lnb2, lntmp)
                        S.defer("sp", ok_, out_d[ti * 128:(ti + 1) * 128, :], ob[:], reads=[ok_])
                S.barrier()
        S.barrier()
    return nc


def prep_inputs(inp, b):
    f = lambda a: np.ascontiguousarray(a, dtype=np.float32)
    x = inp["x"][b]
    d = {}
    d["xT"] = f(x.T)
    d["x"] = f(x)
    d["pT"] = f(inp["p"][0, b].T)
    w_in = inp["w_in"][0]
    d["wch"] = f(np.stack([w_in[:, c:c + 128].reshape(8, 128, 128).transpose(1, 0, 2).reshape(128, 1024) for c in W_CHUNK_COLS]))
    d["wvh"] = f(np.stack([w_in[:, OFF_V + h * 256: OFF_V + (h + 1) * 256].reshape(8, 128, 256).transpose(1, 0, 2).reshape(128, 2048) for h in range(4)]))
    d["convw"] = f(inp["conv_w"][0].reshape(4, 8, 128).transpose(2, 1, 0))
    pv = np.stack([inp["conv_b"][0], inp["lru_lambda"][0], inp["lru_br"][0].reshape(-1), inp["lru_bi"][0].reshape(-1), inp["gla_norm_g"][0]], axis=-1)
    d["pvec"] = f(pv.reshape(8, 128, 5).transpose(1, 0, 2))
    d["bf"] = f(inp["gla_bf"][0].reshape(4, 128).T)
    d["wr"] = f(inp["lru_wr"][0])
    d["wi"] = f(inp["lru_wi"][0])
    d["wf2"] = f(inp["gla_wf2"][0])
    d["w_out"] = f(inp["w_out"][0])
    d["wq"] = f(inp["peer_wq"][0])
    d["wpg"] = f(inp["ple_gate_w"][0])
    d["wpp"] = f(inp["ple_proj_w"][0])
    d["ln"] = f(np.stack([inp["ln1_g"][0], inp["ln1_b"][0], inp["ln2_g"][0], inp["ln2_b"][0]]))
    d["skT"] = f(inp["peer_subkeys"][0].transpose(1, 3, 0, 2).reshape(128, 8, 128))
    d["uT"] = f(inp["peer_u"][0].reshape(128, 128, 8, 128).transpose(0, 3, 2, 1).reshape(128, 128, 1024))
    d["v"] = f(inp["peer_v"][0])
    return d


def kernel(**inputs):
    inp = {k: np.asarray(v) for k, v in inputs.items()}
    n = 8
    nc = build_nc()
    shared = prep_inputs(inp, 0)
    in_maps = []
    for b in range(n):
        d = dict(shared)
        x = inp["x"][b]
        d["xT"] = np.ascontiguousarray(x.T, dtype=np.float32)
        d["x"] = np.ascontiguousarray(x, dtype=np.float32)
        d["pT"] = np.ascontiguousarray(inp["p"][0, b].T, dtype=np.float32)
        in_maps.append(d)
    res = run_bass_kernel_spmd(nc, in_maps, core_ids=list(range(n)))
    out = np.stack([np.asarray(r["out"]) for r in res.results], axis=0)
    return out.astype(np.float32)
```
